# Optimizing a Trainium2 kernel written in Bass

```python
import math
import jax, jax.numpy as jnp
from jax import lax
import numpy as np

D_MODEL = 2048
BATCH = 4
SEQ = 2048
DEPTH = 4

N_EVEN = (DEPTH + 1) // 2
N_ODD = DEPTH // 2
BLOCK = 128
ROPE_THETA = 10000.0
NORM_EPS = 1e-6

A_HEAD_DIM = 128
A_HEADS = D_MODEL // 2 // A_HEAD_DIM
A_WIDTH = A_HEADS * A_HEAD_DIM
A_PATTERNS = ((128, 1), (512, 4), (2048, 16))
B_WIDTH = D_MODEL // 2
B_BLOCKS = 8
B_BLOCK_DIM = B_WIDTH // B_BLOCKS
B_CONV = 4
LRU_C = 8.0
EVEN_IN = 3 * A_WIDTH + 2 * B_WIDTH
EVEN_MIX = A_WIDTH + B_WIDTH

C_HEAD_DIM = 64
C_HEADS = D_MODEL // 2 // C_HEAD_DIM
C_KV_HEADS = C_HEADS // 8
C_GROUP = C_HEADS // C_KV_HEADS
C_WIDTH = C_HEADS * C_HEAD_DIM
C_KV_WIDTH = C_KV_HEADS * C_HEAD_DIM
C_WINDOW = 128
D_WIDTH = D_MODEL // 2
D_GROUP_DIM = 16
D_GROUPS = D_WIDTH // D_GROUP_DIM
D_STATE = 64
ODD_IN = C_WIDTH + 2 * C_KV_WIDTH + D_WIDTH
ODD_MIX = C_WIDTH + D_WIDTH

D_FF = ((8 * D_MODEL // 3 + 127) // 128) * 128
FFN_CONV = 3

kernel_name = 'hybrid_dilated_lru_swa_s5_trunk'

F32 = jnp.float32


def rmsnorm(x, g):
    x32 = x.astype(F32)
    y = x32 * lax.rsqrt(jnp.mean(x32 * x32, axis=-1, keepdims=True) + NORM_EPS)
    return (y * g.astype(F32)).astype(x.dtype)


def modulate(h, shift, scale):
    return (h.astype(F32) * (1.0 + scale[:, None]) + shift[:, None]).astype(h.dtype)


def rope(x, positions):
    half = x.shape[-1] // 2
    inv = ROPE_THETA ** (-jnp.arange(half, dtype=F32) / half)
    ang = positions.astype(F32)[..., None] * inv
    cos, sin = jnp.cos(ang)[:, :, None, :], jnp.sin(ang)[:, :, None, :]
    x1, x2 = x[..., :half].astype(F32), x[..., half:].astype(F32)
    return jnp.concatenate([x1 * cos - x2 * sin, x2 * cos + x1 * sin], axis=-1)


def causal_dwconv(x, w, b):
    k, s = w.shape[0], x.shape[1]
    xp = jnp.pad(x, ((0, 0), (k - 1, 0), (0, 0)))
    out = b
    for i in range(k):
        out = out + w[i] * xp[:, i:i + s]
    return out


def linear_scan(a, b):
    def combine(l, r):
        al, bl = l
        ar, br = r
        return ar * al, ar * bl + br
    _, h = lax.associative_scan(combine, (a, b), axis=1)
    return h


def banded_window_attention(q, k, v, max_dist):
    n, r, l, dh = q.shape
    blk = min(BLOCK, l)
    nb = l // blk
    pad = -(-max_dist // blk) * blk
    span = pad + blk
    kp = jnp.pad(k.astype(F32), ((0, 0), (pad, 0), (0, 0)))
    vp = jnp.pad(v.astype(F32), ((0, 0), (pad, 0), (0, 0)))
    idx = jnp.arange(nb)[:, None] * blk + jnp.arange(span)[None, :]
    kb, vb = kp[:, idx], vp[:, idx]
    qb = q.astype(F32).reshape(n, r, nb, blk, dh)
    s = jnp.einsum('nrbqd,nbkd->nrbqk', qb, kb) * (dh ** -0.5)
    qi = jnp.arange(blk)[:, None]
    kj = jnp.arange(span)[None, :]
    dist = qi + pad - kj
    kpos = jnp.arange(nb)[:, None, None] * blk + kj[None] - pad
    valid = (dist >= 0) & (dist <= max_dist) & (kpos >= 0)
    s = jnp.where(valid, s, -jnp.inf)
    m = jnp.max(s, axis=-1, keepdims=True)
    p = jnp.exp(s - m)
    den = jnp.sum(p, axis=-1)
    o = jnp.einsum('nrbqk,nbkd->nrbqd', p, vb) / den[..., None]
    lse = m[..., 0] + jnp.log(den)
    return o.reshape(n, r, l, dh), lse.reshape(n, r, l)


def dilated_window_attention(q, k, v):
    b, s, h, dh = q.shape
    outs, lses = [], []
    for window, dil in A_PATTERNS:
        l = s // dil
        def to_sub(t):
            return t.reshape(b, l, dil, h, dh).transpose(0, 2, 3, 1, 4).reshape(b * dil * h, l, dh)
        o, lse = banded_window_attention(to_sub(q)[:, None], to_sub(k), to_sub(v), window // dil)
        outs.append(o.reshape(b, dil, h, l, dh).transpose(0, 3, 1, 2, 4).reshape(b, s, h, dh))
        lses.append(lse.reshape(b, dil, h, l).transpose(0, 3, 1, 2).reshape(b, s, h))
    w = jax.nn.softmax(jnp.stack(lses, axis=0), axis=0)
    return jnp.einsum('pbsh,pbshd->bshd', w, jnp.stack(outs, axis=0))


def rg_lru(xb, conv_w, conv_b, ga_w, ga_b, gx_w, gx_b, lam):
    b, s, _ = xb.shape
    xc = causal_dwconv(xb.astype(F32), conv_w.astype(F32), conv_b.astype(F32))
    xh = xc.reshape(b, s, B_BLOCKS, B_BLOCK_DIM)
    r = jax.nn.sigmoid(jnp.einsum('bshi,hij->bshj', xh, ga_w.astype(F32)).reshape(b, s, B_WIDTH) + ga_b.astype(F32))
    i = jax.nn.sigmoid(jnp.einsum('bshi,hij->bshj', xh, gx_w.astype(F32)).reshape(b, s, B_WIDTH) + gx_b.astype(F32))
    log_a = -LRU_C * r * jax.nn.softplus(-lam.astype(F32))
    a = jnp.exp(log_a)
    mult = jnp.sqrt(-jnp.expm1(2.0 * log_a))
    return linear_scan(a, mult * (i * xc))


def sink_window_attention(q, k, v, sinks):
    b, s, _, dh = q.shape
    qg = q.reshape(b, s, C_KV_HEADS, C_GROUP, dh).transpose(0, 2, 3, 1, 4).reshape(b * C_KV_HEADS, C_GROUP, s, dh)
    kg = k.transpose(0, 2, 1, 3).reshape(b * C_KV_HEADS, s, dh)
    vg = v.transpose(0, 2, 1, 3).reshape(b * C_KV_HEADS, s, dh)
    o, lse = banded_window_attention(qg, kg, vg, C_WINDOW - 1)
    sink = sinks.astype(F32).reshape(1, C_KV_HEADS, C_GROUP, 1)
    keep = jax.nn.sigmoid(lse.reshape(b, C_KV_HEADS, C_GROUP, s) - sink)
    o = o.reshape(b, C_KV_HEADS, C_GROUP, s, dh) * keep[..., None]
    return o.transpose(0, 3, 1, 2, 4).reshape(b, s, C_WIDTH)


def s5_ssm(u, a_re, a_im, b_re, b_im, c_re, c_im, d_skip, log_dt, glu_w, glu_b):
    b, s, _ = u.shape
    u32 = u.astype(F32).reshape(b, s, D_GROUPS, D_GROUP_DIM)
    lam = lax.complex(a_re.astype(F32), a_im.astype(F32))
    dt = jnp.exp(log_dt.astype(F32))[:, None]
    a_bar = jnp.exp(lam * dt)
    b_mat = lax.complex(b_re.astype(F32), b_im.astype(F32))
    b_bar = ((a_bar - 1.0) / lam)[..., None] * b_mat
    bu = jnp.einsum('bsgc,gpc->bsgp', u32.astype(jnp.complex64), b_bar)
    state = linear_scan(jnp.broadcast_to(a_bar, bu.shape), bu)
    c_mat = lax.complex(c_re.astype(F32), c_im.astype(F32))
    y = jnp.einsum('bsgp,gcp->bsgc', state, c_mat).real + d_skip.astype(F32).reshape(D_GROUPS, D_GROUP_DIM) * u32
    z = jax.nn.gelu(y.reshape(b, s, D_WIDTH))
    return z * jax.nn.sigmoid(z @ glu_w.astype(F32) + glu_b.astype(F32))


def even_mixer(h, positions, w_in, conv_w, conv_b, ga_w, ga_b, gx_w, gx_b, lam, w_out):
    b, s, _ = h.shape
    proj = h @ w_in
    q, k, v, xb, yb = jnp.split(proj, [A_WIDTH, 2 * A_WIDTH, 3 * A_WIDTH, 3 * A_WIDTH + B_WIDTH], axis=-1)
    q = rope(q.reshape(b, s, A_HEADS, A_HEAD_DIM), positions)
    k = rope(k.reshape(b, s, A_HEADS, A_HEAD_DIM), positions)
    attn = dilated_window_attention(q, k, v.reshape(b, s, A_HEADS, A_HEAD_DIM)).reshape(b, s, A_WIDTH)
    lru = rg_lru(xb, conv_w, conv_b, ga_w, ga_b, gx_w, gx_b, lam) * jax.nn.gelu(yb.astype(F32))
    return jnp.concatenate([attn, lru], axis=-1).astype(h.dtype) @ w_out


def odd_mixer(h, positions, w_in, sinks, a_re, a_im, b_re, b_im, c_re, c_im, d_skip, log_dt, glu_w, glu_b, w_out):
    b, s, _ = h.shape
    proj = h @ w_in
    q, k, v, u = jnp.split(proj, [C_WIDTH, C_WIDTH + C_KV_WIDTH, C_WIDTH + 2 * C_KV_WIDTH], axis=-1)
    q = rope(q.reshape(b, s, C_HEADS, C_HEAD_DIM), positions)
    k = rope(k.reshape(b, s, C_KV_HEADS, C_HEAD_DIM), positions)
    attn = sink_window_attention(q, k, v.reshape(b, s, C_KV_HEADS, C_HEAD_DIM), sinks)
    ssm = s5_ssm(u, a_re, a_im, b_re, b_im, c_re, c_im, d_skip, log_dt, glu_w, glu_b)
    return jnp.concatenate([attn, ssm], axis=-1).astype(h.dtype) @ w_out


def conv_ffn(h, w_in, conv_w, conv_b, w_out):
    u = causal_dwconv((h @ w_in).astype(F32), conv_w.astype(F32), conv_b.astype(F32))
    g, v = jnp.split(u, 2, axis=-1)
    return (jax.nn.gelu(g) * v).astype(h.dtype) @ w_out


def setup_inputs(seed: int = 0) -> dict:
    key = jax.random.key(seed)
    ks = iter(jax.random.split(key, 48))

    def nrm(shape, scale):
        return scale * jax.random.normal(next(ks), shape, F32)

    def unif(shape, lo, hi):
        return jax.random.uniform(next(ks), shape, F32, lo, hi)

    x = nrm((BATCH, SEQ, D_MODEL), 1.0)
    c = nrm((BATCH, D_MODEL), 1.0)
    positions = (jax.random.randint(next(ks), (BATCH, 1), 0, 1024, dtype=jnp.int32)
                 + jnp.arange(SEQ, dtype=jnp.int32)[None, :])
    gate_offset = jnp.repeat(jnp.array([0.0, 0.0, 1.0, 0.0, 0.0, 1.0], F32), D_MODEL)
    ada_w = nrm((DEPTH, D_MODEL, 6 * D_MODEL), 0.1 * D_MODEL ** -0.5)
    ada_b = nrm((DEPTH, 6 * D_MODEL), 0.02) + gate_offset
    norm_mix = 1.0 + nrm((DEPTH, D_MODEL), 0.05)
    norm_ffn = 1.0 + nrm((DEPTH, D_MODEL), 0.05)
    norm_final = 1.0 + nrm((D_MODEL,), 0.05)

    ev_w_in = nrm((N_EVEN, D_MODEL, EVEN_IN), D_MODEL ** -0.5)
    ev_conv_w = nrm((N_EVEN, B_CONV, B_WIDTH), B_CONV ** -0.5)
    ev_conv_b = nrm((N_EVEN, B_WIDTH), 0.02)
    ev_gate_a_w = nrm((N_EVEN, B_BLOCKS, B_BLOCK_DIM, B_BLOCK_DIM), B_BLOCK_DIM ** -0.5)
    ev_gate_a_b = nrm((N_EVEN, B_WIDTH), 0.02)
    ev_gate_x_w = nrm((N_EVEN, B_BLOCKS, B_BLOCK_DIM, B_BLOCK_DIM), B_BLOCK_DIM ** -0.5)
    ev_gate_x_b = nrm((N_EVEN, B_WIDTH), 0.02)
    a_pow_c = unif((N_EVEN, B_WIDTH), 0.9, 0.999)
    a_base = a_pow_c ** (1.0 / LRU_C)
    ev_lambda = jnp.log(a_base) - jnp.log1p(-a_base)
    ev_w_out = nrm((N_EVEN, EVEN_MIX, D_MODEL), EVEN_MIX ** -0.5)

    od_w_in = nrm((N_ODD, D_MODEL, ODD_IN), D_MODEL ** -0.5)
    od_sinks = 3.0 + nrm((N_ODD, C_HEADS), 1.0)
    od_a_re = -0.5 + nrm((N_ODD, D_GROUPS, D_STATE), 0.01)
    od_a_im = math.pi * jnp.arange(D_STATE, dtype=F32) + nrm((N_ODD, D_GROUPS, D_STATE), 0.01)
    od_b_re = nrm((N_ODD, D_GROUPS, D_STATE, D_GROUP_DIM), (2.0 * D_GROUP_DIM) ** -0.5)
    od_b_im = nrm((N_ODD, D_GROUPS, D_STATE, D_GROUP_DIM), (2.0 * D_GROUP_DIM) ** -0.5)
    od_c_re = nrm((N_ODD, D_GROUPS, D_GROUP_DIM, D_STATE), (2.0 * D_STATE) ** -0.5)
    od_c_im = nrm((N_ODD, D_GROUPS, D_GROUP_DIM, D_STATE), (2.0 * D_STATE) ** -0.5)
    od_d = nrm((N_ODD, D_WIDTH), 0.5)
    od_log_dt = unif((N_ODD, D_GROUPS), math.log(1e-3), math.log(1e-1))
    od_glu_w = nrm((N_ODD, D_WIDTH, D_WIDTH), D_WIDTH ** -0.5)
    od_glu_b = nrm((N_ODD, D_WIDTH), 0.02)
    od_w_out = nrm((N_ODD, ODD_MIX, D_MODEL), ODD_MIX ** -0.5)

    ffn_w_in = nrm((DEPTH, D_MODEL, 2 * D_FF), D_MODEL ** -0.5)
    ffn_conv_w = nrm((DEPTH, FFN_CONV, 2 * D_FF), FFN_CONV ** -0.5)
    ffn_conv_b = nrm((DEPTH, 2 * D_FF), 0.02)
    ffn_w_out = nrm((DEPTH, D_FF, D_MODEL), D_FF ** -0.5)

    return {'x': x, 'c': c, 'positions': positions,
            'ada_w': ada_w, 'ada_b': ada_b, 'norm_mix': norm_mix, 'norm_ffn': norm_ffn, 'norm_final': norm_final,
            'ev_w_in': ev_w_in, 'ev_conv_w': ev_conv_w, 'ev_conv_b': ev_conv_b,
            'ev_gate_a_w': ev_gate_a_w, 'ev_gate_a_b': ev_gate_a_b, 'ev_gate_x_w': ev_gate_x_w, 'ev_gate_x_b': ev_gate_x_b,
            'ev_lambda': ev_lambda, 'ev_w_out': ev_w_out,
            'od_w_in': od_w_in, 'od_sinks': od_sinks, 'od_a_re': od_a_re, 'od_a_im': od_a_im,
            'od_b_re': od_b_re, 'od_b_im': od_b_im, 'od_c_re': od_c_re, 'od_c_im': od_c_im,
            'od_d': od_d, 'od_log_dt': od_log_dt, 'od_glu_w': od_glu_w, 'od_glu_b': od_glu_b, 'od_w_out': od_w_out,
            'ffn_w_in': ffn_w_in, 'ffn_conv_w': ffn_conv_w, 'ffn_conv_b': ffn_conv_b, 'ffn_w_out': ffn_w_out}


def reference(x, c, positions, ada_w, ada_b, norm_mix, norm_ffn, norm_final,
              ev_w_in, ev_conv_w, ev_conv_b, ev_gate_a_w, ev_gate_a_b, ev_gate_x_w, ev_gate_x_b, ev_lambda, ev_w_out,
              od_w_in, od_sinks, od_a_re, od_a_im, od_b_re, od_b_im, od_c_re, od_c_im, od_d, od_log_dt,
              od_glu_w, od_glu_b, od_w_out,
              ffn_w_in, ffn_conv_w, ffn_conv_b, ffn_w_out):
    cond = jax.nn.silu(c.astype(F32))
    for layer in range(DEPTH):
        mod = cond @ ada_w[layer].astype(F32) + ada_b[layer].astype(F32)
        sh1, sc1, g1, sh2, sc2, g2 = jnp.split(mod, 6, axis=-1)
        h = modulate(rmsnorm(x, norm_mix[layer]), sh1, sc1)
        if layer % 2 == 0:
            e = layer // 2
            y = even_mixer(h, positions, ev_w_in[e], ev_conv_w[e], ev_conv_b[e], ev_gate_a_w[e], ev_gate_a_b[e],
                           ev_gate_x_w[e], ev_gate_x_b[e], ev_lambda[e], ev_w_out[e])
        else:
            o = layer // 2
            y = odd_mixer(h, positions, od_w_in[o], od_sinks[o], od_a_re[o], od_a_im[o], od_b_re[o], od_b_im[o],
                          od_c_re[o], od_c_im[o], od_d[o], od_log_dt[o], od_glu_w[o], od_glu_b[o], od_w_out[o])
        x = x + (g1[:, None] * y.astype(F32)).astype(x.dtype)
        h = modulate(rmsnorm(x, norm_ffn[layer]), sh2, sc2)
        f = conv_ffn(h, ffn_w_in[layer], ffn_conv_w[layer], ffn_conv_b[layer], ffn_w_out[layer])
        x = x + (g2[:, None] * f.astype(F32)).astype(x.dtype)
    return rmsnorm(x, norm_final)
```

```python
import contextlib
import numpy as np
import ml_dtypes
import concourse.bass as bass
import concourse.mybir as mybir
from concourse.bass_utils import run_bass_kernel_spmd

F32 = mybir.dt.float32
BF16 = mybir.dt.bfloat16
I32 = mybir.dt.int32
ALU = mybir.AluOpType
AF = mybir.ActivationFunctionType

D = 2048
T = 2048
KC = 16
NT = 4
DEPTH = 4
DFF = 5504
NJ = 43
EPS = 1e-6
NEG = -30000.0
ENGS = ['pe', 'act', 'dve', 'pool', 'sp']


class Res:
    def __init__(self, name):
        self.name = name
        self.wr = []
        self.rd = []


class Chan:
    def __init__(self, name):
        self.name = name
        self.count = 0
        self.sem = None


def _add(lst, tok):
    if tok[0] == 'c':
        for i, t in enumerate(lst):
            if t[0] == 'c' and t[1] == tok[1]:
                if t[2] < tok[2]:
                    lst[i] = tok
                return
        lst.append(tok)
    else:
        if tok not in lst:
            lst.append(tok)


class Sched:
    def __init__(self, nc):
        self.nc = nc
        self.ops = {e: [] for e in ENGS}
        self.chans = {}
        self.pending = {e: [] for e in ENGS}

    def _deps(self, eng, reads, writes, join):
        toks = []
        for r in reads:
            toks += r.wr
        for w in writes:
            if not join:
                toks += w.wr
            toks += w.rd
        toks += self.pending[eng]
        self.pending[eng] = []
        deps = []
        for t in toks:
            if t[0] == 'd':
                deps.append(('d', t[1], t[1].count) if len(t) == 2 else t)
            else:
                deps.append(t)
        return deps

    def _commit(self, tok, reads, writes, join):
        for r in reads:
            _add(r.rd, tok)
        for w in writes:
            if join:
                _add(w.wr, tok)
            else:
                w.wr = [tok]
            w.rd = []

    def op(self, eng, fn, reads=(), writes=(), join=False):
        deps = self._deps(eng, reads, writes, join)
        idx = len(self.ops[eng])
        o = dict(kind='c', fn=fn, deps=deps, sig=False)
        self.ops[eng].append(o)
        self._commit(('c', eng, idx), reads, writes, join)
        return o

    def dma(self, eng, fn, reads=(), writes=(), join=False, chan=None):
        cname = chan or ('ch_' + writes[0].name)
        if cname not in self.chans:
            self.chans[cname] = Chan(cname)
        ch = self.chans[cname]
        deps = self._deps(eng, reads, writes, join)
        ch.count += 1
        o = dict(kind='d', fn=fn, deps=deps, chan=ch)
        self.ops[eng].append(o)
        self._commit(('d', ch), reads, writes, join)
        return o

    def barrier(self, engs=('pe', 'act', 'dve', 'sp')):
        toks = []
        for e in ENGS:
            if self.ops[e]:
                for i in range(len(self.ops[e]) - 1, -1, -1):
                    if self.ops[e][i]['kind'] == 'c':
                        toks.append(('c', e, i))
                        break
        for ch in self.chans.values():
            toks.append(('d', ch, ch.count))
        for e in engs:
            self.pending[e] = list(toks)

    def emit(self):
        nc = self.nc
        for e in ENGS:
            for o in self.ops[e]:
                for d in o['deps']:
                    if d[0] == 'c':
                        self.ops[d[1]][d[2]]['sig'] = True
        for e in ENGS:
            n = 0
            for o in self.ops[e]:
                if o['kind'] == 'c' and o['sig']:
                    n += 1
                    o['sigval'] = n
        with contextlib.ExitStack() as st:
            esem = {e: st.enter_context(nc.semaphore('s_' + e)) for e in ENGS}
            for i, (cname, ch) in enumerate(self.chans.items()):
                ch.sem = st.enter_context(nc.semaphore('d%d' % i))
            block = st.enter_context(nc.Block())

            def run(e, eng):
                seen = {}
                for o in self.ops[e]:
                    need = {}
                    for d in o['deps']:
                        if d[0] == 'c':
                            key = ('c', d[1]); sem = esem[d[1]]
                            val = self.ops[d[1]][d[2]]['sigval']
                        else:
                            key = ('d', d[1].name); sem = d[1].sem
                            val = 16 * d[2]
                        if val > need.get(key, (None, 0))[1]:
                            need[key] = (sem, val)
                    for key, (sem, val) in need.items():
                        if seen.get(key, 0) >= val:
                            continue
                        seen[key] = val
                        eng.wait_ge(sem, val)
                    ins = o['fn'](eng)
                    if o['kind'] == 'd':
                        ins.then_inc(o['chan'].sem, 16)
                    elif o['sig']:
                        ins.then_inc(esem[e], 1)
                if e == 'sp':
                    for ch in self.chans.values():
                        eng.wait_ge(ch.sem, 16 * ch.count)

            @block.tensor
            def _(eng):
                run('pe', eng)

            @block.scalar
            def _(eng):
                run('act', eng)

            @block.vector
            def _(eng):
                run('dve', eng)

            @block.gpsimd
            def _(eng):
                run('pool', eng)

            @block.sync
            def _(eng):
                run('sp', eng)


WSLOT = 11264


class Builder:
    def __init__(self, nlayers=DEPTH, dbg=False):
        self.nlayers = nlayers
        self.dbg = dbg
        self.nc = bass.Bass("TRN2", target_bir_lowering=False)
        self.S = Sched(self.nc)
        self.inputs = {}
        self.gst = contextlib.ExitStack()
        self._sbn = 0
        self.wq = []
        self.wissued = 0
        self.wtaken = 0

    def din(self, name, shape, dt=F32):
        t = self.nc.dram_tensor(name, list(shape), dt, kind="ExternalInput").ap()
        self.inputs[name] = (list(shape), dt)
        return t

    def dscr(self, name, shape, dt):
        return self.nc.dram_tensor(name, list(shape), dt, kind="Internal").ap()

    def sb(self, st, name, shape, dt):
        self._sbn += 1
        return st.enter_context(self.nc.sbuf_tensor('s%d_%s' % (self._sbn, name), list(shape), dt))

    def mm(self, out, lhsT, rhs, start, stop, reads, writes, join):
        self.S.op('pe', lambda e: e.matmul(out, lhsT=lhsT, rhs=rhs, start=start, stop=stop,
                                           skip_group_check=True), reads=reads, writes=writes, join=join)

    def tr(self, out, in_, reads, writes, join):
        self.S.op('pe', lambda e: e.transpose(out, in_, self.identf[:]), reads=list(reads) + [self.R_identf], writes=writes, join=join)

    def act(self, out, in_, func, reads, writes, bias=None, scale=None, join=False):
        kw = {}
        if bias is not None:
            kw['bias'] = bias
        if scale is not None:
            kw['scale'] = scale
        self.S.op('act', lambda e: e.activation(out=out, in_=in_, func=func, **kw), reads=reads, writes=writes, join=join)

    def tt(self, out, in0, in1, op, reads, writes, join=False):
        self.S.op('dve', lambda e: e.tensor_tensor(out=out, in0=in0, in1=in1, op=op), reads=reads, writes=writes, join=join)

    def ts(self, out, in0, s1, s2, op0, op1, reads, writes, join=False):
        if op1 is None:
            self.S.op('dve', lambda e: e.tensor_scalar(out=out, in0=in0, scalar1=s1, scalar2=None, op0=op0), reads=reads, writes=writes, join=join)
        else:
            self.S.op('dve', lambda e: e.tensor_scalar(out=out, in0=in0, scalar1=s1, scalar2=s2, op0=op0, op1=op1), reads=reads, writes=writes, join=join)

    def stt(self, out, in0, scalar, in1, op0, op1, reads, writes, join=False):
        self.S.op('dve', lambda e: e.scalar_tensor_tensor(out=out, in0=in0, scalar=scalar, in1=in1, op0=op0, op1=op1), reads=reads, writes=writes, join=join)

    def cp(self, out, in_, reads, writes, eng='dve', join=False):
        if eng == 'dve':
            self.S.op('dve', lambda e: e.tensor_copy(out=out, in_=in_), reads=reads, writes=writes, join=join)
        else:
            self.S.op('act', lambda e: e.activation(out=out, in_=in_, func=AF.Copy), reads=reads, writes=writes, join=join)

    def memset(self, ap, val, writes, join=False):
        self.S.op('dve', lambda e: e.memset(ap, val), writes=writes, join=join)

    def ld(self, out, in_, reads, writes, join=False, eng='sp', chan=None):
        self.S.dma(eng, lambda e: e.dma_start(out=out, in_=in_), reads=reads, writes=writes, join=join, chan=chan)

    def bank(self, grp=None):
        if grp is None:
            i = self.pnext % 8; self.pnext += 1
        elif grp == '7':
            i = self.p7 % 7; self.p7 += 1
        elif grp == 'A':
            i = self.pa % 4; self.pa += 1
        else:
            i = 4 + self.pbn % 4; self.pbn += 1
        return self.pbanks[i]

    def wplan(self, key, parts):
        self.wq.append((key, parts))

    def _wissue(self, upto):
        while self.wissued < min(upto, len(self.wq)):
            i = self.wissued
            slot, R = self.wslots[i % 3]
            key, parts = self.wq[i]
            for pi, (dst, src) in enumerate(parts):
                self.ld(dst(slot), src, [], [R], join=(pi > 0), eng='pool')
            self.wissued += 1

    def wnext(self, key):
        i = self.wtaken
        assert self.wq[i][0] == key, (self.wq[i][0], key)
        self._wissue(i + 3)
        self.wtaken += 1
        return self.wslots[i % 3]

    @staticmethod
    def wview(slot, k, n):
        return slot[:, 0:k * n].rearrange("p (k n) -> p k n", n=n)

    def plan_k2048(self, key, W, c0, w):
        def dst(lo, hi):
            return lambda slot: self.wview(slot, KC, w)[:, lo:hi, :]
        parts = []
        for lo in (0, 8):
            parts.append((dst(lo, lo + 8), W[lo * 128:(lo + 8) * 128, c0:c0 + w].rearrange("(k p) n -> p k n", p=128)))
        self.wplan(key, parts)

    def build(self):
        nc, S = self.nc, self.S
        g = self.gst
        L = self.nlayers
        x_d = self.din('x', [T, D])
        self.out_d = nc.dram_tensor('out', [T, D], F32, kind="ExternalOutput").ap()
        identf_d = self.din('identf', [128, 128])
        identb_d = self.din('identb', [128, 128], BF16)
        masks_d = self.din('masks', [128, 3, 512], BF16)
        m3_d = self.din('m3', [128, 4, 512], BF16)
        pswap_d = self.din('pswap', [128, 2, 128], BF16)
        ropec_d = self.din('ropec', [128, 4])
        pos_d = self.din('pos128', [128, T], I32)
        cT_d = self.din('cT', [128, KC])
        normfin_d = self.din('norm_final_T', [128, KC])
        normmix_d = self.din('norm_mix_T', [128, DEPTH, KC])
        normffn_d = self.din('norm_ffn_T', [128, DEPTH, KC])
        self.ada_w = self.din('ada_w', [L, D, 6 * D])
        adab_d = self.din('ada_bT', [128, DEPTH, 96])
        NE = (L + 1) // 2
        self.ev_w_in = self.din('ev_w_in', [NE, D, 5120])
        self.ev_w_out = self.din('ev_w_out', [NE, D, D])
        self.ev_convw_d = self.din('ev_convw_T', [128, 2, 8, 4])
        self.ev_vec_d = self.din('ev_vec_T', [128, 2, 4, 8])
        self.ev_gw_d = self.din('ev_gw', [128, 2, 2, 8, 128])
        NO = max(L // 2, 1)
        self.od_w_in = self.din('od_w_in', [NO, D, 2304])
        self.od_w_out = self.din('od_w_out', [NO, D, D])
        self.od_glu_w = self.din('od_glu_w', [NO, 1024, 1024])
        self.s5v_d = self.din('s5v', [128, 2, 3, 32])
        self.s5b_d = self.din('s5b', [128, 2, 2, 32, 16])
        self.s5c_d = self.din('s5c', [128, 2, 2, 32, 16])
        self.odv_d = self.din('odv', [128, 2, 3, 8])
        self.ffn_w_in = self.din('ffn_w_in', [L, D, 2 * DFF])
        self.ffn_w_out = self.din('ffn_w_out', [L, DFF, D])
        self.ffn_convw_d = self.din('ffn_convw_T', [128, DEPTH, 86, 3])
        self.ffn_convb_d = self.din('ffn_convb_T', [128, DEPTH, 86])
        if self.dbg:
            self.dbg_x = nc.dram_tensor('dbg_x', [D, T], F32, kind="ExternalOutput").ap()
            self.dbg_mix = nc.dram_tensor('dbg_mix', [D, T], BF16, kind="ExternalOutput").ap()
            self.dbg_h = nc.dram_tensor('dbg_h', [128, KC, T], BF16, kind="ExternalOutput").ap()
            self.dbg_mod = nc.dram_tensor('dbg_mod', [128, 96], F32, kind="ExternalOutput").ap()
        self.xT = self.dscr('xT', [D, T], F32); self.R_xTn = [Res('xT%d' % n) for n in range(NT)]
        self.qT = self.dscr('qT', [1024, T], BF16); self.R_qT = Res('qT')
        self.kT = self.dscr('kT', [1024, T], BF16); self.R_kT = Res('kT')
        self.vv = self.dscr('vv', [T, 1024], BF16); self.R_vv = Res('vv')
        self.xbT = self.dscr('xbT', [1024, T], F32); self.R_xbT = Res('xbT')
        self.ybT = self.dscr('ybT', [1024, T], BF16); self.R_ybT = Res('ybT')
        self.mixT = self.dscr('mixT', [D, T], BF16); self.R_mixT = Res('mixT')
        self.midT = self.dscr('midT', [DFF, T], BF16); self.R_midT = Res('midT')
        self.ropeT = self.dscr('ropeT', [4, 128, T], F32); self.R_ropeT = Res('ropeT')
        self.identf = self.sb(g, 'identf', [128, 128], F32); self.R_identf = Res('identf')
        self.identb = self.sb(g, 'identb', [128, 128], BF16); self.R_identb = Res('identb')
        self.onesb = self.sb(g, 'onesb', [128, 128], BF16); self.R_onesb = Res('onesb')
        self.masks = self.sb(g, 'masks', [128, 3, 512], BF16); self.R_masks = Res('masks')
        self.m3 = self.sb(g, 'm3', [128, 4, 512], BF16); self.R_m3 = Res('m3')
        self.pswap = self.sb(g, 'pswap', [128, 2, 128], BF16); self.R_pswap = Res('pswap')
        self.normfin = self.sb(g, 'normfin', [128, KC], F32); self.R_normfin = Res('normfin')
        self.normmix = self.sb(g, 'normmix', [128, DEPTH, KC], F32); self.R_normmix = Res('normmix')
        self.normffn = self.sb(g, 'normffn', [128, DEPTH, KC], F32); self.R_normffn = Res('normffn')
        self.adab = self.sb(g, 'adab', [128, DEPTH, 96], F32); self.R_adab = Res('adab')
        self.epsc = self.sb(g, 'epsc', [128, 1], F32); self.R_epsc = Res('epsc')
        self.cond = self.sb(g, 'cond', [128, KC], BF16); self.R_cond = Res('cond')
        self.modT_ = [self.sb(g, 'modT%d' % i, [128, 96], F32) for i in range(2)]; self.R_modT_ = [Res('modT%d' % i) for i in range(2)]
        self.gm_ = [self.sb(g, 'gm%d' % i, [128, 2, KC], F32) for i in range(2)]; self.R_gm_ = [Res('gm%d' % i) for i in range(2)]
        self.modT, self.R_modT, self.gm, self.R_gm = self.modT_[0], self.R_modT_[0], self.gm_[0], self.R_gm_[0]
        self.actbuf = self.sb(g, 'actbuf', [128, KC, T], BF16); self.R_act = Res('actbuf')
        self.wslots = []
        for i in range(3):
            self.wslots.append((self.sb(g, 'wslot%d' % i, [128, WSLOT], BF16), Res('wslot%d' % i)))
        self.pb = [g.enter_context(nc.psum_tensor('pb%d' % i, [128, 512], F32)) for i in range(8)]
        self.pbanks = [(self.pb[i], Res('pb%d' % i)) for i in range(8)]
        self.pnext = 0; self.pa = 0; self.pbn = 0; self.p7 = 0
        for dst, src, R in [(self.identf, identf_d, self.R_identf), (self.identb, identb_d, self.R_identb),
                            (self.masks, masks_d, self.R_masks), (self.m3, m3_d, self.R_m3), (self.pswap, pswap_d, self.R_pswap),
                            (self.normfin, normfin_d, self.R_normfin), (self.normmix, normmix_d, self.R_normmix),
                            (self.normffn, normffn_d, self.R_normffn), (self.adab, adab_d, self.R_adab)]:
            self.ld(dst[:], src, [], [R], chan='init')
        self.memset(self.onesb[:], 1.0, [self.R_onesb])
        self.memset(self.epsc[:], EPS, [self.R_epsc])

        for l in range(L):
            self.plan_layer(l)

        self.phase_cond(cT_d)
        self.phase_rope_tables(pos_d, ropec_d)
        self.phase_load_x(x_d)
        for l in range(L):
            self.layer(l)
        if self.dbg:
            self.ld(self.dbg_x, self.xT, self.R_xTn, [Res('dbgx')], chan='ch_out')
            self.ld(self.dbg_mix, self.mixT, [self.R_mixT], [Res('dbgm')], chan='ch_out')
        self.phase_final(self.out_d)
        S.emit()
        return nc

    def plan_layer(self, l):
        if l == 0:
            for t in range(24):
                self.plan_k2048(('ada', l, t), self.ada_w[l], t * 512, 512)
        if l % 2 == 0:
            e = l // 2
            for t in range(10):
                self.plan_k2048(('evin', l, t), self.ev_w_in[e], t * 512, 512)
            for t in range(4):
                self.plan_k2048(('wout', l, t), self.ev_w_out[e], t * 512, 512)
        else:
            o = l // 2
            W = self.od_w_in[o]
            for t in range(2):
                parts = []
                for c in range(4):
                    for half in range(2):
                        hd = half * 8 + t * 4 + c
                        parts.append(((lambda slot, c=c, half=half: self.wview(slot, KC, 512)[:, :, c * 128 + half * 64: c * 128 + (half + 1) * 64]),
                                      W[:, hd * 64:(hd + 1) * 64].rearrange("(k p) n -> p k n", p=128)))
                self.wplan(('odq', l, t), parts)
            self.plan_k2048(('odkv', l, 0), W, 1024, 256)
            for t in range(2):
                self.plan_k2048(('odu', l, t), W, 1280 + t * 512, 512)
            G = self.od_glu_w[o]
            for t in range(2):
                self.wplan(('glu', l, t), [((lambda slot: self.wview(slot, 8, 512)), G[:, t * 512:(t + 1) * 512].rearrange("(k p) n -> p k n", p=128))])
            W2_ = self.od_w_out[o]
            for t in range(4):
                cs_ = slice(t * 512, (t + 1) * 512)
                parts = [((lambda slot: self.wview(slot, KC, 512)[0:64, 0:8, :]), W2_[0:512, cs_].rearrange("(k p) n -> p k n", p=64)),
                         ((lambda slot: self.wview(slot, KC, 512)[64:128, 0:8, :]), W2_[512:1024, cs_].rearrange("(k p) n -> p k n", p=64)),
                         ((lambda slot: self.wview(slot, KC, 512)[:, 8:16, :]), W2_[1024:2048, cs_].rearrange("(k p) n -> p k n", p=128))]
                self.wplan(('wout', l, t), parts)
        W = self.ffn_w_in[l]
        for t in range(22):
            j0 = 2 * t
            nj = min(2, NJ - j0)
            w = nj * 128
            parts = []
            for br in range(2):
                for lo in (0, 8):
                    parts.append(((lambda slot, lo=lo, br=br, w=w: self.wview(slot, KC, 2 * w)[:, lo:lo + 8, br * w:(br + 1) * w]),
                                  W[lo * 128:(lo + 8) * 128, br * DFF + j0 * 128: br * DFF + j0 * 128 + w].rearrange("(k p) n -> p k n", p=128)))
            self.wplan(('ffin', l, t), parts)
            if l + 1 < self.nlayers:
                self.plan_k2048(('ada', l + 1, t), self.ada_w[l + 1], t * 512, 512)
        if l + 1 < self.nlayers:
            for t in (22, 23):
                self.plan_k2048(('ada', l + 1, t), self.ada_w[l + 1], t * 512, 512)
        W2 = self.ffn_w_out[l]
        for th in range(2):
            for dp in range(8):
                parts = []
                for (lo, hi) in ((0, 11), (11, 22), (22, 33), (33, 43)):
                    parts.append(((lambda slot, lo=lo, hi=hi: self.wview(slot, NJ, 256)[:, lo:hi, :]),
                                  W2[lo * 128:hi * 128, dp * 256:(dp + 1) * 256].rearrange("(k p) n -> p k n", p=128)))
                self.wplan(('ffout', l, th, dp), parts)

    def phase_cond(self, cT_d):
        with contextlib.ExitStack() as st:
            c = self.sb(st, 'c_in', [128, KC], F32); R_c = Res('c_in')
            self.ld(c[:], cT_d, [], [R_c])
            self.act(self.cond[:], c[:], AF.Silu, [R_c], [self.R_cond])
        self.S.barrier()

    def phase_rope_tables(self, pos_d, ropec_d):
        S = self.S
        with contextlib.ExitStack() as st:
            posi = self.sb(st, 'posi', [128, T], I32); R_posi = Res('posi')
            posf = self.sb(st, 'posf', [128, T], F32); R_posf = Res('posf')
            rc = self.sb(st, 'ropec', [128, 4], F32); R_rc = Res('ropec')
            ang = self.sb(st, 'ang', [128, T], F32); R_ang = Res('ang')
            ki = self.sb(st, 'ki', [128, T], I32); R_ki = Res('ki')
            kf = self.sb(st, 'kf', [128, T], F32); R_kf = Res('kf')
            tmp = self.sb(st, 'rtmp', [128, T], F32); R_tmp = Res('rtmp')
            cs = self.sb(st, 'rcos', [128, T], F32); R_cs = Res('rcos')
            self.ld(posi[:], pos_d, [], [R_posi])
            self.ld(rc[:], ropec_d, [], [R_rc])
            self.cp(posf[:], posi[:], [R_posi], [R_posf])
            C1 = 6.28125
            C2 = float(2 * np.pi - C1)
            for v in range(2):
                inv = rc[:, 2 * v:2 * v + 1]; sgn = rc[:, 2 * v + 1:2 * v + 2]
                self.ts(ang[:], posf[:], inv, None, ALU.mult, None, [R_posf, R_rc], [R_ang])
                self.ts(tmp[:], ang[:], float(1.0 / (2 * np.pi)), None, ALU.mult, None, [R_ang], [R_tmp])
                self.cp(ki[:], tmp[:], [R_tmp], [R_ki])
                self.cp(kf[:], ki[:], [R_ki], [R_kf])
                self.stt(ang[:], kf[:], -C1, ang[:], ALU.mult, ALU.add, [R_kf, R_ang], [R_ang])
                self.stt(ang[:], kf[:], -C2, ang[:], ALU.mult, ALU.add, [R_kf, R_ang], [R_ang])
                self.ts(tmp[:], ang[:], float(np.pi), float(-2 * np.pi), ALU.is_gt, ALU.mult, [R_ang], [R_tmp])
                self.tt(ang[:], ang[:], tmp[:], ALU.add, [R_ang, R_tmp], [R_ang])
                self.ts(tmp[:], ang[:], float(-np.pi), float(2 * np.pi), ALU.is_lt, ALU.mult, [R_ang], [R_tmp])
                self.tt(ang[:], ang[:], tmp[:], ALU.add, [R_ang, R_tmp], [R_ang])
                self.ts(ang[:], ang[:], 3.14159, -3.14159, ALU.min, ALU.max, [R_ang], [R_ang])
                self.act(tmp[:], ang[:], AF.Sin, [R_ang], [R_tmp])
                self.ts(tmp[:], tmp[:], sgn, None, ALU.mult, None, [R_tmp, R_rc], [R_tmp])
                self.ld(self.ropeT[2 * v + 1], tmp[:], [R_tmp], [self.R_ropeT], join=True)
                self.act(cs[:], ang[:], AF.Sin, [R_ang], [R_cs], scale=0.5)
                self.act(cs[:], cs[:], AF.Square, [R_cs], [R_cs])
                self.ts(cs[:], cs[:], -2.0, 1.0, ALU.mult, ALU.add, [R_cs], [R_cs])
                self.ld(self.ropeT[2 * v], cs[:], [R_cs], [self.R_ropeT], join=True)
        S.barrier()

    def phase_load_x(self, x_d):
        S = self.S
        with contextlib.ExitStack() as st:
            xt = [self.sb(st, 'lx_in%d' % i, [128, 4, D], F32) for i in range(1)]
            R_xt = [Res('lx_in%d' % i) for i in range(1)]
            ot = [self.sb(st, 'lx_o%d' % i, [128, 512], F32) for i in range(4)]
            R_ot = [Res('lx_o%d' % i) for i in range(4)]
            cnt = 0
            for n in range(NT):
                b = 0
                self.ld(xt[b][:], x_d[n * 512:(n + 1) * 512, :].rearrange("(a p) d -> p a d", p=128), [], [R_xt[b]])
                for kc in range(KC):
                    pt, R_pt = self.bank()
                    for a in range(4):
                        self.tr(pt[:, a * 128:(a + 1) * 128], xt[b][:, a, kc * 128:(kc + 1) * 128], [R_xt[b]], [R_pt], a > 0)
                    o = cnt % 4; cnt += 1
                    self.cp(ot[o][:], pt[:], [R_pt], [R_ot[o]], eng=('act' if kc % 2 else 'dve'))
                    self.ld(self.xT[kc * 128:(kc + 1) * 128, n * 512:(n + 1) * 512], ot[o][:], [R_ot[o]], [self.R_xTn[n]], join=(kc > 0))
        S.barrier()

    def norm_tile(self, xin, R_xin, sq, R_sq, rstd, R_rstd, ncols):
        S = self.S
        self.act(sq[:], xin[:], AF.Square, [R_xin], [R_sq])
        pt, R_pt = self.bank()
        for kc in range(KC):
            self.mm(pt[:, 0:ncols], self.onesb[:], sq[:, kc, :], kc == 0, kc == KC - 1, [self.R_onesb, R_sq], [R_pt], kc > 0)
        self.act(rstd[:], pt[:, 0:ncols], AF.Sqrt, [R_pt, self.R_epsc], [R_rstd], bias=self.epsc[:], scale=1.0 / D)
        S.op('dve', lambda e: e.reciprocal(out=rstd[:], in_=rstd[:]), reads=[R_rstd], writes=[R_rstd])

    def phase_final(self, out_d):
        S = self.S
        NC_ = 256
        with contextlib.ExitStack() as st:
            xin = [self.sb(st, 'fx%d' % i, [128, KC, NC_], F32) for i in range(1)]
            R_xin = [Res('fx%d' % i) for i in range(1)]
            sq = self.sb(st, 'fsq', [128, KC, NC_], BF16); R_sq = Res('fsq')
            rstd = self.sb(st, 'frstd', [128, NC_], F32); R_rstd = Res('frstd')
            yt = self.sb(st, 'fy', [128, KC, NC_], F32); R_yt = Res('fy')
            ot = [self.sb(st, 'fo%d' % i, [128, D], F32) for i in range(2)]
            R_ot = [Res('fo%d' % i) for i in range(2)]
            oc = 0
            for n in range(T // NC_):
                b = 0
                self.ld(xin[b][:], self.xT[:, n * NC_:(n + 1) * NC_].rearrange("(k p) t -> p k t", p=128), [self.R_xTn[n // 2]], [R_xin[b]])
                self.norm_tile(xin[b], R_xin[b], sq, R_sq, rstd, R_rstd, NC_)
                for kc in range(KC):
                    self.stt(yt[:, kc, :], xin[b][:, kc, :], self.normfin[:, kc:kc + 1], rstd[:], ALU.mult, ALU.mult,
                             [R_xin[b], self.R_normfin, R_rstd], [R_yt], join=(kc > 0))
                for a in range(NC_ // 128):
                    o = oc % 2; oc += 1
                    for q in range(4):
                        pt, R_pt = self.bank()
                        for j in range(4):
                            kc = q * 4 + j
                            self.tr(pt[:, j * 128:(j + 1) * 128], yt[:, kc, a * 128:(a + 1) * 128], [R_yt], [R_pt], j > 0)
                        self.cp(ot[o][:, q * 512:(q + 1) * 512], pt[:], [R_pt], [R_ot[o]], eng=('act' if q % 2 else 'dve'), join=(q > 0))
                    r0 = n * NC_ + a * 128
                    self.ld(out_d[r0:r0 + 128, :], ot[o][:], [R_ot[o]], [Res('outd')], chan='ch_out')
        S.barrier()

    def layer(self, l):
        self.modT, self.R_modT, self.gm, self.R_gm = self.modT_[l % 2], self.R_modT_[l % 2], self.gm_[l % 2], self.R_gm_[l % 2]
        stop = getattr(self, 'stop', None)
        order = ['ada', 'norm', 'inproj', 'attn', 'lru', 'outproj', 'norm2', 'ffn_in', 'ffn_out']
        lim = order.index(stop) if (stop and l == self.nlayers - 1) else 99
        if l == 0:
            self.phase_ada(l)
        if lim < 1: return
        self.phase_norm(l, 0)
        if self.dbg and l == self.nlayers - 1:
            self.ld(self.dbg_h, self.actbuf[:], [self.R_act], [Res('dbgh')], chan='ch_out')
            self.ld(self.dbg_mod, self.modT[:], [self.R_modT], [Res('dbgmod')], chan='ch_out')
        if lim < 2: return
        if l % 2 == 0:
            self.phase_inproj_even(l)
            if lim < 3: return
            self.phase_attn_A(l)
            if lim < 4: return
            self.phase_lru(l)
        else:
            self.phase_inproj_odd(l)
            if lim < 3: return
            self.phase_attn_C(l)
            if lim < 4: return
            self.phase_s5(l)
        if lim < 5: return
        self.phase_outproj(l)
        if lim < 6: return
        self.phase_norm(l, 1)
        if lim < 7: return
        self.phase_ffn_in(l)
        if lim < 8: return
        self.phase_ffn_out(l)

    def ada_tile(self, l, t):
        pt, R_pt = self.pbanks[7]
        slot, R_w = self.wnext(('ada', l, t))
        wt = self.wview(slot, KC, 512)
        for j in range(4):
            col = t * 4 + j
            for kc in range(KC):
                self.mm(pt[:, col:col + 1], wt[:, kc, j * 128:(j + 1) * 128], self.cond[:, kc:kc + 1], kc == 0, kc == KC - 1,
                        [R_w, self.R_cond], [R_pt], not (t == 0 and j == 0 and kc == 0))

    def ada_finish(self, l):
        pt, R_pt = self.pbanks[7]
        modT, R_modT, gm, R_gm = self.modT_[l % 2], self.R_modT_[l % 2], self.gm_[l % 2], self.R_gm_[l % 2]
        self.tt(modT[:], pt[:, 0:96], self.adab[:, l, :], ALU.add, [R_pt, self.R_adab], [R_modT])
        self.stt(gm[:, 0, :], modT[:, 16:32], 1.0, self.normmix[:, l, :], ALU.add, ALU.mult, [R_modT, self.R_normmix], [R_gm])
        self.stt(gm[:, 1, :], modT[:, 64:80], 1.0, self.normffn[:, l, :], ALU.add, ALU.mult, [R_modT, self.R_normffn], [R_gm], join=True)

    def phase_ada(self, l):
        for t in range(24):
            self.ada_tile(l, t)
        self.ada_finish(l)
        self.S.barrier()

    def phase_norm(self, l, which):
        S = self.S
        NC_ = 256
        sh0 = 0 if which == 0 else 48
        with contextlib.ExitStack() as st:
            xin = [self.sb(st, 'nx%d' % i, [128, KC, NC_], F32) for i in range(2)]
            R_xin = [Res('nx%d' % i) for i in range(2)]
            sq_ = [self.sb(st, 'nsq%d' % i, [128, KC, NC_], BF16) for i in range(2)]; R_sq_ = [Res('nsq%d' % i) for i in range(2)]
            rstd_ = [self.sb(st, 'nrstd%d' % i, [128, NC_], F32) for i in range(2)]; R_rstd_ = [Res('nrstd%d' % i) for i in range(2)]
            tmp = [self.sb(st, 'ntmp%d' % i, [128, NC_], F32) for i in range(2)]
            R_tmp = [Res('ntmp%d' % i) for i in range(2)]
            for n in range(T // NC_):
                b = n % 2
                sq, R_sq, rstd, R_rstd = sq_[b], R_sq_[b], rstd_[b], R_rstd_[b]
                self.ld(xin[b][:], self.xT[:, n * NC_:(n + 1) * NC_].rearrange("(k p) t -> p k t", p=128), [self.R_xTn[n // 2]], [R_xin[b]])
                self.norm_tile(xin[b], R_xin[b], sq, R_sq, rstd, R_rstd, NC_)
                for kc in range(KC):
                    tb = kc % 2
                    self.tt(tmp[tb][:], xin[b][:, kc, :], rstd[:], ALU.mult, [R_xin[b], R_rstd], [R_tmp[tb]])
                    self.act(self.actbuf[:, kc, n * NC_:(n + 1) * NC_], tmp[tb][:], AF.Identity, [R_tmp[tb], self.R_gm, self.R_modT], [self.R_act],
                             bias=self.modT[:, sh0 + kc:sh0 + kc + 1], scale=self.gm[:, which, kc:kc + 1], join=not (n == 0 and kc == 0))
        S.barrier()

    def phase_inproj_even(self, l):
        S = self.S
        with contextlib.ExitStack() as st:
            cosT = self.sb(st, 'cosA', [128, T], F32); R_cos = Res('cosA')
            sinT = self.sb(st, 'sinA', [128, T], F32); R_sin = Res('sinA')
            self.ld(cosT[:], self.ropeT[0], [self.R_ropeT], [R_cos])
            self.ld(sinT[:], self.ropeT[1], [self.R_ropeT], [R_sin])
            qb = [self.sb(st, 'qraw%d' % i, [128, 512], BF16) for i in range(3)]; R_qb = [Res('qraw%d' % i) for i in range(3)]
            t1 = [self.sb(st, 'rt1_%d' % i, [128, 512], F32) for i in range(3)]; R_t1 = [Res('rt1_%d' % i) for i in range(3)]
            t2 = [self.sb(st, 'rt2_%d' % i, [128, 512], F32) for i in range(3)]; R_t2 = [Res('rt2_%d' % i) for i in range(3)]
            ob = [self.sb(st, 'obf%d' % i, [128, 512], BF16) for i in range(4)]; R_ob = [Res('obf%d' % i) for i in range(4)]
            of = [self.sb(st, 'of32_%d' % i, [128, 512], F32) for i in range(2)]; R_of = [Res('of32_%d' % i) for i in range(2)]
            cnt = {'q': 0, 'o': 0, 'f': 0}
            first = {'qT': True, 'kT': True, 'vv': True, 'xbT': True, 'ybT': True}
            pend = []

            def store(dst, src, R_src, name, R_dst):
                self.ld(dst, src, [R_src], [R_dst], join=not first[name])
                first[name] = False

            def flush():
                while pend:
                    pt, R_pt, i, ns, kind, ch = pend.pop(0)
                    o = cnt['o'] % 4; cnt['o'] += 1
                    p2, R_p2 = self.bank('B')
                    self.mm(p2[:], self.pswap[:, 0, :], qb[i][:], True, True, [self.R_pswap, R_qb[i]], [R_p2], False)
                    self.tt(t1[i][:], pt[:], cosT[:, ns], ALU.mult, [R_pt, R_cos, R_qb[i]], [R_t1[i]])
                    self.tt(t2[i][:], p2[:], sinT[:, ns], ALU.mult, [R_p2, R_sin], [R_t2[i]])
                    self.tt(ob[o][:], t1[i][:], t2[i][:], ALU.add, [R_t1[i], R_t2[i]], [R_ob[o]])
                    if kind == 'q':
                        store(self.qT[ch * 128:(ch + 1) * 128, ns], ob[o][:], R_ob[o], 'qT', self.R_qT)
                    else:
                        store(self.kT[ch * 128:(ch + 1) * 128, ns], ob[o][:], R_ob[o], 'kT', self.R_kT)

            for t in range(10):
                slot, R_w = self.wnext(('evin', l, t))
                if t >= getattr(self, 'ntile_lim', 99):
                    continue
                wt = self.wview(slot, KC, 512)
                kind = ['q', 'q', 'k', 'k', 'v', 'v', 'xb', 'xb', 'yb', 'yb'][t]
                half = t % 2
                if kind == 'v':
                    flush()
                    for tt_ in range(16):
                        pt, R_pt = self.bank('A')
                        for kc in range(KC):
                            self.mm(pt[:], self.actbuf[:, kc, tt_ * 128:(tt_ + 1) * 128], wt[:, kc, :], kc == 0, kc == KC - 1, [R_w, self.R_act], [R_pt], kc > 0)
                        o = cnt['o'] % 4; cnt['o'] += 1
                        self.cp(ob[o][:], pt[:], [R_pt], [R_ob[o]], eng=('act' if tt_ % 2 else 'dve'))
                        store(self.vv[tt_ * 128:(tt_ + 1) * 128, half * 512:(half + 1) * 512], ob[o][:], R_ob[o], 'vv', self.R_vv)
                    continue
                for c in range(4):
                    ch = half * 4 + c
                    for n in range(NT):
                        pt, R_pt = self.bank('A')
                        for kc in range(KC):
                            self.mm(pt[:], wt[:, kc, c * 128:(c + 1) * 128], self.actbuf[:, kc, n * 512:(n + 1) * 512], kc == 0, kc == KC - 1, [R_w, self.R_act], [R_pt], kc > 0)
                        ns = slice(n * 512, (n + 1) * 512)
                        if kind in ('q', 'k'):
                            i = cnt['q'] % 3; cnt['q'] += 1
                            self.cp(qb[i][:], pt[:], [R_pt], [R_qb[i]], eng='act')
                            flush()
                            pend.append((pt, R_pt, i, ns, kind, ch))
                        elif kind == 'xb':
                            f = cnt['f'] % 2; cnt['f'] += 1
                            self.cp(of[f][:], pt[:], [R_pt], [R_of[f]], eng=('act' if n % 2 else 'dve'))
                            store(self.xbT[ch * 128:(ch + 1) * 128, ns], of[f][:], R_of[f], 'xbT', self.R_xbT)
                        else:
                            o = cnt['o'] % 4; cnt['o'] += 1
                            self.act(ob[o][:], pt[:], AF.Gelu_apprx_tanh, [R_pt], [R_ob[o]])
                            store(self.ybT[ch * 128:(ch + 1) * 128, ns], ob[o][:], R_ob[o], 'ybT', self.R_ybT)
            flush()
        S.barrier()

    def phase_attn_A(self, l):
        S = self.S
        scale = 128.0 ** -0.5
        with contextlib.ExitStack() as st:
            qh = [self.sb(st, 'qh%d' % i, [128, T], BF16) for i in range(2)]; R_qh = [Res('qh%d' % i) for i in range(2)]
            kh = [self.sb(st, 'kh%d' % i, [128, T], BF16) for i in range(2)]; R_kh = [Res('kh%d' % i) for i in range(2)]
            v1 = [self.sb(st, 'v1_%d' % i, [128, 16, 128], BF16) for i in range(2)]; R_v1 = [Res('v1_%d' % i) for i in range(2)]
            v2 = [self.sb(st, 'v2_%d' % i, [128, 4, 4, 128], BF16) for i in range(2)]; R_v2 = [Res('v2_%d' % i) for i in range(2)]
            v3 = [self.sb(st, 'v3_%d' % i, [128, 16, 128], BF16) for i in range(2)]; R_v3 = [Res('v3_%d' % i) for i in range(2)]
            P = [self.sb(st, 'P%d' % i, [128, 512], BF16) for i in range(4)]; R_P = [Res('P%d' % i) for i in range(4)]
            rden = [self.sb(st, 'rden%d' % i, [128, 512], F32) for i in range(2)]; R_rden = [Res('rden%d' % i) for i in range(2)]
            ao = [self.sb(st, 'ao%d' % i, [128, 512], BF16) for i in range(2)]; R_ao = [Res('ao%d' % i) for i in range(2)]
            cur4 = self.masks[:, 0, :]; prev4 = self.masks[:, 1, :]
            state = {'pc': 0, 'ec': 0, 'first_store': True}
            rounds = []

            def loads(hd):
                b = hd % 2
                hs = slice(hd * 128, (hd + 1) * 128)
                self.ld(qh[b][:], self.qT[hs, :], [self.R_qT], [R_qh[b]])
                self.ld(kh[b][:], self.kT[hs, :], [self.R_kT], [R_kh[b]])
                self.ld(v1[b][:], self.vv[:, hs].rearrange("(blk j) c -> j blk c", j=128), [self.R_vv], [R_v1[b]])
                for r in range(4):
                    self.ld(v2[b][:, r, :, :], self.vv[:, hs].rearrange("(sb j r) c -> j r sb c", j=128, r=4)[:, r, :, :], [self.R_vv], [R_v2[b]], join=(r > 0))
                self.ld(v3[b][:], self.vv[:, hs].rearrange("(j r) c -> j r c", r=16), [self.R_vv], [R_v3[b]])

            for hd in range(8):
                for qt in range(NT):
                    ctx = {'O': None, 'first': True}
                    specs = []
                    for pat in (1, 2):
                        for which in ('prev', 'cur'):
                            if which == 'prev' and pat == 2 and qt == 0:
                                continue
                            specs.append((pat, which))
                    specs.append((3, 'cur'))
                    for si_, (pat, which) in enumerate(specs):
                        rd = {}

                        def S_part(hd=hd, qt=qt, pat=pat, which=which, ctx=ctx, rd=rd, si_=si_):
                            b = hd % 2
                            if qt == 0 and si_ == 0:
                                loads(hd)
                            if si_ == 0:
                                ctx['O'] = self.bank('A'); ctx['DEN'] = self.bank('A')
                                ctx['stO'] = True; ctx['stD'] = True
                            q_, k_ = qh[b], kh[b]
                            Rq, Rk = R_qh[b], R_kh[b]
                            Sb, R_S = self.bank('B')
                            p_i = state['pc'] % 4; state['pc'] += 1
                            Pt, R_Pt = P[p_i], R_P[p_i]
                            rd['Pt'] = (Pt, R_Pt)
                            if pat == 3:
                                for r in range(16):
                                    self.mm(Sb[:, r * 32:(r + 1) * 32], k_[:, r:T:16], q_[:, qt * 512 + r:(qt + 1) * 512:16], r == 0, False, [Rk, Rq], [R_S], r > 0)
                                self.mm(Sb[:], self.identb[:], self.m3[:, qt, :], False, True, [self.R_identb, self.R_m3], [R_S], True)
                                self.act(Pt[:], Sb[:], AF.Exp, [R_S], [R_Pt], scale=scale)
                                return
                            c0 = 128 if (pat == 1 and which == 'prev' and qt == 0) else 0
                            rd['c0'] = c0
                            fs = True
                            for s_ in range(4):
                                if pat == 1:
                                    B = qt * 4 + s_
                                    kb = B - 1 if which == 'prev' else B
                                    if kb < 0:
                                        continue
                                    kap = k_[:, kb * 128:(kb + 1) * 128]
                                    qap = q_[:, B * 128:(B + 1) * 128]
                                else:
                                    sbk = qt - 1 if which == 'prev' else qt
                                    kap = k_[:, sbk * 512 + s_:(sbk + 1) * 512:4]
                                    qap = q_[:, qt * 512 + s_:(qt + 1) * 512:4]
                                self.mm(Sb[:, s_ * 128:(s_ + 1) * 128], kap, qap, fs, False, [Rk, Rq], [R_S], not fs)
                                fs = False
                            msk = prev4 if which == 'prev' else cur4
                            self.mm(Sb[:, c0:512], self.identb[:], msk[:, c0:512], False, True, [self.R_identb, self.R_masks], [R_S], True)
                            self.act(Pt[:, c0:512], Sb[:, c0:512], AF.Exp, [R_S], [R_Pt], scale=scale)

                        def PV_part(hd=hd, qt=qt, pat=pat, which=which, ctx=ctx, rd=rd):
                            b = hd % 2
                            O, R_O = ctx['O']; DEN, R_DEN = ctx['DEN']
                            Pt, R_Pt = rd['Pt']

                            def pv(out_ap, lhsT, rhs, reads):
                                self.mm(out_ap, lhsT, rhs, ctx['stO'], False, reads, [R_O], not ctx['stO'])
                                ctx['stO'] = False

                            def den(out_ap, rhs, reads):
                                self.mm(out_ap, self.onesb[:], rhs, ctx['stD'], False, reads + [self.R_onesb], [R_DEN], not ctx['stD'])
                                ctx['stD'] = False

                            if pat == 3:
                                for r in range(16):
                                    pv(O[:, r:512:16], v3[b][:, r, :], Pt[:, r * 32:(r + 1) * 32], [R_v3[b], R_Pt])
                                    den(DEN[:, r:512:16], Pt[:, r * 32:(r + 1) * 32], [R_Pt])
                                return
                            c0 = rd['c0']
                            for s_ in range(4):
                                if pat == 1:
                                    B = qt * 4 + s_
                                    kb = B - 1 if which == 'prev' else B
                                    if kb < 0:
                                        continue
                                    pv(O[:, s_ * 128:(s_ + 1) * 128], v1[b][:, kb, :], Pt[:, s_ * 128:(s_ + 1) * 128], [R_v1[b], R_Pt])
                                else:
                                    sbk = qt - 1 if which == 'prev' else qt
                                    pv(O[:, s_:512:4], v2[b][:, s_, sbk, :], Pt[:, s_ * 128:(s_ + 1) * 128], [R_v2[b], R_Pt])
                                    den(DEN[:, s_:512:4], Pt[:, s_ * 128:(s_ + 1) * 128], [R_Pt])
                            if pat == 1:
                                den(DEN[:, c0:512], Pt[:, c0:512], [R_Pt])

                        post = None
                        if si_ == len(specs) - 1:
                            def post(hd=hd, qt=qt, ctx=ctx):
                                O, R_O = ctx['O']; DEN, R_DEN = ctx['DEN']
                                hs = slice(hd * 128, (hd + 1) * 128)
                                e = state['ec'] % 2; state['ec'] += 1
                                S.op('dve', lambda en, e=e, DEN=DEN: en.reciprocal(out=rden[e][:], in_=DEN[:]), reads=[R_DEN], writes=[R_rden[e]])
                                self.tt(ao[e][:], O[:], rden[e][:], ALU.mult, [R_O, R_rden[e]], [R_ao[e]])
                                self.ld(self.mixT[hs, qt * 512:(qt + 1) * 512], ao[e][:], [R_ao[e]], [self.R_mixT], join=not state['first_store'])
                                state['first_store'] = False
                        rounds.append((S_part, PV_part, post))
            prev = None
            for rnd in rounds:
                rnd[0]()
                if prev is not None:
                    prev[1]()
                    if prev[2]:
                        prev[2]()
                prev = rnd
            prev[1]()
            if prev[2]:
                prev[2]()
        S.barrier()

    def phase_lru(self, l):
        S = self.S
        e_ = l // 2
        with contextlib.ExitStack() as st:
            cw = self.sb(st, 'lcw', [128, 8, 4], F32); R_cw = Res('lcw')
            vec = self.sb(st, 'lvec', [128, 4, 8], F32); R_vec = Res('lvec')
            gwf = self.sb(st, 'lgwf', [128, 2, 8, 128], F32); R_gwf = Res('lgwf')
            gw = self.sb(st, 'lgw', [128, 2, 8, 128], BF16); R_gw = Res('lgw')
            cc = self.sb(st, 'lcc', [128, 2, 8], F32); R_cc = Res('lcc')
            self.ld(cw[:], self.ev_convw_d[:, e_], [], [R_cw])
            self.ld(vec[:], self.ev_vec_d[:, e_], [], [R_vec])
            self.ld(gwf[:], self.ev_gw_d[:, e_], [], [R_gwf])
            self.cp(gw[:], gwf[:], [R_gwf], [R_gw], eng='act')
            self.act(cc[:, 0, :], vec[:, 3, :], AF.Exp, [R_vec], [R_cc], scale=-1.0)
            self.act(cc[:, 0, :], cc[:, 0, :], AF.Ln, [R_cc], [R_cc], bias=1.0)
            self.ts(cc[:, 1, :], cc[:, 0, :], -16.0, None, ALU.mult, None, [R_cc], [R_cc])
            self.ts(cc[:, 0, :], cc[:, 0, :], -8.0, None, ALU.mult, None, [R_cc], [R_cc])
            xpad = self.sb(st, 'lxpad', [128, T + 3], F32); R_xpad = Res('lxpad')
            xc = self.sb(st, 'lxc', [128, T], F32); R_xc = Res('lxc')
            xcb = self.sb(st, 'lxcb', [128, T], BF16); R_xcb = Res('lxcb')
            rg = self.sb(st, 'lrg', [128, T], F32); R_rg = Res('lrg')
            ig = self.sb(st, 'lig', [128, T], F32); R_ig = Res('lig')
            aa = self.sb(st, 'laa', [128, T], F32); R_aa = Res('laa')
            yb = self.sb(st, 'lyb', [128, T], BF16); R_yb = Res('lyb')
            ob = self.sb(st, 'lob', [128, T], BF16); R_ob = Res('lob')
            self.memset(xpad[:, 0:3], 0.0, [R_xpad])
            for c in range(8):
                cs = slice(c * 128, (c + 1) * 128)
                self.ld(xpad[:, 3:T + 3], self.xbT[cs, :], [self.R_xbT], [R_xpad], join=True)
                self.ld(yb[:], self.ybT[cs, :], [self.R_ybT], [R_yb])
                self.ts(xc[:], xpad[:, 3:T + 3], cw[:, c, 3:4], vec[:, 0, c:c + 1], ALU.mult, ALU.add, [R_xpad, R_cw, R_vec], [R_xc])
                for i in (2, 1, 0):
                    self.stt(xc[:], xpad[:, i:T + i], cw[:, c, i:i + 1], xc[:], ALU.mult, ALU.add, [R_xpad, R_cw, R_xc], [R_xc])
                self.cp(xcb[:], xc[:], [R_xc], [R_xcb], eng='act')
                for gi, (dst, R_dst) in enumerate(((rg, R_rg), (ig, R_ig))):
                    for n in range(NT):
                        pt, R_pt = self.bank()
                        self.mm(pt[:], gw[:, gi, c, :], xcb[:, n * 512:(n + 1) * 512], True, True, [R_gw, R_xcb], [R_pt], False)
                        self.act(dst[:, n * 512:(n + 1) * 512], pt[:], AF.Sigmoid, [R_pt, R_vec], [R_dst], bias=vec[:, 1 + gi, c:c + 1], join=(n > 0))
                self.act(aa[:], rg[:], AF.Exp, [R_rg, R_cc], [R_aa], scale=cc[:, 0, c:c + 1])
                self.act(rg[:], rg[:], AF.Exp, [R_rg, R_cc], [R_rg], scale=cc[:, 1, c:c + 1])
                self.act(rg[:], rg[:], AF.Sqrt, [R_rg], [R_rg], scale=-1.0, bias=1.0)
                self.tt(ig[:], ig[:], xc[:], ALU.mult, [R_ig, R_xc], [R_ig])
                self.tt(ig[:], ig[:], rg[:], ALU.mult, [R_ig, R_rg], [R_ig])
                S.op('dve', lambda en: en.tensor_tensor_scan(out=xc[:], data0=aa[:], data1=ig[:], initial=0.0, op0=ALU.mult, op1=ALU.add),
                     reads=[R_aa, R_ig], writes=[R_xc])
                self.tt(ob[:], xc[:], yb[:], ALU.mult, [R_xc, R_yb], [R_ob])
                self.ld(self.mixT[(8 + c) * 128:(9 + c) * 128, :], ob[:], [R_ob], [self.R_mixT], join=True)
        S.barrier()


    def phase_inproj_odd(self, l):
        S = self.S
        with contextlib.ExitStack() as st:
            cosT = self.sb(st, 'cosC', [128, T], F32); R_cos = Res('cosC')
            sinT = self.sb(st, 'sinC', [128, T], F32); R_sin = Res('sinC')
            self.ld(cosT[:], self.ropeT[2], [self.R_ropeT], [R_cos])
            self.ld(sinT[:], self.ropeT[3], [self.R_ropeT], [R_sin])
            qb = [self.sb(st, 'qraw%d' % i, [128, 512], BF16) for i in range(2)]; R_qb = [Res('qraw%d' % i) for i in range(2)]
            t1 = [self.sb(st, 'rt1_%d' % i, [128, 512], F32) for i in range(2)]; R_t1 = [Res('rt1_%d' % i) for i in range(2)]
            t2 = [self.sb(st, 'rt2_%d' % i, [128, 512], F32) for i in range(2)]; R_t2 = [Res('rt2_%d' % i) for i in range(2)]
            ob = [self.sb(st, 'obf%d' % i, [128, 512], BF16) for i in range(4)]; R_ob = [Res('obf%d' % i) for i in range(4)]
            cnt = {'q': 0, 'o': 0}
            first = {'qT': True, 'kT': True, 'vv': True, 'ybT': True}

            def store(dst, src, R_src, name, R_dst):
                self.ld(dst, src, [R_src], [R_dst], join=not first[name])
                first[name] = False

            def proj_fm(wt, R_w, c, n):
                pt, R_pt = self.bank('A')
                for kc in range(KC):
                    self.mm(pt[:], wt[:, kc, c * 128:(c + 1) * 128], self.actbuf[:, kc, n * 512:(n + 1) * 512], kc == 0, kc == KC - 1, [R_w, self.R_act], [R_pt], kc > 0)
                return pt, R_pt

            def rope_store(pt, R_pt, n, dst, name, R_dst):
                ns = slice(n * 512, (n + 1) * 512)
                i = cnt['q'] % 2; cnt['q'] += 1
                o = cnt['o'] % 4; cnt['o'] += 1
                self.cp(qb[i][:], pt[:], [R_pt], [R_qb[i]], eng='act')
                p2, R_p2 = self.bank('B')
                self.mm(p2[:], self.pswap[:, 1, :], qb[i][:], True, True, [self.R_pswap, R_qb[i]], [R_p2], False)
                self.tt(t1[i][:], pt[:], cosT[:, ns], ALU.mult, [R_pt, R_cos, R_qb[i]], [R_t1[i]])
                self.tt(t2[i][:], p2[:], sinT[:, ns], ALU.mult, [R_p2, R_sin], [R_t2[i]])
                self.tt(ob[o][:], t1[i][:], t2[i][:], ALU.add, [R_t1[i], R_t2[i]], [R_ob[o]])
                store(dst, ob[o][:], R_ob[o], name, R_dst)

            for t in range(2):
                slot, R_w = self.wnext(('odq', l, t))
                wt = self.wview(slot, KC, 512)
                for c in range(4):
                    jt = t * 4 + c
                    for n in range(NT):
                        pt, R_pt = proj_fm(wt, R_w, c, n)
                        rope_store(pt, R_pt, n, self.qT[jt * 128:(jt + 1) * 128, n * 512:(n + 1) * 512], 'qT', self.R_qT)
            slot, R_w = self.wnext(('odkv', l, 0))
            wt = self.wview(slot, KC, 256)
            for n in range(NT):
                pt, R_pt = proj_fm(wt, R_w, 0, n)
                rope_store(pt, R_pt, n, self.kT[0:128, n * 512:(n + 1) * 512], 'kT', self.R_kT)
            for tt_ in range(16):
                pt, R_pt = self.bank('A')
                for kc in range(KC):
                    self.mm(pt[:, 0:128], self.actbuf[:, kc, tt_ * 128:(tt_ + 1) * 128], wt[:, kc, 128:256], kc == 0, kc == KC - 1, [R_w, self.R_act], [R_pt], kc > 0)
                o = cnt['o'] % 4; cnt['o'] += 1
                self.cp(ob[o][:, 0:128], pt[:, 0:128], [R_pt], [R_ob[o]], eng=('act' if tt_ % 2 else 'dve'))
                store(self.vv[tt_ * 128:(tt_ + 1) * 128, 0:128], ob[o][:, 0:128], R_ob[o], 'vv', self.R_vv)
            for t in range(2):
                slot, R_w = self.wnext(('odu', l, t))
                wt = self.wview(slot, KC, 512)
                for c in range(4):
                    ch = t * 4 + c
                    for n in range(NT):
                        pt, R_pt = proj_fm(wt, R_w, c, n)
                        o = cnt['o'] % 4; cnt['o'] += 1
                        self.cp(ob[o][:], pt[:], [R_pt], [R_ob[o]], eng=('act' if n % 2 else 'dve'))
                        store(self.ybT[ch * 128:(ch + 1) * 128, n * 512:(n + 1) * 512], ob[o][:], R_ob[o], 'ybT', self.R_ybT)
        S.barrier()

    def phase_attn_C(self, l):
        S = self.S
        o_ = l // 2
        scale = 64.0 ** -0.5
        with contextlib.ExitStack() as st:
            odv = self.sb(st, 'codv', [128, 3, 8], F32); R_odv = Res('codv')
            esk = self.sb(st, 'cesk', [128, 8], F32); R_esk = Res('cesk')
            self.ld(odv[:], self.odv_d[:, o_], [], [R_odv])
            self.act(esk[:], odv[:, 2, :], AF.Exp, [R_odv], [R_esk])
            kh = self.sb(st, 'ckh', [128, T], BF16); R_kh = Res('ckh')
            v1 = self.sb(st, 'cv1', [128, 16, 128], BF16); R_v1 = Res('cv1')
            self.ld(kh[:], self.kT[0:128, :], [self.R_kT], [R_kh])
            self.ld(v1[:], self.vv[:, 0:128].rearrange("(blk j) c -> j blk c", j=128), [self.R_vv], [R_v1])
            qh = [self.sb(st, 'cqh%d' % i, [128, T], BF16) for i in range(2)]; R_qh = [Res('cqh%d' % i) for i in range(2)]
            P = [self.sb(st, 'cP%d' % i, [128, 512], BF16) for i in range(4)]; R_P = [Res('cP%d' % i) for i in range(4)]
            rden = [self.sb(st, 'crden%d' % i, [128, 512], F32) for i in range(2)]; R_rden = [Res('crden%d' % i) for i in range(2)]
            ao = [self.sb(st, 'cao%d' % i, [128, 512], BF16) for i in range(2)]; R_ao = [Res('cao%d' % i) for i in range(2)]
            cur4 = self.masks[:, 0, :]; prev4 = self.masks[:, 2, :]
            state = {'pc': 0, 'ec': 0, 'first_store': True}
            rounds = []
            for jt in range(8):
                for qt in range(NT):
                    ctx = {}
                    specs = [(hb, which) for hb in range(2) for which in ('prev', 'cur')]
                    for si_, (hb, which) in enumerate(specs):
                        rd = {}

                        def S_part(jt=jt, qt=qt, hb=hb, which=which, ctx=ctx, rd=rd, si_=si_):
                            b = jt % 2
                            if qt == 0 and si_ == 0:
                                self.ld(qh[b][:], self.qT[jt * 128:(jt + 1) * 128, :], [self.R_qT], [R_qh[b]])
                            if si_ == 0:
                                ctx['O'] = self.bank('A'); ctx['DEN'] = self.bank('A')
                            q_ = qh[b]; Rq = R_qh[b]
                            ps = slice(hb * 64, (hb + 1) * 64)
                            Sb, R_S = self.bank('B')
                            p_i = state['pc'] % 4; state['pc'] += 1
                            Pt, R_Pt = P[p_i], R_P[p_i]
                            rd['Pt'] = (Pt, R_Pt)
                            c0 = 128 if (which == 'prev' and qt == 0) else 0
                            rd['c0'] = c0
                            fs = True
                            for s_ in range(4):
                                B = qt * 4 + s_
                                kb = B - 1 if which == 'prev' else B
                                if kb < 0:
                                    continue
                                self.mm(Sb[:, s_ * 128:(s_ + 1) * 128], kh[ps, kb * 128:(kb + 1) * 128], q_[ps, B * 128:(B + 1) * 128], fs, False, [R_kh, Rq], [R_S], not fs)
                                fs = False
                            msk = prev4 if which == 'prev' else cur4
                            self.mm(Sb[:, c0:512], self.identb[:], msk[:, c0:512], False, True, [self.R_identb, self.R_masks], [R_S], True)
                            self.act(Pt[:, c0:512], Sb[:, c0:512], AF.Exp, [R_S], [R_Pt], scale=scale)

                        def PV_part(jt=jt, qt=qt, hb=hb, which=which, ctx=ctx, rd=rd, si_=si_):
                            O, R_O = ctx['O']; DEN, R_DEN = ctx['DEN']
                            Pt, R_Pt = rd['Pt']; c0 = rd['c0']
                            ps = slice(hb * 64, (hb + 1) * 64)
                            for s_ in range(4):
                                B = qt * 4 + s_
                                kb = B - 1 if which == 'prev' else B
                                if kb < 0:
                                    continue
                                stO = ctx.get(('stO', hb), True)
                                self.mm(O[ps, s_ * 128:(s_ + 1) * 128], v1[:, kb, ps], Pt[:, s_ * 128:(s_ + 1) * 128], stO, False, [R_v1, R_Pt], [R_O], not (stO and hb == 0))
                                ctx[('stO', hb)] = False
                            stD = ctx.get(('stD', hb), True)
                            self.mm(DEN[ps, c0:512], self.onesb[:, 0:64], Pt[:, c0:512], stD, False, [self.R_onesb, R_Pt], [R_DEN], not (stD and hb == 0))
                            ctx[('stD', hb)] = False

                        post = None
                        if si_ == len(specs) - 1:
                            def post(jt=jt, qt=qt, ctx=ctx):
                                O, R_O = ctx['O']; DEN, R_DEN = ctx['DEN']
                                e = state['ec'] % 2; state['ec'] += 1
                                self.ts(rden[e][:], DEN[:], esk[:, jt:jt + 1], None, ALU.add, None, [R_DEN, R_esk], [R_rden[e]])
                                S.op('dve', lambda en, e=e: en.reciprocal(out=rden[e][:], in_=rden[e][:]), reads=[R_rden[e]], writes=[R_rden[e]])
                                self.tt(ao[e][:], O[:], rden[e][:], ALU.mult, [R_O, R_rden[e]], [R_ao[e]])
                                self.ld(self.mixT[jt * 128:(jt + 1) * 128, qt * 512:(qt + 1) * 512], ao[e][:], [R_ao[e]], [self.R_mixT], join=not state['first_store'])
                                state['first_store'] = False
                        rounds.append((S_part, PV_part, post))
            prev = None
            for rnd in rounds:
                rnd[0]()
                if prev is not None:
                    prev[1]()
                    if prev[2]:
                        prev[2]()
                prev = rnd
            prev[1]()
            if prev[2]:
                prev[2]()
        S.barrier()

    def LB(self, st_, r):
        return self.actbuf[:, 8 + st_ // 8, ((st_ % 8) * 2 + r) * 128:((st_ % 8) * 2 + r + 1) * 128]

    def LC(self, st_, r):
        return self.actbuf[:, 12 + st_ // 8, ((st_ % 8) * 2 + r) * 128:((st_ % 8) * 2 + r + 1) * 128]

    def phase_s5(self, l):
        S = self.S
        o_ = l // 2
        TWO_PI = float(2 * np.pi)
        R_LB = Res('LB'); R_LC = Res('LC'); R_z = Res('z')
        with contextlib.ExitStack() as st:
            odv = self.sb(st, 's5odv', [128, 3, 8], F32); R_odv = Res('s5odv')
            sm = self.sb(st, 's5sm', [128, 12, 32], F32); R_sm = Res('s5sm')
            dsh = self.sb(st, 's5dsh', [128, 11, 32], F32); R_dsh = Res('s5dsh')
            st2 = contextlib.ExitStack()
            sv = self.sb(st2, 's5v', [128, 3, 32], F32); R_sv = Res('s5v')
            Bt = self.sb(st2, 's5b', [128, 2, 32, 16], F32); R_Bt = Res('s5b')
            Ct = self.sb(st2, 's5c', [128, 2, 32, 16], F32); R_Ct = Res('s5c')
            self.ld(sv[:], self.s5v_d[:, o_], [], [R_sv])
            self.ld(Bt[:], self.s5b_d[:, o_], [], [R_Bt])
            self.ld(Ct[:], self.s5c_d[:, o_], [], [R_Ct])
            self.ld(odv[:], self.odv_d[:, o_], [], [R_odv])
            DT, LR, TH, RHO, SN, CS, X_, Y_, KRE, KIM, NKIM, TMP = [sm[:, i, :] for i in range(12)]
            rs = [R_sm]
            self.act(DT, sv[:, 2, :], AF.Exp, [R_sv], rs)
            self.tt(LR, sv[:, 0, :], DT, ALU.mult, [R_sv, R_sm], rs)
            self.tt(TH, sv[:, 1, :], DT, ALU.mult, [R_sv, R_sm], rs)
            self.act(RHO, LR, AF.Exp, rs, rs)
            for _ in range(4):
                self.ts(TMP, TH, float(np.pi), -TWO_PI, ALU.is_gt, ALU.mult, rs, rs)
                self.tt(TH, TH, TMP, ALU.add, rs, rs)
            self.act(SN, TH, AF.Sin, rs, rs)
            self.act(CS, TH, AF.Sin, rs, rs, scale=0.5)
            self.act(CS, CS, AF.Square, rs, rs)
            self.ts(CS, CS, -2.0, 1.0, ALU.mult, ALU.add, rs, rs)
            self.tt(X_, RHO, CS, ALU.mult, rs, rs)
            self.ts(X_, X_, -1.0, None, ALU.add, None, rs, rs)
            self.tt(Y_, RHO, SN, ALU.mult, rs, rs)
            self.tt(TMP, sv[:, 0, :], sv[:, 0, :], ALU.mult, [R_sv], rs)
            self.tt(KRE, sv[:, 1, :], sv[:, 1, :], ALU.mult, [R_sv], rs)
            self.tt(TMP, TMP, KRE, ALU.add, rs, rs)
            S.op('dve', lambda en: en.reciprocal(out=TMP, in_=TMP), reads=rs, writes=rs)
            self.tt(KRE, X_, sv[:, 0, :], ALU.mult, rs + [R_sv], rs)
            self.tt(KIM, Y_, sv[:, 1, :], ALU.mult, rs + [R_sv], rs)
            self.tt(KRE, KRE, KIM, ALU.add, rs, rs)
            self.tt(KRE, KRE, TMP, ALU.mult, rs, rs)
            self.tt(KIM, Y_, sv[:, 0, :], ALU.mult, rs + [R_sv], rs)
            self.tt(NKIM, X_, sv[:, 1, :], ALU.mult, rs + [R_sv], rs)
            self.tt(KIM, KIM, NKIM, ALU.subtract, rs, rs)
            self.tt(KIM, KIM, TMP, ALU.mult, rs, rs)
            self.ts(NKIM, KIM, -1.0, None, ALU.mult, None, rs, rs)
            self.ts(TMP, TH, 0.0, TWO_PI, ALU.is_lt, ALU.mult, rs, rs)
            self.tt(dsh[:, 0, :], TH, TMP, ALU.add, rs, [R_dsh])
            for k in range(10):
                self.ts(dsh[:, k + 1, :], dsh[:, k, :], 2.0, None, ALU.mult, None, [R_dsh], [R_dsh])
                self.ts(TMP, dsh[:, k + 1, :], TWO_PI, -TWO_PI, ALU.is_ge, ALU.mult, [R_dsh], rs)
                self.tt(dsh[:, k + 1, :], dsh[:, k + 1, :], TMP, ALU.add, [R_dsh, R_sm], [R_dsh])
            Mt = [self.sb(st2, 's5M%d' % i, [128, 128], F32) for i in range(2)]; R_Mt = [Res('s5M%d' % i) for i in range(2)]
            self.memset(self.actbuf[:, 8:16, :], 0.0, [self.R_act])
            mc = 0
            for s_ in range(32):
                p0 = (s_ % 4) * 32
                for r in range(2):
                    m = mc % 2; mc += 1
                    self.memset(Mt[m][:], 0.0, [R_Mt[m]])
                    for hf in range(2):
                        rows = slice(hf * 64, (hf + 1) * 64)
                        cols = slice(p0 + hf * 16, p0 + hf * 16 + 16)
                        if r == 0:
                            self.ts(Mt[m][rows, cols], Bt[rows, 0, s_, :], KRE[rows, s_:s_ + 1], None, ALU.mult, None, [R_Bt, R_sm], [R_Mt[m]], join=True)
                            self.stt(Mt[m][rows, cols], Bt[rows, 1, s_, :], NKIM[rows, s_:s_ + 1], Mt[m][rows, cols], ALU.mult, ALU.add, [R_Bt, R_sm, R_Mt[m]], [R_Mt[m]])
                        else:
                            self.ts(Mt[m][rows, cols], Bt[rows, 1, s_, :], KRE[rows, s_:s_ + 1], None, ALU.mult, None, [R_Bt, R_sm], [R_Mt[m]], join=True)
                            self.stt(Mt[m][rows, cols], Bt[rows, 0, s_, :], KIM[rows, s_:s_ + 1], Mt[m][rows, cols], ALU.mult, ALU.add, [R_Bt, R_sm, R_Mt[m]], [R_Mt[m]])
                    pt, R_pt = self.bank('B')
                    self.tr(pt[:, 0:128], Mt[m][:], [R_Mt[m]], [R_pt], False)
                    self.cp(self.LB(s_, r), pt[:, 0:128], [R_pt], [R_LB, self.R_act], eng='act', join=True)
                    for hf in range(2):
                        rows = slice(hf * 64, (hf + 1) * 64)
                        cols = slice(p0 + hf * 16, p0 + hf * 16 + 16)
                        if r == 0:
                            self.cp(self.LC(s_, 0)[rows, cols], Ct[rows, 0, s_, :], [R_Ct], [R_LC, self.R_act], join=True)
                        else:
                            self.ts(self.LC(s_, 1)[rows, cols], Ct[rows, 1, s_, :], -1.0, None, ALU.mult, None, [R_Ct], [R_LC, self.R_act], join=True)
            S.barrier()
            st2.close()
            st3 = contextlib.ExitStack()
            uc = self.sb(st3, 's5u', [128, T], BF16); R_uc = Res('s5u')
            bufA = self.sb(st3, 's5A', [128, T], F32); R_A = Res('s5A')
            bufB = self.sb(st3, 's5B', [128, T], F32); R_B = Res('s5B')
            mtmp = self.sb(st3, 's5mt', [128, 1024], F32); R_mt = Res('s5mt')
            cosT = self.sb(st3, 's5cos', [128, T], BF16); R_cos = Res('s5cos')
            sinT = self.sb(st3, 's5sin', [128, T], BF16); R_sin = Res('s5sin')
            prb = self.sb(st3, 's5prb', [128, T], BF16); R_prb = Res('s5prb')
            pib = self.sb(st3, 's5pib', [128, T], BF16); R_pib = Res('s5pib')
            bre = self.sb(st3, 's5bre', [128, T], BF16); R_bre = Res('s5bre')
            bim = self.sb(st3, 's5bim', [128, T], BF16); R_bim = Res('s5bim')
            sre = self.sb(st3, 's5sre', [128, T], BF16); R_sre = Res('s5sre')
            sim = self.sb(st3, 's5sim', [128, T], BF16); R_sim = Res('s5sim')
            zt = self.sb(st3, 's5zt', [128, 512], F32); R_zt = Res('s5zt')

            R_Ahi = Res('s5Ahi')

            def conv(lo, hi, R_Ax):
                self.act(sinT[:, lo:hi], bufA[:, lo:hi], AF.Sin, [R_Ax], [R_sin], scale=-1.0, bias=float(np.pi) - 1e-6, join=(lo > 0))
                self.act(bufB[:, lo:hi], bufA[:, lo:hi], AF.Sin, [R_Ax], [R_B], scale=0.5, join=(lo > 0))
                self.act(bufB[:, lo:hi], bufB[:, lo:hi], AF.Square, [R_B], [R_B], join=(lo > 0))
                self.act(cosT[:, lo:hi], bufB[:, lo:hi], AF.Identity, [R_B], [R_cos], scale=-2.0, bias=1.0, join=(lo > 0))

            for c in range(8):
                self.ld(uc[:], self.ybT[c * 128:(c + 1) * 128, :], [self.R_ybT], [R_uc])
                ybanks = [self.pbanks[n] for n in range(NT)]
                for si in range(4):
                    s_ = c * 4 + si
                    for n in range(NT):
                        ns = slice(n * 512, (n + 1) * 512)
                        pre, R_pre = self.bank('B')
                        pim, R_pim = self.bank('B')
                        self.mm(pre[:], self.LB(s_, 0), uc[:, ns], True, True, [R_LB, self.R_act, R_uc], [R_pre], False)
                        self.mm(pim[:], self.LB(s_, 1), uc[:, ns], True, True, [R_LB, self.R_act, R_uc], [R_pim], False)
                        self.cp(prb[:, ns], pre[:], [R_pre], [R_prb], eng='act', join=(n > 0))
                        self.cp(pib[:, ns], pim[:], [R_pim], [R_pib], eng='act', join=(n > 0))
                    self.memset(bufA[:, 0:1], 0.0, [R_A])
                    for k in range(11):
                        Lk = 1 << k
                        Rw = R_Ahi if k == 10 else R_A
                        self.ts(bufA[:, Lk:2 * Lk], bufA[:, 0:Lk], dsh[:, k, s_:s_ + 1], None, ALU.add, None, [R_A, R_dsh], [Rw])
                        self.ts(mtmp[:, 0:Lk], bufA[:, Lk:2 * Lk], TWO_PI, -TWO_PI, ALU.is_ge, ALU.mult, [Rw], [R_mt])
                        self.tt(bufA[:, Lk:2 * Lk], bufA[:, Lk:2 * Lk], mtmp[:, 0:Lk], ALU.add, [Rw, R_mt], [Rw])
                        if k == 9:
                            conv(0, 1024, R_A)
                    conv(1024, 2048, R_Ahi)
                    self.tt(bre[:], prb[:], cosT[:], ALU.mult, [R_prb, R_cos], [R_bre])
                    self.tt(sre[:], pib[:], sinT[:], ALU.mult, [R_pib, R_sin], [R_sre])
                    self.tt(bre[:], bre[:], sre[:], ALU.add, [R_bre, R_sre], [R_bre])
                    self.tt(bim[:], pib[:], cosT[:], ALU.mult, [R_pib, R_cos], [R_bim])
                    self.tt(sim[:], prb[:], sinT[:], ALU.mult, [R_prb, R_sin], [R_sim])
                    self.tt(bim[:], bim[:], sim[:], ALU.subtract, [R_bim, R_sim], [R_bim])
                    rho_b = RHO[:, s_:s_ + 1].broadcast_to([128, T])
                    S.op('dve', lambda en, rho_b=rho_b: en.tensor_tensor_scan(out=bre[:], data0=rho_b, data1=bre[:], initial=0.0, op0=ALU.mult, op1=ALU.add),
                         reads=[R_sm, R_bre], writes=[R_bre])
                    S.op('dve', lambda en, rho_b=rho_b: en.tensor_tensor_scan(out=bim[:], data0=rho_b, data1=bim[:], initial=0.0, op0=ALU.mult, op1=ALU.add),
                         reads=[R_sm, R_bim], writes=[R_bim])
                    self.tt(prb[:], bre[:], cosT[:], ALU.mult, [R_bre, R_cos], [R_prb])
                    self.tt(pib[:], bim[:], sinT[:], ALU.mult, [R_bim, R_sin], [R_pib])
                    self.tt(sre[:], prb[:], pib[:], ALU.subtract, [R_prb, R_pib], [R_sre])
                    self.tt(prb[:], bim[:], cosT[:], ALU.mult, [R_bim, R_cos], [R_prb])
                    self.tt(pib[:], bre[:], sinT[:], ALU.mult, [R_bre, R_sin], [R_pib])
                    self.tt(sim[:], prb[:], pib[:], ALU.add, [R_prb, R_pib], [R_sim])
                    for n in range(NT):
                        ns = slice(n * 512, (n + 1) * 512)
                        yb_, R_yb = ybanks[n]
                        self.mm(yb_[:], self.LC(s_, 0), sre[:, ns], si == 0, False, [R_LC, self.R_act, R_sre], [R_yb], si > 0)
                        self.mm(yb_[:], self.LC(s_, 1), sim[:, ns], False, si == 3, [R_LC, self.R_act, R_sim], [R_yb], True)
                for n in range(NT):
                    ns = slice(n * 512, (n + 1) * 512)
                    yb_, R_yb = ybanks[n]
                    self.stt(zt[:], uc[:, ns], odv[:, 0, c:c + 1], yb_[:], ALU.mult, ALU.add, [R_uc, R_odv, R_yb], [R_zt])
                    self.act(self.actbuf[:, c, ns], zt[:], AF.Gelu_apprx_tanh, [R_zt], [R_z, self.R_act], join=True)
            S.barrier()
            st3.close()
            sg = [self.sb(st, 's5sg%d' % i, [128, 512], F32) for i in range(2)]; R_sg = [Res('s5sg%d' % i) for i in range(2)]
            go = [self.sb(st, 's5go%d' % i, [128, 512], BF16) for i in range(2)]; R_go = [Res('s5go%d' % i) for i in range(2)]
            gc = 0
            for t in range(2):
                slot, R_w = self.wnext(('glu', l, t))
                wt = self.wview(slot, 8, 512)
                for c in range(4):
                    oc = t * 4 + c
                    for n in range(NT):
                        ns = slice(n * 512, (n + 1) * 512)
                        pt, R_pt = self.bank('B')
                        for kc in range(8):
                            self.mm(pt[:], wt[:, kc, c * 128:(c + 1) * 128], self.actbuf[:, kc, ns], kc == 0, kc == 7, [R_w, R_z, self.R_act], [R_pt], kc > 0)
                        g_ = gc % 2; gc += 1
                        self.act(sg[g_][:], pt[:], AF.Sigmoid, [R_pt, R_odv], [R_sg[g_]], bias=odv[:, 1, oc:oc + 1])
                        self.tt(go[g_][:], sg[g_][:], self.actbuf[:, oc, ns], ALU.mult, [R_sg[g_], R_z, self.R_act], [R_go[g_]])
                        self.ld(self.mixT[(8 + oc) * 128:(9 + oc) * 128, ns], go[g_][:], [R_go[g_]], [self.R_mixT], join=True)
        S.barrier()

    def resid_epilogue(self, pt, R_pt, gcol, dch, n, xt, R_xt, i, first):
        ns = slice(n * 512, (n + 1) * 512)
        rows = slice(dch * 128, (dch + 1) * 128)
        self.ld(xt[i][:], self.xT[rows, ns], [self.R_xTn[n]], [R_xt[i]])
        self.stt(xt[i][:], pt[:], self.modT[:, gcol + dch:gcol + dch + 1], xt[i][:], ALU.mult, ALU.add, [R_pt, self.R_modT, R_xt[i]], [R_xt[i]])
        self.ld(self.xT[rows, ns], xt[i][:], [R_xt[i]], [self.R_xTn[n]], join=True)

    def phase_outproj(self, l):
        S = self.S
        with contextlib.ExitStack() as st:
            xt = [self.sb(st, 'opx%d' % i, [128, 512], F32) for i in range(4)]; R_xt = [Res('opx%d' % i) for i in range(4)]
            R_ag = [Res('opag%d' % i) for i in range(4)]
            for gi in range(4):
                self.ld(self.actbuf[:, gi * 4:(gi + 1) * 4, :], self.mixT[gi * 512:(gi + 1) * 512, :].rearrange("(k p) t -> p k t", p=128), [self.R_mixT], [R_ag[gi]])
            cnt = 0
            for t in range(4):
                slot, R_w = self.wnext(('wout', l, t))
                wt = self.wview(slot, KC, 512)
                for c in range(4):
                    dch = t * 4 + c
                    for n in range(NT):
                        pt, R_pt = self.bank()
                        for kc in range(KC):
                            self.mm(pt[:], wt[:, kc, c * 128:(c + 1) * 128], self.actbuf[:, kc, n * 512:(n + 1) * 512], kc == 0, kc == KC - 1, [R_w, R_ag[kc // 4]], [R_pt], kc > 0)
                        self.resid_epilogue(pt, R_pt, 32, dch, n, xt, R_xt, cnt % 4, cnt == 0)
                        cnt += 1
        S.barrier()

    def phase_ffn_in(self, l):
        S = self.S
        with contextlib.ExitStack() as st:
            cw = self.sb(st, 'fcw', [128, 86, 3], F32); R_cw = Res('fcw')
            cb = self.sb(st, 'fcb', [128, 86], F32); R_cb = Res('fcb')
            self.ld(cw[:], self.ffn_convw_d[:, l], [], [R_cw])
            self.ld(cb[:], self.ffn_convb_d[:, l], [], [R_cb])
            U = [[self.sb(st, 'fU%d_%d' % (s_, br), [128, T + 2], F32) for br in range(2)] for s_ in range(2)]
            R_U = [[Res('fU%d_%d' % (s_, br)) for br in range(2)] for s_ in range(2)]
            acc = [self.sb(st, 'facc%d' % br, [128, T], F32) for br in range(2)]; R_acc = [Res('facc%d' % br) for br in range(2)]
            gg = self.sb(st, 'fgg', [128, T], BF16); R_gg = Res('fgg')
            mid = [self.sb(st, 'fmid%d' % i, [128, T], BF16) for i in range(2)]; R_mid = [Res('fmid%d' % i) for i in range(2)]
            for s_ in range(2):
                for br in range(2):
                    self.memset(U[s_][br][:, 0:2], 0.0, [R_U[s_][br]])
            jc = 0
            for t in range(22):
                slot, R_w = self.wnext(('ffin', l, t))
                j0 = 2 * t
                nj = min(2, NJ - j0)
                w = nj * 128
                wt = self.wview(slot, KC, 2 * w)
                for jj in range(nj):
                    j = j0 + jj
                    s_ = jc % 2; jc += 1
                    for br in range(2):
                        cidx = br * NJ + j
                        for n in range(NT):
                            pt, R_pt = self.bank('7')
                            for kc in range(KC):
                                self.mm(pt[:], wt[:, kc, br * w + jj * 128: br * w + (jj + 1) * 128], self.actbuf[:, kc, n * 512:(n + 1) * 512],
                                        kc == 0, kc == KC - 1, [R_w, self.R_act], [R_pt], kc > 0)
                            self.cp(U[s_][br][:, 2 + n * 512: 2 + (n + 1) * 512], pt[:], [R_pt], [R_U[s_][br]], eng='act', join=True)
                        Ub, R_Ub = U[s_][br], R_U[s_][br]
                        self.ts(acc[br][:], Ub[:, 2:T + 2], cw[:, cidx, 2:3], cb[:, cidx:cidx + 1], ALU.mult, ALU.add, [R_Ub, R_cw, R_cb], [R_acc[br]])
                        for i in (1, 0):
                            self.stt(acc[br][:], Ub[:, i:T + i], cw[:, cidx, i:i + 1], acc[br][:], ALU.mult, ALU.add, [R_Ub, R_cw, R_acc[br]], [R_acc[br]])
                    self.act(gg[:], acc[0][:], AF.Gelu_apprx_tanh, [R_acc[0]], [R_gg])
                    m = j % 2
                    self.tt(mid[m][:], gg[:], acc[1][:], ALU.mult, [R_gg, R_acc[1]], [R_mid[m]])
                    self.ld(self.midT[j * 128:(j + 1) * 128, :], mid[m][:], [R_mid[m]], [self.R_midT], join=(j > 0))
                if l + 1 < self.nlayers:
                    self.ada_tile(l + 1, t)
            if l + 1 < self.nlayers:
                for t in (22, 23):
                    self.ada_tile(l + 1, t)
                self.ada_finish(l + 1)
        S.barrier()

    def phase_ffn_out(self, l):
        S = self.S
        NA = 31
        with contextlib.ExitStack() as st:
            midB = self.sb(st, 'fmidB', [128, NJ - NA, 1024], BF16); R_midB = Res('fmidB')
            midA = self.actbuf[:].rearrange("p k t -> p (k t)").rearrange("p (j t) -> p j t", t=1024)
            xt = [self.sb(st, 'fox%d' % i, [128, 512], F32) for i in range(4)]; R_xt = [Res('fox%d' % i) for i in range(4)]
            cnt = 0; bs = 0
            grp_bounds = [(0, 8), (8, 16), (16, 24), (24, NA), (NA, 37), (37, NJ)]
            R_mg = [Res('fmg%d' % i) for i in range(6)]
            grp_of = {}
            for gi, (a_, b_) in enumerate(grp_bounds):
                for j in range(a_, b_):
                    grp_of[j] = gi
            for th in range(2):
                tsl = slice(th * 1024, (th + 1) * 1024)
                for gi, (a_, b_) in enumerate(grp_bounds):
                    dst = midA[:, a_:b_, :] if b_ <= NA else midB[:, a_ - NA:b_ - NA, :]
                    self.ld(dst, self.midT[a_ * 128:b_ * 128, tsl].rearrange("(j p) t -> p j t", p=128), [self.R_midT], [R_mg[gi]])
                for dp in range(8):
                    slot, R_w = self.wnext(('ffout', l, th, dp))
                    wt = self.wview(slot, NJ, 256)
                    base = (bs % 2) * 4; bs += 1
                    banks = [[self.pbanks[base + dd * 2 + n2] for n2 in range(2)] for dd in range(2)]
                    for j in range(NJ):
                        R_src = R_mg[grp_of[j]]
                        src = midA[:, j, :] if j < NA else midB[:, j - NA, :]
                        for dd in range(2):
                            for n2 in range(2):
                                pt, R_pt = banks[dd][n2]
                                self.mm(pt[:], wt[:, j, dd * 128:(dd + 1) * 128], src[:, n2 * 512:(n2 + 1) * 512], j == 0, j == NJ - 1, [R_w, R_src], [R_pt], j > 0)
                    for dd in range(2):
                        for n2 in range(2):
                            pt, R_pt = banks[dd][n2]
                            self.resid_epilogue(pt, R_pt, 80, dp * 2 + dd, th * 2 + n2, xt, R_xt, cnt % 4, cnt == 0)
                            cnt += 1
        S.barrier()


_CACHE = {}
BF = ml_dtypes.bfloat16


def _const_inputs():
    c = {}
    c['identf'] = np.eye(128, dtype=np.float32)
    c['identb'] = np.eye(128, dtype=np.float32).astype(BF)
    j = np.arange(128)[:, None]; i = np.arange(128)[None, :]
    cur = np.where(j <= i, 0.0, NEG).astype(np.float32)
    prevA = np.where(j >= i, 0.0, NEG).astype(np.float32)
    prevC = np.where(j > i, 0.0, NEG).astype(np.float32)
    masks = np.stack([np.tile(cur, (1, 4)), np.tile(prevA, (1, 4)), np.tile(prevC, (1, 4))], axis=1)
    c['masks'] = masks.astype(BF)
    m3 = np.stack([np.tile(cur[:, qt * 32:(qt + 1) * 32], (1, 16)) for qt in range(4)], axis=1)
    c['m3'] = m3.astype(BF)
    pa = np.zeros((128, 128), np.float32)
    for m in range(128):
        pa[(m + 64) % 128, m] = 1.0
    pc = np.zeros((128, 128), np.float32)
    for m in range(128):
        blk = m // 64; d = m % 64
        pc[blk * 64 + (d + 32) % 64, m] = 1.0
    c['pswap'] = np.stack([pa, pc], axis=1).astype(BF)
    rc = np.zeros((128, 4), np.float32)
    for p in range(128):
        rc[p, 0] = np.float32(10000.0) ** (-np.float32(p % 64) / np.float32(64))
        rc[p, 1] = -1.0 if p < 64 else 1.0
        d = p % 64
        rc[p, 2] = np.float32(10000.0) ** (-np.float32(d % 32) / np.float32(32))
        rc[p, 3] = -1.0 if d < 32 else 1.0
    c['ropec'] = rc
    return c


def _fm(v, nchunk):
    v = np.asarray(v)
    lead = v.shape[:-1]
    v = v.reshape(lead + (nchunk, 128))
    return np.ascontiguousarray(np.moveaxis(v, -1, 0))


def _shared_inputs(inp):
    m = _const_inputs()
    m['norm_final_T'] = _fm(inp['norm_final'], KC)
    m['norm_mix_T'] = _fm(inp['norm_mix'], KC)
    m['norm_ffn_T'] = _fm(inp['norm_ffn'], KC)
    m['ada_w'] = np.asarray(inp['ada_w'])
    m['ada_bT'] = _fm(inp['ada_b'], 96)
    m['ev_w_in'] = np.asarray(inp['ev_w_in'])
    m['ev_w_out'] = np.asarray(inp['ev_w_out'])
    m['ev_convw_T'] = np.ascontiguousarray(np.transpose(_fm(inp['ev_conv_w'], 8), (0, 1, 3, 2)))
    vec = np.stack([_fm(inp['ev_conv_b'], 8), _fm(inp['ev_gate_a_b'], 8), _fm(inp['ev_gate_x_b'], 8), _fm(inp['ev_lambda'], 8)], axis=2)
    m['ev_vec_T'] = np.ascontiguousarray(vec)
    ga = np.asarray(inp['ev_gate_a_w']); gx = np.asarray(inp['ev_gate_x_w'])
    gw = np.stack([ga, gx], axis=1)
    m['ev_gw'] = np.ascontiguousarray(np.transpose(gw, (3, 0, 1, 2, 4)))
    m['od_w_in'] = np.asarray(inp['od_w_in'])
    m['od_w_out'] = np.asarray(inp['od_w_out'])
    m['od_glu_w'] = np.asarray(inp['od_glu_w'])
    def st_layout(v):
        v = np.asarray(v)
        v = v.reshape((v.shape[0], 32, 128) + v.shape[2:])
        return np.ascontiguousarray(np.moveaxis(v, 2, 0))
    a_re = np.asarray(inp['od_a_re']).reshape(2, 4096); a_im = np.asarray(inp['od_a_im']).reshape(2, 4096)
    ldt = np.repeat(np.asarray(inp['od_log_dt']), 64, axis=1)
    m['s5v'] = np.ascontiguousarray(np.stack([st_layout(a_re), st_layout(a_im), st_layout(ldt)], axis=2))
    b_re = np.asarray(inp['od_b_re']).reshape(2, 4096, 16); b_im = np.asarray(inp['od_b_im']).reshape(2, 4096, 16)
    m['s5b'] = np.ascontiguousarray(np.stack([st_layout(b_re), st_layout(b_im)], axis=2))
    c_re = np.transpose(np.asarray(inp['od_c_re']), (0, 1, 3, 2)).reshape(2, 4096, 16)
    c_im = np.transpose(np.asarray(inp['od_c_im']), (0, 1, 3, 2)).reshape(2, 4096, 16)
    m['s5c'] = np.ascontiguousarray(np.stack([st_layout(c_re), st_layout(c_im)], axis=2))
    sk = np.asarray(inp['od_sinks'])
    skT = np.concatenate([np.broadcast_to(sk[:, None, 0:8], (2, 64, 8)), np.broadcast_to(sk[:, None, 8:16], (2, 64, 8))], axis=1)
    m['odv'] = np.ascontiguousarray(np.stack([_fm(inp['od_d'], 8), _fm(inp['od_glu_b'], 8), np.transpose(skT, (1, 0, 2))], axis=2))
    m['ffn_w_in'] = np.asarray(inp['ffn_w_in'])
    m['ffn_w_out'] = np.asarray(inp['ffn_w_out'])
    m['ffn_convw_T'] = np.ascontiguousarray(np.transpose(_fm(inp['ffn_conv_w'], 86), (0, 1, 3, 2)))
    m['ffn_convb_T'] = _fm(inp['ffn_conv_b'], 86)
    return m


def kernel(**inputs):
    x = np.asarray(inputs['x'])
    B = x.shape[0]
    if 'nc' not in _CACHE:
        bld = Builder()
        _CACHE['nc'] = bld.build()
        _CACHE['names'] = bld.inputs
    nc = _CACHE['nc']
    shared = _shared_inputs(inputs)
    pos = np.asarray(inputs['positions']).astype(np.int32)
    c = np.asarray(inputs['c'])
    in_maps = []
    NCORE = B
    for core in range(NCORE):
        b = core % B
        m = dict(shared)
        m['x'] = np.ascontiguousarray(x[b])
        m['pos128'] = np.ascontiguousarray(np.broadcast_to(pos[b][None, :], (128, T)))
        m['cT'] = _fm(c[b], KC)
        m = {k: v for k, v in m.items() if k in _CACHE['names']}
        in_maps.append(m)
    res = run_bass_kernel_spmd(nc, in_maps, core_ids=list(range(NCORE)))
    out = np.stack([res.results[b]['out'] for b in range(B)], axis=0)
    return out.astype(np.float32)
```

```python
import contextlib
import numpy as np
import ml_dtypes
import concourse.bass as bass
import concourse.mybir as mybir
from concourse.bass_utils import run_bass_kernel_spmd

F32 = mybir.dt.float32
BF16 = mybir.dt.bfloat16
I32 = mybir.dt.int32
ALU = mybir.AluOpType
AF = mybir.ActivationFunctionType

D = 2048
T = 2048
KC = 16
NT = 4
DEPTH = 4
DFF = 5504
NJ = 43
EPS = 1e-6
NEG = -30000.0
ENGS = ['pe', 'act', 'dve', 'pool', 'sp']


class Res:
    def __init__(self, name):
        self.name = name
        self.wr = []
        self.rd = []


class Chan:
    def __init__(self, name):
        self.name = name
        self.count = 0
        self.sem = None


def _add(lst, tok):
    if tok[0] == 'c':
        for i, t in enumerate(lst):
            if t[0] == 'c' and t[1] == tok[1]:
                if t[2] < tok[2]:
                    lst[i] = tok
                return
        lst.append(tok)
    else:
        if tok not in lst:
            lst.append(tok)


class Sched:
    def __init__(self, nc):
        self.nc = nc
        self.ops = {e: [] for e in ENGS}
        self.chans = {}
        self.pending = {e: [] for e in ENGS}

    def _deps(self, eng, reads, writes, join):
        toks = []
        for r in reads:
            toks += r.wr
        for w in writes:
            if not join:
                toks += w.wr
            toks += w.rd
        toks += self.pending[eng]
        self.pending[eng] = []
        deps = []
        for t in toks:
            if t[0] == 'd':
                deps.append(('d', t[1], t[1].count) if len(t) == 2 else t)
            else:
                deps.append(t)
        return deps

    def _commit(self, tok, reads, writes, join):
        for r in reads:
            _add(r.rd, tok)
        for w in writes:
            if join:
                _add(w.wr, tok)
            else:
                w.wr = [tok]
            w.rd = []

    def op(self, eng, fn, reads=(), writes=(), join=False):
        deps = self._deps(eng, reads, writes, join)
        idx = len(self.ops[eng])
        o = dict(kind='c', fn=fn, deps=deps, sig=False)
        self.ops[eng].append(o)
        self._commit(('c', eng, idx), reads, writes, join)
        return o

    def dma(self, eng, fn, reads=(), writes=(), join=False, chan=None):
        cname = chan or ('ch_' + writes[0].name)
        if cname not in self.chans:
            self.chans[cname] = Chan(cname)
        ch = self.chans[cname]
        deps = self._deps(eng, reads, writes, join)
        ch.count += 1
        o = dict(kind='d', fn=fn, deps=deps, chan=ch)
        self.ops[eng].append(o)
        self._commit(('d', ch), reads, writes, join)
        return o

    def barrier(self, engs=('pe', 'act', 'dve', 'sp')):
        toks = []
        for e in ENGS:
            if self.ops[e]:
                for i in range(len(self.ops[e]) - 1, -1, -1):
                    if self.ops[e][i]['kind'] == 'c':
                        toks.append(('c', e, i))
                        break
        for ch in self.chans.values():
            toks.append(('d', ch, ch.count))
        for e in engs:
            self.pending[e] = list(toks)

    def emit(self):
        nc = self.nc
        for e in ENGS:
            for o in self.ops[e]:
                for d in o['deps']:
                    if d[0] == 'c':
                        self.ops[d[1]][d[2]]['sig'] = True
        for e in ENGS:
            n = 0
            for o in self.ops[e]:
                if o['kind'] == 'c' and o['sig']:
                    n += 1
                    o['sigval'] = n
        with contextlib.ExitStack() as st:
            esem = {e: st.enter_context(nc.semaphore('s_' + e)) for e in ENGS}
            for i, (cname, ch) in enumerate(self.chans.items()):
                ch.sem = st.enter_context(nc.semaphore('d%d' % i))
            block = st.enter_context(nc.Block())

            def run(e, eng):
                seen = {}
                for o in self.ops[e]:
                    need = {}
                    for d in o['deps']:
                        if d[0] == 'c':
                            key = ('c', d[1]); sem = esem[d[1]]
                            val = self.ops[d[1]][d[2]]['sigval']
                        else:
                            key = ('d', d[1].name); sem = d[1].sem
                            val = 16 * d[2]
                        if val > need.get(key, (None, 0))[1]:
                            need[key] = (sem, val)
                    for key, (sem, val) in need.items():
                        if seen.get(key, 0) >= val:
                            continue
                        seen[key] = val
                        eng.wait_ge(sem, val)
                    ins = o['fn'](eng)
                    if o['kind'] == 'd':
                        ins.then_inc(o['chan'].sem, 16)
                    elif o['sig']:
                        ins.then_inc(esem[e], 1)
                if e == 'sp':
                    for ch in self.chans.values():
                        eng.wait_ge(ch.sem, 16 * ch.count)

            @block.tensor
            def _(eng):
                run('pe', eng)

            @block.scalar
            def _(eng):
                run('act', eng)

            @block.vector
            def _(eng):
                run('dve', eng)

            @block.gpsimd
            def _(eng):
                run('pool', eng)

            @block.sync
            def _(eng):
                run('sp', eng)


WSLOT = 11264


class Builder:
    def __init__(self, nlayers=DEPTH, dbg=False):
        self.nlayers = nlayers
        self.dbg = dbg
        self.nc = bass.Bass("TRN2", target_bir_lowering=False)
        self.S = Sched(self.nc)
        self.inputs = {}
        self.gst = contextlib.ExitStack()
        self._sbn = 0
        self.wq = []
        self.wissued = 0
        self.wtaken = 0

    def din(self, name, shape, dt=F32):
        t = self.nc.dram_tensor(name, list(shape), dt, kind="ExternalInput").ap()
        self.inputs[name] = (list(shape), dt)
        return t

    def dscr(self, name, shape, dt):
        return self.nc.dram_tensor(name, list(shape), dt, kind="Internal").ap()

    def sb(self, st, name, shape, dt):
        self._sbn += 1
        return st.enter_context(self.nc.sbuf_tensor('s%d_%s' % (self._sbn, name), list(shape), dt))

    def mm(self, out, lhsT, rhs, start, stop, reads, writes, join):
        self.S.op('pe', lambda e: e.matmul(out, lhsT=lhsT, rhs=rhs, start=start, stop=stop,
                                           skip_group_check=True), reads=reads, writes=writes, join=join)

    def tr(self, out, in_, reads, writes, join):
        self.S.op('pe', lambda e: e.transpose(out, in_, self.identf[:]), reads=list(reads) + [self.R_identf], writes=writes, join=join)

    def act(self, out, in_, func, reads, writes, bias=None, scale=None, join=False):
        kw = {}
        if bias is not None:
            kw['bias'] = bias
        if scale is not None:
            kw['scale'] = scale
        self.S.op('act', lambda e: e.activation(out=out, in_=in_, func=func, **kw), reads=reads, writes=writes, join=join)

    def tt(self, out, in0, in1, op, reads, writes, join=False):
        self.S.op('dve', lambda e: e.tensor_tensor(out=out, in0=in0, in1=in1, op=op), reads=reads, writes=writes, join=join)

    def ts(self, out, in0, s1, s2, op0, op1, reads, writes, join=False):
        if op1 is None:
            self.S.op('dve', lambda e: e.tensor_scalar(out=out, in0=in0, scalar1=s1, scalar2=None, op0=op0), reads=reads, writes=writes, join=join)
        else:
            self.S.op('dve', lambda e: e.tensor_scalar(out=out, in0=in0, scalar1=s1, scalar2=s2, op0=op0, op1=op1), reads=reads, writes=writes, join=join)

    def stt(self, out, in0, scalar, in1, op0, op1, reads, writes, join=False):
        self.S.op('dve', lambda e: e.scalar_tensor_tensor(out=out, in0=in0, scalar=scalar, in1=in1, op0=op0, op1=op1), reads=reads, writes=writes, join=join)

    def cp(self, out, in_, reads, writes, eng='dve', join=False):
        if eng == 'dve':
            self.S.op('dve', lambda e: e.tensor_copy(out=out, in_=in_), reads=reads, writes=writes, join=join)
        else:
            self.S.op('act', lambda e: e.activation(out=out, in_=in_, func=AF.Copy), reads=reads, writes=writes, join=join)

    def memset(self, ap, val, writes, join=False):
        self.S.op('dve', lambda e: e.memset(ap, val), writes=writes, join=join)

    def ld(self, out, in_, reads, writes, join=False, eng='sp', chan=None):
        self.S.dma(eng, lambda e: e.dma_start(out=out, in_=in_), reads=reads, writes=writes, join=join, chan=chan)

    def bank(self, grp=None):
        if grp is None:
            i = self.pnext % 8; self.pnext += 1
        elif grp == '7':
            i = self.p7 % 7; self.p7 += 1
        elif grp == 'A':
            i = self.pa % 4; self.pa += 1
        else:
            i = 4 + self.pbn % 4; self.pbn += 1
        return self.pbanks[i]

    def wplan(self, key, parts):
        self.wq.append((key, parts))

    def _wissue(self, upto):
        while self.wissued < min(upto, len(self.wq)):
            i = self.wissued
            slot, R = self.wslots[i % 3]
            key, parts = self.wq[i]
            for pi, (dst, src) in enumerate(parts):
                self.ld(dst(slot), src, [], [R], join=(pi > 0), eng='pool')
            self.wissued += 1

    def wnext(self, key):
        i = self.wtaken
        assert self.wq[i][0] == key, (self.wq[i][0], key)
        self._wissue(i + 3)
        self.wtaken += 1
        return self.wslots[i % 3]

    @staticmethod
    def wview(slot, k, n):
        return slot[:, 0:k * n].rearrange("p (k n) -> p k n", n=n)

    def plan_k2048(self, key, W, c0, w):
        def dst(lo, hi):
            return lambda slot: self.wview(slot, KC, w)[:, lo:hi, :]
        parts = []
        for lo in (0, 8):
            parts.append((dst(lo, lo + 8), W[lo * 128:(lo + 8) * 128, c0:c0 + w].rearrange("(k p) n -> p k n", p=128)))
        self.wplan(key, parts)

    def build(self):
        nc, S = self.nc, self.S
        g = self.gst
        L = self.nlayers
        x_d = self.din('x', [T, D])
        self.out_d = nc.dram_tensor('out', [T, D], F32, kind="ExternalOutput").ap()
        identf_d = self.din('identf', [128, 128])
        identb_d = self.din('identb', [128, 128], BF16)
        masks_d = self.din('masks', [128, 3, 512], BF16)
        m3_d = self.din('m3', [128, 4, 512], BF16)
        pswap_d = self.din('pswap', [128, 2, 128], BF16)
        ropec_d = self.din('ropec', [128, 4])
        pos_d = self.din('pos128', [128, T], I32)
        cT_d = self.din('cT', [128, KC])
        normfin_d = self.din('norm_final_T', [128, KC])
        normmix_d = self.din('norm_mix_T', [128, DEPTH, KC])
        normffn_d = self.din('norm_ffn_T', [128, DEPTH, KC])
        self.ada_w = self.din('ada_w', [L, D, 6 * D])
        adab_d = self.din('ada_bT', [128, DEPTH, 96])
        NE = (L + 1) // 2
        self.ev_w_in = self.din('ev_w_in', [NE, D, 5120])
        self.ev_w_out = self.din('ev_w_out', [NE, D, D])
        self.ev_convw_d = self.din('ev_convw_T', [128, 2, 8, 4])
        self.ev_vec_d = self.din('ev_vec_T', [128, 2, 4, 8])
        self.ev_gw_d = self.din('ev_gw', [128, 2, 2, 8, 128])
        NO = max(L // 2, 1)
        self.od_w_in = self.din('od_w_in', [NO, D, 2304])
        self.od_w_out = self.din('od_w_out', [NO, D, D])
        self.od_glu_w = self.din('od_glu_w', [NO, 1024, 1024])
        self.s5v_d = self.din('s5v', [128, 2, 3, 32])
        self.s5b_d = self.din('s5b', [128, 2, 2, 32, 16])
        self.s5c_d = self.din('s5c', [128, 2, 2, 32, 16])
        self.odv_d = self.din('odv', [128, 2, 3, 8])
        self.ffn_w_in = self.din('ffn_w_in', [L, D, 2 * DFF])
        self.ffn_w_out = self.din('ffn_w_out', [L, DFF, D])
        self.ffn_convw_d = self.din('ffn_convw_T', [128, DEPTH, 86, 3])
        self.ffn_convb_d = self.din('ffn_convb_T', [128, DEPTH, 86])
        if self.dbg:
            self.dbg_x = nc.dram_tensor('dbg_x', [D, T], F32, kind="ExternalOutput").ap()
            self.dbg_mix = nc.dram_tensor('dbg_mix', [D, T], BF16, kind="ExternalOutput").ap()
            self.dbg_h = nc.dram_tensor('dbg_h', [128, KC, T], BF16, kind="ExternalOutput").ap()
            self.dbg_mod = nc.dram_tensor('dbg_mod', [128, 96], F32, kind="ExternalOutput").ap()
        self.xT = self.dscr('xT', [D, T], F32); self.R_xTn = [Res('xT%d' % n) for n in range(NT)]
        self.qT = self.dscr('qT', [1024, T], BF16); self.R_qT = Res('qT')
        self.kT = self.dscr('kT', [1024, T], BF16); self.R_kT = Res('kT')
        self.vv = self.dscr('vv', [T, 1024], BF16); self.R_vv = Res('vv')
        self.xbT = self.dscr('xbT', [1024, T], F32); self.R_xbT = Res('xbT')
        self.ybT = self.dscr('ybT', [1024, T], BF16); self.R_ybT = Res('ybT')
        self.mixT = self.dscr('mixT', [D, T], BF16); self.R_mixT = Res('mixT')
        self.midT = self.dscr('midT', [DFF, T], BF16); self.R_midT = Res('midT')
        self.ropeT = self.dscr('ropeT', [4, 128, T], F32); self.R_ropeT = Res('ropeT')
        self.identf = self.sb(g, 'identf', [128, 128], F32); self.R_identf = Res('identf')
        self.identb = self.sb(g, 'identb', [128, 128], BF16); self.R_identb = Res('identb')
        self.onesb = self.sb(g, 'onesb', [128, 128], BF16); self.R_onesb = Res('onesb')
        self.masks = self.sb(g, 'masks', [128, 3, 512], BF16); self.R_masks = Res('masks')
        self.m3 = self.sb(g, 'm3', [128, 4, 512], BF16); self.R_m3 = Res('m3')
        self.pswap = self.sb(g, 'pswap', [128, 2, 128], BF16); self.R_pswap = Res('pswap')
        self.normfin = self.sb(g, 'normfin', [128, KC], F32); self.R_normfin = Res('normfin')
        self.normmix = self.sb(g, 'normmix', [128, DEPTH, KC], F32); self.R_normmix = Res('normmix')
        self.normffn = self.sb(g, 'normffn', [128, DEPTH, KC], F32); self.R_normffn = Res('normffn')
        self.adab = self.sb(g, 'adab', [128, DEPTH, 96], F32); self.R_adab = Res('adab')
        self.epsc = self.sb(g, 'epsc', [128, 1], F32); self.R_epsc = Res('epsc')
        self.cond = self.sb(g, 'cond', [128, KC], BF16); self.R_cond = Res('cond')
        self.modT_ = [self.sb(g, 'modT%d' % i, [128, 96], F32) for i in range(2)]; self.R_modT_ = [Res('modT%d' % i) for i in range(2)]
        self.gm_ = [self.sb(g, 'gm%d' % i, [128, 2, KC], F32) for i in range(2)]; self.R_gm_ = [Res('gm%d' % i) for i in range(2)]
        self.modT, self.R_modT, self.gm, self.R_gm = self.modT_[0], self.R_modT_[0], self.gm_[0], self.R_gm_[0]
        self.actbuf = self.sb(g, 'actbuf', [128, KC, T], BF16); self.R_act = Res('actbuf')
        self.wslots = []
        for i in range(3):
            self.wslots.append((self.sb(g, 'wslot%d' % i, [128, WSLOT], BF16), Res('wslot%d' % i)))
        self.pb = [g.enter_context(nc.psum_tensor('pb%d' % i, [128, 512], F32)) for i in range(8)]
        self.pbanks = [(self.pb[i], Res('pb%d' % i)) for i in range(8)]
        self.pnext = 0; self.pa = 0; self.pbn = 0; self.p7 = 0
        for dst, src, R in [(self.identf, identf_d, self.R_identf), (self.identb, identb_d, self.R_identb),
                            (self.masks, masks_d, self.R_masks), (self.m3, m3_d, self.R_m3), (self.pswap, pswap_d, self.R_pswap),
                            (self.normfin, normfin_d, self.R_normfin), (self.normmix, normmix_d, self.R_normmix),
                            (self.normffn, normffn_d, self.R_normffn), (self.adab, adab_d, self.R_adab)]:
            self.ld(dst[:], src, [], [R], chan='init')
        self.memset(self.onesb[:], 1.0, [self.R_onesb])
        self.memset(self.epsc[:], EPS, [self.R_epsc])

        for l in range(L):
            self.plan_layer(l)

        self.phase_cond(cT_d)
        self.phase_rope_tables(pos_d, ropec_d)
        self.phase_load_x(x_d)
        for l in range(L):
            self.layer(l)
        if self.dbg:
            self.ld(self.dbg_x, self.xT, self.R_xTn, [Res('dbgx')], chan='ch_out')
            self.ld(self.dbg_mix, self.mixT, [self.R_mixT], [Res('dbgm')], chan='ch_out')
        self.phase_final(self.out_d)
        S.emit()
        return nc

    def plan_layer(self, l):
        if l == 0:
            for t in range(24):
                self.plan_k2048(('ada', l, t), self.ada_w[l], t * 512, 512)
        if l % 2 == 0:
            e = l // 2
            for t in range(10):
                self.plan_k2048(('evin', l, t), self.ev_w_in[e], t * 512, 512)
            for t in range(4):
                self.plan_k2048(('wout', l, t), self.ev_w_out[e], t * 512, 512)
        else:
            o = l // 2
            W = self.od_w_in[o]
            for t in range(2):
                parts = []
                for c in range(4):
                    for half in range(2):
                        hd = half * 8 + t * 4 + c
                        parts.append(((lambda slot, c=c, half=half: self.wview(slot, KC, 512)[:, :, c * 128 + half * 64: c * 128 + (half + 1) * 64]),
                                      W[:, hd * 64:(hd + 1) * 64].rearrange("(k p) n -> p k n", p=128)))
                self.wplan(('odq', l, t), parts)
            self.plan_k2048(('odkv', l, 0), W, 1024, 256)
            for t in range(2):
                self.plan_k2048(('odu', l, t), W, 1280 + t * 512, 512)
            G = self.od_glu_w[o]
            for t in range(2):
                self.wplan(('glu', l, t), [((lambda slot: self.wview(slot, 8, 512)), G[:, t * 512:(t + 1) * 512].rearrange("(k p) n -> p k n", p=128))])
            W2_ = self.od_w_out[o]
            for t in range(4):
                cs_ = slice(t * 512, (t + 1) * 512)
                parts = [((lambda slot: self.wview(slot, KC, 512)[0:64, 0:8, :]), W2_[0:512, cs_].rearrange("(k p) n -> p k n", p=64)),
                         ((lambda slot: self.wview(slot, KC, 512)[64:128, 0:8, :]), W2_[512:1024, cs_].rearrange("(k p) n -> p k n", p=64)),
                         ((lambda slot: self.wview(slot, KC, 512)[:, 8:16, :]), W2_[1024:2048, cs_].rearrange("(k p) n -> p k n", p=128))]
                self.wplan(('wout', l, t), parts)
        W = self.ffn_w_in[l]
        for t in range(22):
            j0 = 2 * t
            nj = min(2, NJ - j0)
            w = nj * 128
            parts = []
            for br in range(2):
                for lo in (0, 8):
                    parts.append(((lambda slot, lo=lo, br=br, w=w: self.wview(slot, KC, 2 * w)[:, lo:lo + 8, br * w:(br + 1) * w]),
                                  W[lo * 128:(lo + 8) * 128, br * DFF + j0 * 128: br * DFF + j0 * 128 + w].rearrange("(k p) n -> p k n", p=128)))
            self.wplan(('ffin', l, t), parts)
            if l + 1 < self.nlayers:
                self.plan_k2048(('ada', l + 1, t), self.ada_w[l + 1], t * 512, 512)
        if l + 1 < self.nlayers:
            for t in (22, 23):
                self.plan_k2048(('ada', l + 1, t), self.ada_w[l + 1], t * 512, 512)
        W2 = self.ffn_w_out[l]
        for th in range(2):
            for dp in range(8):
                parts = []
                for (lo, hi) in ((0, 11), (11, 22), (22, 33), (33, 43)):
                    parts.append(((lambda slot, lo=lo, hi=hi: self.wview(slot, NJ, 256)[:, lo:hi, :]),
                                  W2[lo * 128:hi * 128, dp * 256:(dp + 1) * 256].rearrange("(k p) n -> p k n", p=128)))
                self.wplan(('ffout', l, th, dp), parts)

    def phase_cond(self, cT_d):
        with contextlib.ExitStack() as st:
            c = self.sb(st, 'c_in', [128, KC], F32); R_c = Res('c_in')
            self.ld(c[:], cT_d, [], [R_c])
            self.act(self.cond[:], c[:], AF.Silu, [R_c], [self.R_cond])
        self.S.barrier()

    def phase_rope_tables(self, pos_d, ropec_d):
        S = self.S
        with contextlib.ExitStack() as st:
            posi = self.sb(st, 'posi', [128, T], I32); R_posi = Res('posi')
            posf = self.sb(st, 'posf', [128, T], F32); R_posf = Res('posf')
            rc = self.sb(st, 'ropec', [128, 4], F32); R_rc = Res('ropec')
            ang = self.sb(st, 'ang', [128, T], F32); R_ang = Res('ang')
            ki = self.sb(st, 'ki', [128, T], I32); R_ki = Res('ki')
            kf = self.sb(st, 'kf', [128, T], F32); R_kf = Res('kf')
            tmp = self.sb(st, 'rtmp', [128, T], F32); R_tmp = Res('rtmp')
            cs = self.sb(st, 'rcos', [128, T], F32); R_cs = Res('rcos')
            self.ld(posi[:], pos_d, [], [R_posi])
            self.ld(rc[:], ropec_d, [], [R_rc])
            self.cp(posf[:], posi[:], [R_posi], [R_posf])
            C1 = 6.28125
            C2 = float(2 * np.pi - C1)
            for v in range(2):
                inv = rc[:, 2 * v:2 * v + 1]; sgn = rc[:, 2 * v + 1:2 * v + 2]
                self.ts(ang[:], posf[:], inv, None, ALU.mult, None, [R_posf, R_rc], [R_ang])
                self.ts(tmp[:], ang[:], float(1.0 / (2 * np.pi)), None, ALU.mult, None, [R_ang], [R_tmp])
                self.cp(ki[:], tmp[:], [R_tmp], [R_ki])
                self.cp(kf[:], ki[:], [R_ki], [R_kf])
                self.stt(ang[:], kf[:], -C1, ang[:], ALU.mult, ALU.add, [R_kf, R_ang], [R_ang])
                self.stt(ang[:], kf[:], -C2, ang[:], ALU.mult, ALU.add, [R_kf, R_ang], [R_ang])
                self.ts(tmp[:], ang[:], float(np.pi), float(-2 * np.pi), ALU.is_gt, ALU.mult, [R_ang], [R_tmp])
                self.tt(ang[:], ang[:], tmp[:], ALU.add, [R_ang, R_tmp], [R_ang])
                self.ts(tmp[:], ang[:], float(-np.pi), float(2 * np.pi), ALU.is_lt, ALU.mult, [R_ang], [R_tmp])
                self.tt(ang[:], ang[:], tmp[:], ALU.add, [R_ang, R_tmp], [R_ang])
                self.ts(ang[:], ang[:], 3.14159, -3.14159, ALU.min, ALU.max, [R_ang], [R_ang])
                self.act(tmp[:], ang[:], AF.Sin, [R_ang], [R_tmp])
                self.ts(tmp[:], tmp[:], sgn, None, ALU.mult, None, [R_tmp, R_rc], [R_tmp])
                self.ld(self.ropeT[2 * v + 1], tmp[:], [R_tmp], [self.R_ropeT], join=True)
                self.act(cs[:], ang[:], AF.Sin, [R_ang], [R_cs], scale=0.5)
                self.act(cs[:], cs[:], AF.Square, [R_cs], [R_cs])
                self.ts(cs[:], cs[:], -2.0, 1.0, ALU.mult, ALU.add, [R_cs], [R_cs])
                self.ld(self.ropeT[2 * v], cs[:], [R_cs], [self.R_ropeT], join=True)
        S.barrier()

    def phase_load_x(self, x_d):
        S = self.S
        with contextlib.ExitStack() as st:
            xt = [self.sb(st, 'lx_in%d' % i, [128, 4, D], F32) for i in range(1)]
            R_xt = [Res('lx_in%d' % i) for i in range(1)]
            ot = [self.sb(st, 'lx_o%d' % i, [128, 512], F32) for i in range(4)]
            R_ot = [Res('lx_o%d' % i) for i in range(4)]
            cnt = 0
            for n in range(NT):
                b = 0
                self.ld(xt[b][:], x_d[n * 512:(n + 1) * 512, :].rearrange("(a p) d -> p a d", p=128), [], [R_xt[b]])
                for kc in range(KC):
                    pt, R_pt = self.bank()
                    for a in range(4):
                        self.tr(pt[:, a * 128:(a + 1) * 128], xt[b][:, a, kc * 128:(kc + 1) * 128], [R_xt[b]], [R_pt], a > 0)
                    o = cnt % 4; cnt += 1
                    self.cp(ot[o][:], pt[:], [R_pt], [R_ot[o]], eng=('act' if kc % 2 else 'dve'))
                    self.ld(self.xT[kc * 128:(kc + 1) * 128, n * 512:(n + 1) * 512], ot[o][:], [R_ot[o]], [self.R_xTn[n]], join=(kc > 0))
        S.barrier()

    def norm_tile(self, xin, R_xin, sq, R_sq, rstd, R_rstd, ncols):
        S = self.S
        self.act(sq[:], xin[:], AF.Square, [R_xin], [R_sq])
        pt, R_pt = self.bank()
        for kc in range(KC):
            self.mm(pt[:, 0:ncols], self.onesb[:], sq[:, kc, :], kc == 0, kc == KC - 1, [self.R_onesb, R_sq], [R_pt], kc > 0)
        self.act(rstd[:], pt[:, 0:ncols], AF.Sqrt, [R_pt, self.R_epsc], [R_rstd], bias=self.epsc[:], scale=1.0 / D)
        S.op('dve', lambda e: e.reciprocal(out=rstd[:], in_=rstd[:]), reads=[R_rstd], writes=[R_rstd])

    def phase_final(self, out_d):
        S = self.S
        NC_ = 256
        with contextlib.ExitStack() as st:
            xin = [self.sb(st, 'fx%d' % i, [128, KC, NC_], F32) for i in range(1)]
            R_xin = [Res('fx%d' % i) for i in range(1)]
            sq = self.sb(st, 'fsq', [128, KC, NC_], BF16); R_sq = Res('fsq')
            rstd = self.sb(st, 'frstd', [128, NC_], F32); R_rstd = Res('frstd')
            yt = self.sb(st, 'fy', [128, KC, NC_], F32); R_yt = Res('fy')
            ot = [self.sb(st, 'fo%d' % i, [128, D], F32) for i in range(2)]
            R_ot = [Res('fo%d' % i) for i in range(2)]
            oc = 0
            for n in range(T // NC_):
                b = 0
                self.ld(xin[b][:], self.xT[:, n * NC_:(n + 1) * NC_].rearrange("(k p) t -> p k t", p=128), [self.R_xTn[n // 2]], [R_xin[b]])
                self.norm_tile(xin[b], R_xin[b], sq, R_sq, rstd, R_rstd, NC_)
                for kc in range(KC):
                    self.stt(yt[:, kc, :], xin[b][:, kc, :], self.normfin[:, kc:kc + 1], rstd[:], ALU.mult, ALU.mult,
                             [R_xin[b], self.R_normfin, R_rstd], [R_yt], join=(kc > 0))
                for a in range(NC_ // 128):
                    o = oc % 2; oc += 1
                    for q in range(4):
                        pt, R_pt = self.bank()
                        for j in range(4):
                            kc = q * 4 + j
                            self.tr(pt[:, j * 128:(j + 1) * 128], yt[:, kc, a * 128:(a + 1) * 128], [R_yt], [R_pt], j > 0)
                        self.cp(ot[o][:, q * 512:(q + 1) * 512], pt[:], [R_pt], [R_ot[o]], eng=('act' if q % 2 else 'dve'), join=(q > 0))
                    r0 = n * NC_ + a * 128
                    self.ld(out_d[r0:r0 + 128, :], ot[o][:], [R_ot[o]], [Res('outd')], chan='ch_out')
        S.barrier()

    def layer(self, l):
        self.modT, self.R_modT, self.gm, self.R_gm = self.modT_[l % 2], self.R_modT_[l % 2], self.gm_[l % 2], self.R_gm_[l % 2]
        stop = getattr(self, 'stop', None)
        order = ['ada', 'norm', 'inproj', 'attn', 'lru', 'outproj', 'norm2', 'ffn_in', 'ffn_out']
        lim = order.index(stop) if (stop and l == self.nlayers - 1) else 99
        if l == 0:
            self.phase_ada(l)
        if lim < 1: return
        self.phase_norm(l, 0)
        if self.dbg and l == self.nlayers - 1:
            self.ld(self.dbg_h, self.actbuf[:], [self.R_act], [Res('dbgh')], chan='ch_out')
            self.ld(self.dbg_mod, self.modT[:], [self.R_modT], [Res('dbgmod')], chan='ch_out')
        if lim < 2: return
        if l % 2 == 0:
            self.phase_inproj_even(l)
            if lim < 3: return
            self.phase_attn_A(l)
            if lim < 4: return
            self.phase_lru(l)
        else:
            self.phase_inproj_odd(l)
            if lim < 3: return
            self.phase_attn_C(l)
            if lim < 4: return
            self.phase_s5(l)
        if lim < 5: return
        self.phase_outproj(l)
        if lim < 6: return
        self.phase_norm(l, 1)
        if lim < 7: return
        self.phase_ffn_in(l)
        if lim < 8: return
        self.phase_ffn_out(l)

    def ada_tile(self, l, t):
        pt, R_pt = self.pbanks[7]
        slot, R_w = self.wnext(('ada', l, t))
        wt = self.wview(slot, KC, 512)
        for j in range(4):
            col = t * 4 + j
            for kc in range(KC):
                self.mm(pt[:, col:col + 1], wt[:, kc, j * 128:(j + 1) * 128], self.cond[:, kc:kc + 1], kc == 0, kc == KC - 1,
                        [R_w, self.R_cond], [R_pt], not (t == 0 and j == 0 and kc == 0))

    def ada_finish(self, l):
        pt, R_pt = self.pbanks[7]
        modT, R_modT, gm, R_gm = self.modT_[l % 2], self.R_modT_[l % 2], self.gm_[l % 2], self.R_gm_[l % 2]
        self.tt(modT[:], pt[:, 0:96], self.adab[:, l, :], ALU.add, [R_pt, self.R_adab], [R_modT])
        self.stt(gm[:, 0, :], modT[:, 16:32], 1.0, self.normmix[:, l, :], ALU.add, ALU.mult, [R_modT, self.R_normmix], [R_gm])
        self.stt(gm[:, 1, :], modT[:, 64:80], 1.0, self.normffn[:, l, :], ALU.add, ALU.mult, [R_modT, self.R_normffn], [R_gm], join=True)

    def phase_ada(self, l):
        for t in range(24):
            self.ada_tile(l, t)
        self.ada_finish(l)
        self.S.barrier()

    def phase_norm(self, l, which):
        S = self.S
        NC_ = 256
        sh0 = 0 if which == 0 else 48
        with contextlib.ExitStack() as st:
            xin = [self.sb(st, 'nx%d' % i, [128, KC, NC_], F32) for i in range(2)]
            R_xin = [Res('nx%d' % i) for i in range(2)]
            sq_ = [self.sb(st, 'nsq%d' % i, [128, KC, NC_], BF16) for i in range(2)]; R_sq_ = [Res('nsq%d' % i) for i in range(2)]
            rstd_ = [self.sb(st, 'nrstd%d' % i, [128, NC_], F32) for i in range(2)]; R_rstd_ = [Res('nrstd%d' % i) for i in range(2)]
            tmp = [self.sb(st, 'ntmp%d' % i, [128, NC_], F32) for i in range(2)]
            R_tmp = [Res('ntmp%d' % i) for i in range(2)]
            for n in range(T // NC_):
                b = n % 2
                sq, R_sq, rstd, R_rstd = sq_[b], R_sq_[b], rstd_[b], R_rstd_[b]
                self.ld(xin[b][:], self.xT[:, n * NC_:(n + 1) * NC_].rearrange("(k p) t -> p k t", p=128), [self.R_xTn[n // 2]], [R_xin[b]])
                self.norm_tile(xin[b], R_xin[b], sq, R_sq, rstd, R_rstd, NC_)
                for kc in range(KC):
                    tb = kc % 2
                    self.tt(tmp[tb][:], xin[b][:, kc, :], rstd[:], ALU.mult, [R_xin[b], R_rstd], [R_tmp[tb]])
                    self.act(self.actbuf[:, kc, n * NC_:(n + 1) * NC_], tmp[tb][:], AF.Identity, [R_tmp[tb], self.R_gm, self.R_modT], [self.R_act],
                             bias=self.modT[:, sh0 + kc:sh0 + kc + 1], scale=self.gm[:, which, kc:kc + 1], join=not (n == 0 and kc == 0))
        S.barrier()

    def phase_inproj_even(self, l):
        S = self.S
        with contextlib.ExitStack() as st:
            cosT = self.sb(st, 'cosA', [128, T], F32); R_cos = Res('cosA')
            sinT = self.sb(st, 'sinA', [128, T], F32); R_sin = Res('sinA')
            self.ld(cosT[:], self.ropeT[0], [self.R_ropeT], [R_cos])
            self.ld(sinT[:], self.ropeT[1], [self.R_ropeT], [R_sin])
            qb = [self.sb(st, 'qraw%d' % i, [128, 512], BF16) for i in range(3)]; R_qb = [Res('qraw%d' % i) for i in range(3)]
            t1 = [self.sb(st, 'rt1_%d' % i, [128, 512], F32) for i in range(3)]; R_t1 = [Res('rt1_%d' % i) for i in range(3)]
            t2 = [self.sb(st, 'rt2_%d' % i, [128, 512], F32) for i in range(3)]; R_t2 = [Res('rt2_%d' % i) for i in range(3)]
            ob = [self.sb(st, 'obf%d' % i, [128, 512], BF16) for i in range(4)]; R_ob = [Res('obf%d' % i) for i in range(4)]
            of = [self.sb(st, 'of32_%d' % i, [128, 512], F32) for i in range(2)]; R_of = [Res('of32_%d' % i) for i in range(2)]
            cnt = {'q': 0, 'o': 0, 'f': 0}
            first = {'qT': True, 'kT': True, 'vv': True, 'xbT': True, 'ybT': True}
            pend = []

            def store(dst, src, R_src, name, R_dst):
                self.ld(dst, src, [R_src], [R_dst], join=not first[name])
                first[name] = False

            def flush():
                while pend:
                    pt, R_pt, i, ns, kind, ch = pend.pop(0)
                    o = cnt['o'] % 4; cnt['o'] += 1
                    p2, R_p2 = self.bank('B')
                    self.mm(p2[:], self.pswap[:, 0, :], qb[i][:], True, True, [self.R_pswap, R_qb[i]], [R_p2], False)
                    self.tt(t1[i][:], pt[:], cosT[:, ns], ALU.mult, [R_pt, R_cos, R_qb[i]], [R_t1[i]])
                    self.tt(t2[i][:], p2[:], sinT[:, ns], ALU.mult, [R_p2, R_sin], [R_t2[i]])
                    self.tt(ob[o][:], t1[i][:], t2[i][:], ALU.add, [R_t1[i], R_t2[i]], [R_ob[o]])
                    if kind == 'q':
                        store(self.qT[ch * 128:(ch + 1) * 128, ns], ob[o][:], R_ob[o], 'qT', self.R_qT)
                    else:
                        store(self.kT[ch * 128:(ch + 1) * 128, ns], ob[o][:], R_ob[o], 'kT', self.R_kT)

            for t in range(10):
                slot, R_w = self.wnext(('evin', l, t))
                if t >= getattr(self, 'ntile_lim', 99):
                    continue
                wt = self.wview(slot, KC, 512)
                kind = ['q', 'q', 'k', 'k', 'v', 'v', 'xb', 'xb', 'yb', 'yb'][t]
                half = t % 2
                if kind == 'v':
                    flush()
                    for tt_ in range(16):
                        pt, R_pt = self.bank('A')
                        for kc in range(KC):
                            self.mm(pt[:], self.actbuf[:, kc, tt_ * 128:(tt_ + 1) * 128], wt[:, kc, :], kc == 0, kc == KC - 1, [R_w, self.R_act], [R_pt], kc > 0)
                        o = cnt['o'] % 4; cnt['o'] += 1
                        self.cp(ob[o][:], pt[:], [R_pt], [R_ob[o]], eng=('act' if tt_ % 2 else 'dve'))
                        store(self.vv[tt_ * 128:(tt_ + 1) * 128, half * 512:(half + 1) * 512], ob[o][:], R_ob[o], 'vv', self.R_vv)
                    continue
                for c in range(4):
                    ch = half * 4 + c
                    for n in range(NT):
                        pt, R_pt = self.bank('A')
                        for kc in range(KC):
                            self.mm(pt[:], wt[:, kc, c * 128:(c + 1) * 128], self.actbuf[:, kc, n * 512:(n + 1) * 512], kc == 0, kc == KC - 1, [R_w, self.R_act], [R_pt], kc > 0)
                        ns = slice(n * 512, (n + 1) * 512)
                        if kind in ('q', 'k'):
                            i = cnt['q'] % 3; cnt['q'] += 1
                            self.cp(qb[i][:], pt[:], [R_pt], [R_qb[i]], eng='act')
                            flush()
                            pend.append((pt, R_pt, i, ns, kind, ch))
                        elif kind == 'xb':
                            f = cnt['f'] % 2; cnt['f'] += 1
                            self.cp(of[f][:], pt[:], [R_pt], [R_of[f]], eng=('act' if n % 2 else 'dve'))
                            store(self.xbT[ch * 128:(ch + 1) * 128, ns], of[f][:], R_of[f], 'xbT', self.R_xbT)
                        else:
                            o = cnt['o'] % 4; cnt['o'] += 1
                            self.act(ob[o][:], pt[:], AF.Gelu_apprx_tanh, [R_pt], [R_ob[o]])
                            store(self.ybT[ch * 128:(ch + 1) * 128, ns], ob[o][:], R_ob[o], 'ybT', self.R_ybT)
            flush()
        S.barrier()

    def phase_attn_A(self, l):
        S = self.S
        scale = 128.0 ** -0.5
        with contextlib.ExitStack() as st:
            qh = [self.sb(st, 'qh%d' % i, [128, T], BF16) for i in range(2)]; R_qh = [Res('qh%d' % i) for i in range(2)]
            kh = [self.sb(st, 'kh%d' % i, [128, T], BF16) for i in range(2)]; R_kh = [Res('kh%d' % i) for i in range(2)]
            v1 = [self.sb(st, 'v1_%d' % i, [128, 16, 128], BF16) for i in range(2)]; R_v1 = [Res('v1_%d' % i) for i in range(2)]
            v2 = [self.sb(st, 'v2_%d' % i, [128, 4, 4, 128], BF16) for i in range(2)]; R_v2 = [Res('v2_%d' % i) for i in range(2)]
            v3 = [self.sb(st, 'v3_%d' % i, [128, 16, 128], BF16) for i in range(2)]; R_v3 = [Res('v3_%d' % i) for i in range(2)]
            P = [self.sb(st, 'P%d' % i, [128, 512], BF16) for i in range(4)]; R_P = [Res('P%d' % i) for i in range(4)]
            rden = [self.sb(st, 'rden%d' % i, [128, 512], F32) for i in range(2)]; R_rden = [Res('rden%d' % i) for i in range(2)]
            ao = [self.sb(st, 'ao%d' % i, [128, 512], BF16) for i in range(2)]; R_ao = [Res('ao%d' % i) for i in range(2)]
            cur4 = self.masks[:, 0, :]; prev4 = self.masks[:, 1, :]
            state = {'pc': 0, 'ec': 0, 'first_store': True}
            rounds = []

            def loads(hd):
                b = hd % 2
                hs = slice(hd * 128, (hd + 1) * 128)
                self.ld(qh[b][:], self.qT[hs, :], [self.R_qT], [R_qh[b]])
                self.ld(kh[b][:], self.kT[hs, :], [self.R_kT], [R_kh[b]])
                self.ld(v1[b][:], self.vv[:, hs].rearrange("(blk j) c -> j blk c", j=128), [self.R_vv], [R_v1[b]])
                for r in range(4):
                    self.ld(v2[b][:, r, :, :], self.vv[:, hs].rearrange("(sb j r) c -> j r sb c", j=128, r=4)[:, r, :, :], [self.R_vv], [R_v2[b]], join=(r > 0))
                self.ld(v3[b][:], self.vv[:, hs].rearrange("(j r) c -> j r c", r=16), [self.R_vv], [R_v3[b]])

            for hd in range(8):
                for qt in range(NT):
                    ctx = {'O': None, 'first': True}
                    specs = []
                    for pat in (1, 2):
                        for which in ('prev', 'cur'):
                            if which == 'prev' and pat == 2 and qt == 0:
                                continue
                            specs.append((pat, which))
                    specs.append((3, 'cur'))
                    for si_, (pat, which) in enumerate(specs):
                        rd = {}

                        def S_part(hd=hd, qt=qt, pat=pat, which=which, ctx=ctx, rd=rd, si_=si_):
                            b = hd % 2
                            if qt == 0 and si_ == 0:
                                loads(hd)
                            if si_ == 0:
                                ctx['O'] = self.bank('A'); ctx['DEN'] = self.bank('A')
                                ctx['stO'] = True; ctx['stD'] = True
                            q_, k_ = qh[b], kh[b]
                            Rq, Rk = R_qh[b], R_kh[b]
                            Sb, R_S = self.bank('B')
                            p_i = state['pc'] % 4; state['pc'] += 1
                            Pt, R_Pt = P[p_i], R_P[p_i]
                            rd['Pt'] = (Pt, R_Pt)
                            if pat == 3:
                                for r in range(16):
                                    self.mm(Sb[:, r * 32:(r + 1) * 32], k_[:, r:T:16], q_[:, qt * 512 + r:(qt + 1) * 512:16], r == 0, False, [Rk, Rq], [R_S], r > 0)
                                self.mm(Sb[:], self.identb[:], self.m3[:, qt, :], False, True, [self.R_identb, self.R_m3], [R_S], True)
                                self.act(Pt[:], Sb[:], AF.Exp, [R_S], [R_Pt], scale=scale)
                                return
                            c0 = 128 if (pat == 1 and which == 'prev' and qt == 0) else 0
                            rd['c0'] = c0
                            fs = True
                            for s_ in range(4):
                                if pat == 1:
                                    B = qt * 4 + s_
                                    kb = B - 1 if which == 'prev' else B
                                    if kb < 0:
                                        continue
                                    kap = k_[:, kb * 128:(kb + 1) * 128]
                                    qap = q_[:, B * 128:(B + 1) * 128]
                                else:
                                    sbk = qt - 1 if which == 'prev' else qt
                                    kap = k_[:, sbk * 512 + s_:(sbk + 1) * 512:4]
                                    qap = q_[:, qt * 512 + s_:(qt + 1) * 512:4]
                                self.mm(Sb[:, s_ * 128:(s_ + 1) * 128], kap, qap, fs, False, [Rk, Rq], [R_S], not fs)
                                fs = False
                            msk = prev4 if which == 'prev' else cur4
                            self.mm(Sb[:, c0:512], self.identb[:], msk[:, c0:512], False, True, [self.R_identb, self.R_masks], [R_S], True)
                            self.act(Pt[:, c0:512], Sb[:, c0:512], AF.Exp, [R_S], [R_Pt], scale=scale)

                        def PV_part(hd=hd, qt=qt, pat=pat, which=which, ctx=ctx, rd=rd):
                            b = hd % 2
                            O, R_O = ctx['O']; DEN, R_DEN = ctx['DEN']
                            Pt, R_Pt = rd['Pt']

                            def pv(out_ap, lhsT, rhs, reads):
                                self.mm(out_ap, lhsT, rhs, ctx['stO'], False, reads, [R_O], not ctx['stO'])
                                ctx['stO'] = False

                            def den(out_ap, rhs, reads):
                                self.mm(out_ap, self.onesb[:], rhs, ctx['stD'], False, reads + [self.R_onesb], [R_DEN], not ctx['stD'])
                                ctx['stD'] = False

                            if pat == 3:
                                for r in range(16):
                                    pv(O[:, r:512:16], v3[b][:, r, :], Pt[:, r * 32:(r + 1) * 32], [R_v3[b], R_Pt])
                                    den(DEN[:, r:512:16], Pt[:, r * 32:(r + 1) * 32], [R_Pt])
                                return
                            c0 = rd['c0']
                            for s_ in range(4):
                                if pat == 1:
                                    B = qt * 4 + s_
                                    kb = B - 1 if which == 'prev' else B
                                    if kb < 0:
                                        continue
                                    pv(O[:, s_ * 128:(s_ + 1) * 128], v1[b][:, kb, :], Pt[:, s_ * 128:(s_ + 1) * 128], [R_v1[b], R_Pt])
                                else:
                                    sbk = qt - 1 if which == 'prev' else qt
                                    pv(O[:, s_:512:4], v2[b][:, s_, sbk, :], Pt[:, s_ * 128:(s_ + 1) * 128], [R_v2[b], R_Pt])
                                    den(DEN[:, s_:512:4], Pt[:, s_ * 128:(s_ + 1) * 128], [R_Pt])
                            if pat == 1:
                                den(DEN[:, c0:512], Pt[:, c0:512], [R_Pt])

                        post = None
                        if si_ == len(specs) - 1:
                            def post(hd=hd, qt=qt, ctx=ctx):
                                O, R_O = ctx['O']; DEN, R_DEN = ctx['DEN']
                                hs = slice(hd * 128, (hd + 1) * 128)
                                e = state['ec'] % 2; state['ec'] += 1
                                S.op('dve', lambda en, e=e, DEN=DEN: en.reciprocal(out=rden[e][:], in_=DEN[:]), reads=[R_DEN], writes=[R_rden[e]])
                                self.tt(ao[e][:], O[:], rden[e][:], ALU.mult, [R_O, R_rden[e]], [R_ao[e]])
                                self.ld(self.mixT[hs, qt * 512:(qt + 1) * 512], ao[e][:], [R_ao[e]], [self.R_mixT], join=not state['first_store'])
                                state['first_store'] = False
                        rounds.append((S_part, PV_part, post))
            prev = None
            for rnd in rounds:
                rnd[0]()
                if prev is not None:
                    prev[1]()
                    if prev[2]:
                        prev[2]()
                prev = rnd
            prev[1]()
            if prev[2]:
                prev[2]()
        S.barrier()

    def phase_lru(self, l):
        S = self.S
        e_ = l // 2
        with contextlib.ExitStack() as st:
            cw = self.sb(st, 'lcw', [128, 8, 4], F32); R_cw = Res('lcw')
            vec = self.sb(st, 'lvec', [128, 4, 8], F32); R_vec = Res('lvec')
            gwf = self.sb(st, 'lgwf', [128, 2, 8, 128], F32); R_gwf = Res('lgwf')
            gw = self.sb(st, 'lgw', [128, 2, 8, 128], BF16); R_gw = Res('lgw')
            cc = self.sb(st, 'lcc', [128, 2, 8], F32); R_cc = Res('lcc')
            self.ld(cw[:], self.ev_convw_d[:, e_], [], [R_cw])
            self.ld(vec[:], self.ev_vec_d[:, e_], [], [R_vec])
            self.ld(gwf[:], self.ev_gw_d[:, e_], [], [R_gwf])
            self.cp(gw[:], gwf[:], [R_gwf], [R_gw], eng='act')
            self.act(cc[:, 0, :], vec[:, 3, :], AF.Exp, [R_vec], [R_cc], scale=-1.0)
            self.act(cc[:, 0, :], cc[:, 0, :], AF.Ln, [R_cc], [R_cc], bias=1.0)
            self.ts(cc[:, 1, :], cc[:, 0, :], -16.0, None, ALU.mult, None, [R_cc], [R_cc])
            self.ts(cc[:, 0, :], cc[:, 0, :], -8.0, None, ALU.mult, None, [R_cc], [R_cc])
            xpad = self.sb(st, 'lxpad', [128, T + 3], F32); R_xpad = Res('lxpad')
            xc = self.sb(st, 'lxc', [128, T], F32); R_xc = Res('lxc')
            xcb = self.sb(st, 'lxcb', [128, T], BF16); R_xcb = Res('lxcb')
            rg = self.sb(st, 'lrg', [128, T], F32); R_rg = Res('lrg')
            ig = self.sb(st, 'lig', [128, T], F32); R_ig = Res('lig')
            aa = self.sb(st, 'laa', [128, T], F32); R_aa = Res('laa')
            yb = self.sb(st, 'lyb', [128, T], BF16); R_yb = Res('lyb')
            ob = self.sb(st, 'lob', [128, T], BF16); R_ob = Res('lob')
            self.memset(xpad[:, 0:3], 0.0, [R_xpad])
            for c in range(8):
                cs = slice(c * 128, (c + 1) * 128)
                self.ld(xpad[:, 3:T + 3], self.xbT[cs, :], [self.R_xbT], [R_xpad], join=True)
                self.ld(yb[:], self.ybT[cs, :], [self.R_ybT], [R_yb])
                self.ts(xc[:], xpad[:, 3:T + 3], cw[:, c, 3:4], vec[:, 0, c:c + 1], ALU.mult, ALU.add, [R_xpad, R_cw, R_vec], [R_xc])
                for i in (2, 1, 0):
                    self.stt(xc[:], xpad[:, i:T + i], cw[:, c, i:i + 1], xc[:], ALU.mult, ALU.add, [R_xpad, R_cw, R_xc], [R_xc])
                self.cp(xcb[:], xc[:], [R_xc], [R_xcb], eng='act')
                for gi, (dst, R_dst) in enumerate(((rg, R_rg), (ig, R_ig))):
                    for n in range(NT):
                        pt, R_pt = self.bank()
                        self.mm(pt[:], gw[:, gi, c, :], xcb[:, n * 512:(n + 1) * 512], True, True, [R_gw, R_xcb], [R_pt], False)
                        self.act(dst[:, n * 512:(n + 1) * 512], pt[:], AF.Sigmoid, [R_pt, R_vec], [R_dst], bias=vec[:, 1 + gi, c:c + 1], join=(n > 0))
                self.act(aa[:], rg[:], AF.Exp, [R_rg, R_cc], [R_aa], scale=cc[:, 0, c:c + 1])
                self.act(rg[:], rg[:], AF.Exp, [R_rg, R_cc], [R_rg], scale=cc[:, 1, c:c + 1])
                self.act(rg[:], rg[:], AF.Sqrt, [R_rg], [R_rg], scale=-1.0, bias=1.0)
                self.tt(ig[:], ig[:], xc[:], ALU.mult, [R_ig, R_xc], [R_ig])
                self.tt(ig[:], ig[:], rg[:], ALU.mult, [R_ig, R_rg], [R_ig])
                S.op('dve', lambda en: en.tensor_tensor_scan(out=xc[:], data0=aa[:], data1=ig[:], initial=0.0, op0=ALU.mult, op1=ALU.add),
                     reads=[R_aa, R_ig], writes=[R_xc])
                self.tt(ob[:], xc[:], yb[:], ALU.mult, [R_xc, R_yb], [R_ob])
                self.ld(self.mixT[(8 + c) * 128:(9 + c) * 128, :], ob[:], [R_ob], [self.R_mixT], join=True)
        S.barrier()


    def phase_inproj_odd(self, l):
        S = self.S
        with contextlib.ExitStack() as st:
            cosT = self.sb(st, 'cosC', [128, T], F32); R_cos = Res('cosC')
            sinT = self.sb(st, 'sinC', [128, T], F32); R_sin = Res('sinC')
            self.ld(cosT[:], self.ropeT[2], [self.R_ropeT], [R_cos])
            self.ld(sinT[:], self.ropeT[3], [self.R_ropeT], [R_sin])
            qb = [self.sb(st, 'qraw%d' % i, [128, 512], BF16) for i in range(2)]; R_qb = [Res('qraw%d' % i) for i in range(2)]
            t1 = [self.sb(st, 'rt1_%d' % i, [128, 512], F32) for i in range(2)]; R_t1 = [Res('rt1_%d' % i) for i in range(2)]
            t2 = [self.sb(st, 'rt2_%d' % i, [128, 512], F32) for i in range(2)]; R_t2 = [Res('rt2_%d' % i) for i in range(2)]
            ob = [self.sb(st, 'obf%d' % i, [128, 512], BF16) for i in range(4)]; R_ob = [Res('obf%d' % i) for i in range(4)]
            cnt = {'q': 0, 'o': 0}
            first = {'qT': True, 'kT': True, 'vv': True, 'ybT': True}

            def store(dst, src, R_src, name, R_dst):
                self.ld(dst, src, [R_src], [R_dst], join=not first[name])
                first[name] = False

            def proj_fm(wt, R_w, c, n):
                pt, R_pt = self.bank('A')
                for kc in range(KC):
                    self.mm(pt[:], wt[:, kc, c * 128:(c + 1) * 128], self.actbuf[:, kc, n * 512:(n + 1) * 512], kc == 0, kc == KC - 1, [R_w, self.R_act], [R_pt], kc > 0)
                return pt, R_pt

            def rope_store(pt, R_pt, n, dst, name, R_dst):
                ns = slice(n * 512, (n + 1) * 512)
                i = cnt['q'] % 2; cnt['q'] += 1
                o = cnt['o'] % 4; cnt['o'] += 1
                self.cp(qb[i][:], pt[:], [R_pt], [R_qb[i]], eng='act')
                p2, R_p2 = self.bank('B')
                self.mm(p2[:], self.pswap[:, 1, :], qb[i][:], True, True, [self.R_pswap, R_qb[i]], [R_p2], False)
                self.tt(t1[i][:], pt[:], cosT[:, ns], ALU.mult, [R_pt, R_cos, R_qb[i]], [R_t1[i]])
                self.tt(t2[i][:], p2[:], sinT[:, ns], ALU.mult, [R_p2, R_sin], [R_t2[i]])
                self.tt(ob[o][:], t1[i][:], t2[i][:], ALU.add, [R_t1[i], R_t2[i]], [R_ob[o]])
                store(dst, ob[o][:], R_ob[o], name, R_dst)

            for t in range(2):
                slot, R_w = self.wnext(('odq', l, t))
                wt = self.wview(slot, KC, 512)
                for c in range(4):
                    jt = t * 4 + c
                    for n in range(NT):
                        pt, R_pt = proj_fm(wt, R_w, c, n)
                        rope_store(pt, R_pt, n, self.qT[jt * 128:(jt + 1) * 128, n * 512:(n + 1) * 512], 'qT', self.R_qT)
            slot, R_w = self.wnext(('odkv', l, 0))
            wt = self.wview(slot, KC, 256)
            for n in range(NT):
                pt, R_pt = proj_fm(wt, R_w, 0, n)
                rope_store(pt, R_pt, n, self.kT[0:128, n * 512:(n + 1) * 512], 'kT', self.R_kT)
            for tt_ in range(16):
                pt, R_pt = self.bank('A')
                for kc in range(KC):
                    self.mm(pt[:, 0:128], self.actbuf[:, kc, tt_ * 128:(tt_ + 1) * 128], wt[:, kc, 128:256], kc == 0, kc == KC - 1, [R_w, self.R_act], [R_pt], kc > 0)
                o = cnt['o'] % 4; cnt['o'] += 1
                self.cp(ob[o][:, 0:128], pt[:, 0:128], [R_pt], [R_ob[o]], eng=('act' if tt_ % 2 else 'dve'))
                store(self.vv[tt_ * 128:(tt_ + 1) * 128, 0:128], ob[o][:, 0:128], R_ob[o], 'vv', self.R_vv)
            for t in range(2):
                slot, R_w = self.wnext(('odu', l, t))
                wt = self.wview(slot, KC, 512)
                for c in range(4):
                    ch = t * 4 + c
                    for n in range(NT):
                        pt, R_pt = proj_fm(wt, R_w, c, n)
                        o = cnt['o'] % 4; cnt['o'] += 1
                        self.cp(ob[o][:], pt[:], [R_pt], [R_ob[o]], eng=('act' if n % 2 else 'dve'))
                        store(self.ybT[ch * 128:(ch + 1) * 128, n * 512:(n + 1) * 512], ob[o][:], R_ob[o], 'ybT', self.R_ybT)
        S.barrier()

    def phase_attn_C(self, l):
        S = self.S
        o_ = l // 2
        scale = 64.0 ** -0.5
        with contextlib.ExitStack() as st:
            odv = self.sb(st, 'codv', [128, 3, 8], F32); R_odv = Res('codv')
            esk = self.sb(st, 'cesk', [128, 8], F32); R_esk = Res('cesk')
            self.ld(odv[:], self.odv_d[:, o_], [], [R_odv])
            self.act(esk[:], odv[:, 2, :], AF.Exp, [R_odv], [R_esk])
            kh = self.sb(st, 'ckh', [128, T], BF16); R_kh = Res('ckh')
            v1 = self.sb(st, 'cv1', [128, 16, 128], BF16); R_v1 = Res('cv1')
            self.ld(kh[:], self.kT[0:128, :], [self.R_kT], [R_kh])
            self.ld(v1[:], self.vv[:, 0:128].rearrange("(blk j) c -> j blk c", j=128), [self.R_vv], [R_v1])
            qh = [self.sb(st, 'cqh%d' % i, [128, T], BF16) for i in range(2)]; R_qh = [Res('cqh%d' % i) for i in range(2)]
            P = [self.sb(st, 'cP%d' % i, [128, 512], BF16) for i in range(4)]; R_P = [Res('cP%d' % i) for i in range(4)]
            rden = [self.sb(st, 'crden%d' % i, [128, 512], F32) for i in range(2)]; R_rden = [Res('crden%d' % i) for i in range(2)]
            ao = [self.sb(st, 'cao%d' % i, [128, 512], BF16) for i in range(2)]; R_ao = [Res('cao%d' % i) for i in range(2)]
            cur4 = self.masks[:, 0, :]; prev4 = self.masks[:, 2, :]
            state = {'pc': 0, 'ec': 0, 'first_store': True}
            rounds = []
            for jt in range(8):
                for qt in range(NT):
                    ctx = {}
                    specs = [(hb, which) for hb in range(2) for which in ('prev', 'cur')]
                    for si_, (hb, which) in enumerate(specs):
                        rd = {}

                        def S_part(jt=jt, qt=qt, hb=hb, which=which, ctx=ctx, rd=rd, si_=si_):
                            b = jt % 2
                            if qt == 0 and si_ == 0:
                                self.ld(qh[b][:], self.qT[jt * 128:(jt + 1) * 128, :], [self.R_qT], [R_qh[b]])
                            if si_ == 0:
                                ctx['O'] = self.bank('A'); ctx['DEN'] = self.bank('A')
                            q_ = qh[b]; Rq = R_qh[b]
                            ps = slice(hb * 64, (hb + 1) * 64)
                            Sb, R_S = self.bank('B')
                            p_i = state['pc'] % 4; state['pc'] += 1
                            Pt, R_Pt = P[p_i], R_P[p_i]
                            rd['Pt'] = (Pt, R_Pt)
                            c0 = 128 if (which == 'prev' and qt == 0) else 0
                            rd['c0'] = c0
                            fs = True
                            for s_ in range(4):
                                B = qt * 4 + s_
                                kb = B - 1 if which == 'prev' else B
                                if kb < 0:
                                    continue
                                self.mm(Sb[:, s_ * 128:(s_ + 1) * 128], kh[ps, kb * 128:(kb + 1) * 128], q_[ps, B * 128:(B + 1) * 128], fs, False, [R_kh, Rq], [R_S], not fs)
                                fs = False
                            msk = prev4 if which == 'prev' else cur4
                            self.mm(Sb[:, c0:512], self.identb[:], msk[:, c0:512], False, True, [self.R_identb, self.R_masks], [R_S], True)
                            self.act(Pt[:, c0:512], Sb[:, c0:512], AF.Exp, [R_S], [R_Pt], scale=scale)

                        def PV_part(jt=jt, qt=qt, hb=hb, which=which, ctx=ctx, rd=rd, si_=si_):
                            O, R_O = ctx['O']; DEN, R_DEN = ctx['DEN']
                            Pt, R_Pt = rd['Pt']; c0 = rd['c0']
                            ps = slice(hb * 64, (hb + 1) * 64)
                            for s_ in range(4):
                                B = qt * 4 + s_
                                kb = B - 1 if which == 'prev' else B
                                if kb < 0:
                                    continue
                                stO = ctx.get(('stO', hb), True)
                                self.mm(O[ps, s_ * 128:(s_ + 1) * 128], v1[:, kb, ps], Pt[:, s_ * 128:(s_ + 1) * 128], stO, False, [R_v1, R_Pt], [R_O], not (stO and hb == 0))
                                ctx[('stO', hb)] = False
                            stD = ctx.get(('stD', hb), True)
                            self.mm(DEN[ps, c0:512], self.onesb[:, 0:64], Pt[:, c0:512], stD, False, [self.R_onesb, R_Pt], [R_DEN], not (stD and hb == 0))
                            ctx[('stD', hb)] = False

                        post = None
                        if si_ == len(specs) - 1:
                            def post(jt=jt, qt=qt, ctx=ctx):
                                O, R_O = ctx['O']; DEN, R_DEN = ctx['DEN']
                                e = state['ec'] % 2; state['ec'] += 1
                                self.ts(rden[e][:], DEN[:], esk[:, jt:jt + 1], None, ALU.add, None, [R_DEN, R_esk], [R_rden[e]])
                                S.op('dve', lambda en, e=e: en.reciprocal(out=rden[e][:], in_=rden[e][:]), reads=[R_rden[e]], writes=[R_rden[e]])
                                self.tt(ao[e][:], O[:], rden[e][:], ALU.mult, [R_O, R_rden[e]], [R_ao[e]])
                                self.ld(self.mixT[jt * 128:(jt + 1) * 128, qt * 512:(qt + 1) * 512], ao[e][:], [R_ao[e]], [self.R_mixT], join=not state['first_store'])
                                state['first_store'] = False
                        rounds.append((S_part, PV_part, post))
            prev = None
            for rnd in rounds:
                rnd[0]()
                if prev is not None:
                    prev[1]()
                    if prev[2]:
                        prev[2]()
                prev = rnd
            prev[1]()
            if prev[2]:
                prev[2]()
        S.barrier()

    def LB(self, st_, r):
        return self.actbuf[:, 8 + st_ // 8, ((st_ % 8) * 2 + r) * 128:((st_ % 8) * 2 + r + 1) * 128]

    def LC(self, st_, r):
        return self.actbuf[:, 12 + st_ // 8, ((st_ % 8) * 2 + r) * 128:((st_ % 8) * 2 + r + 1) * 128]

    def phase_s5(self, l):
        S = self.S
        o_ = l // 2
        TWO_PI = float(2 * np.pi)
        R_LB = Res('LB'); R_LC = Res('LC'); R_z = Res('z')
        with contextlib.ExitStack() as st:
            odv = self.sb(st, 's5odv', [128, 3, 8], F32); R_odv = Res('s5odv')
            sm = self.sb(st, 's5sm', [128, 12, 32], F32); R_sm = Res('s5sm')
            dsh = self.sb(st, 's5dsh', [128, 11, 32], F32); R_dsh = Res('s5dsh')
            st2 = contextlib.ExitStack()
            sv = self.sb(st2, 's5v', [128, 3, 32], F32); R_sv = Res('s5v')
            Bt = self.sb(st2, 's5b', [128, 2, 32, 16], F32); R_Bt = Res('s5b')
            Ct = self.sb(st2, 's5c', [128, 2, 32, 16], F32); R_Ct = Res('s5c')
            self.ld(sv[:], self.s5v_d[:, o_], [], [R_sv])
            self.ld(Bt[:], self.s5b_d[:, o_], [], [R_Bt])
            self.ld(Ct[:], self.s5c_d[:, o_], [], [R_Ct])
            self.ld(odv[:], self.odv_d[:, o_], [], [R_odv])
            DT, LR, TH, RHO, SN, CS, X_, Y_, KRE, KIM, NKIM, TMP = [sm[:, i, :] for i in range(12)]
            rs = [R_sm]
            self.act(DT, sv[:, 2, :], AF.Exp, [R_sv], rs)
            self.tt(LR, sv[:, 0, :], DT, ALU.mult, [R_sv, R_sm], rs)
            self.tt(TH, sv[:, 1, :], DT, ALU.mult, [R_sv, R_sm], rs)
            self.act(RHO, LR, AF.Exp, rs, rs)
            for _ in range(4):
                self.ts(TMP, TH, float(np.pi), -TWO_PI, ALU.is_gt, ALU.mult, rs, rs)
                self.tt(TH, TH, TMP, ALU.add, rs, rs)
            self.act(SN, TH, AF.Sin, rs, rs)
            self.act(CS, TH, AF.Sin, rs, rs, scale=0.5)
            self.act(CS, CS, AF.Square, rs, rs)
            self.ts(CS, CS, -2.0, 1.0, ALU.mult, ALU.add, rs, rs)
            self.tt(X_, RHO, CS, ALU.mult, rs, rs)
            self.ts(X_, X_, -1.0, None, ALU.add, None, rs, rs)
            self.tt(Y_, RHO, SN, ALU.mult, rs, rs)
            self.tt(TMP, sv[:, 0, :], sv[:, 0, :], ALU.mult, [R_sv], rs)
            self.tt(KRE, sv[:, 1, :], sv[:, 1, :], ALU.mult, [R_sv], rs)
            self.tt(TMP, TMP, KRE, ALU.add, rs, rs)
            S.op('dve', lambda en: en.reciprocal(out=TMP, in_=TMP), reads=rs, writes=rs)
            self.tt(KRE, X_, sv[:, 0, :], ALU.mult, rs + [R_sv], rs)
            self.tt(KIM, Y_, sv[:, 1, :], ALU.mult, rs + [R_sv], rs)
            self.tt(KRE, KRE, KIM, ALU.add, rs, rs)
            self.tt(KRE, KRE, TMP, ALU.mult, rs, rs)
            self.tt(KIM, Y_, sv[:, 0, :], ALU.mult, rs + [R_sv], rs)
            self.tt(NKIM, X_, sv[:, 1, :], ALU.mult, rs + [R_sv], rs)
            self.tt(KIM, KIM, NKIM, ALU.subtract, rs, rs)
            self.tt(KIM, KIM, TMP, ALU.mult, rs, rs)
            self.ts(NKIM, KIM, -1.0, None, ALU.mult, None, rs, rs)
            self.ts(TMP, TH, 0.0, TWO_PI, ALU.is_lt, ALU.mult, rs, rs)
            self.tt(dsh[:, 0, :], TH, TMP, ALU.add, rs, [R_dsh])
            for k in range(10):
                self.ts(dsh[:, k + 1, :], dsh[:, k, :], 2.0, None, ALU.mult, None, [R_dsh], [R_dsh])
                self.ts(TMP, dsh[:, k + 1, :], TWO_PI, -TWO_PI, ALU.is_ge, ALU.mult, [R_dsh], rs)
                self.tt(dsh[:, k + 1, :], dsh[:, k + 1, :], TMP, ALU.add, [R_dsh, R_sm], [R_dsh])
            Mt = [self.sb(st2, 's5M%d' % i, [128, 128], F32) for i in range(2)]; R_Mt = [Res('s5M%d' % i) for i in range(2)]
            self.memset(self.actbuf[:, 8:16, :], 0.0, [self.R_act])
            mc = 0
            for s_ in range(32):
                p0 = (s_ % 4) * 32
                for r in range(2):
                    m = mc % 2; mc += 1
                    self.memset(Mt[m][:], 0.0, [R_Mt[m]])
                    for hf in range(2):
                        rows = slice(hf * 64, (hf + 1) * 64)
                        cols = slice(p0 + hf * 16, p0 + hf * 16 + 16)
                        if r == 0:
                            self.ts(Mt[m][rows, cols], Bt[rows, 0, s_, :], KRE[rows, s_:s_ + 1], None, ALU.mult, None, [R_Bt, R_sm], [R_Mt[m]], join=True)
                            self.stt(Mt[m][rows, cols], Bt[rows, 1, s_, :], NKIM[rows, s_:s_ + 1], Mt[m][rows, cols], ALU.mult, ALU.add, [R_Bt, R_sm, R_Mt[m]], [R_Mt[m]])
                        else:
                            self.ts(Mt[m][rows, cols], Bt[rows, 1, s_, :], KRE[rows, s_:s_ + 1], None, ALU.mult, None, [R_Bt, R_sm], [R_Mt[m]], join=True)
                            self.stt(Mt[m][rows, cols], Bt[rows, 0, s_, :], KIM[rows, s_:s_ + 1], Mt[m][rows, cols], ALU.mult, ALU.add, [R_Bt, R_sm, R_Mt[m]], [R_Mt[m]])
                    pt, R_pt = self.bank('B')
                    self.tr(pt[:, 0:128], Mt[m][:], [R_Mt[m]], [R_pt], False)
                    self.cp(self.LB(s_, r), pt[:, 0:128], [R_pt], [R_LB, self.R_act], eng='act', join=True)
                    for hf in range(2):
                        rows = slice(hf * 64, (hf + 1) * 64)
                        cols = slice(p0 + hf * 16, p0 + hf * 16 + 16)
                        if r == 0:
                            self.cp(self.LC(s_, 0)[rows, cols], Ct[rows, 0, s_, :], [R_Ct], [R_LC, self.R_act], join=True)
                        else:
                            self.ts(self.LC(s_, 1)[rows, cols], Ct[rows, 1, s_, :], -1.0, None, ALU.mult, None, [R_Ct], [R_LC, self.R_act], join=True)
            S.barrier()
            st2.close()
            st3 = contextlib.ExitStack()
            uc = self.sb(st3, 's5u', [128, T], BF16); R_uc = Res('s5u')
            bufA = self.sb(st3, 's5A', [128, T], F32); R_A = Res('s5A')
            bufB = self.sb(st3, 's5B', [128, T], F32); R_B = Res('s5B')
            mtmp = self.sb(st3, 's5mt', [128, 1024], F32); R_mt = Res('s5mt')
            cosT = self.sb(st3, 's5cos', [128, T], BF16); R_cos = Res('s5cos')
            sinT = self.sb(st3, 's5sin', [128, T], BF16); R_sin = Res('s5sin')
            prb = self.sb(st3, 's5prb', [128, T], BF16); R_prb = Res('s5prb')
            pib = self.sb(st3, 's5pib', [128, T], BF16); R_pib = Res('s5pib')
            bre = self.sb(st3, 's5bre', [128, T], BF16); R_bre = Res('s5bre')
            bim = self.sb(st3, 's5bim', [128, T], BF16); R_bim = Res('s5bim')
            sre = self.sb(st3, 's5sre', [128, T], BF16); R_sre = Res('s5sre')
            sim = self.sb(st3, 's5sim', [128, T], BF16); R_sim = Res('s5sim')
            zt = self.sb(st3, 's5zt', [128, 512], F32); R_zt = Res('s5zt')

            R_Ahi = Res('s5Ahi')

            def conv(lo, hi, R_Ax):
                self.act(sinT[:, lo:hi], bufA[:, lo:hi], AF.Sin, [R_Ax], [R_sin], scale=-1.0, bias=float(np.pi) - 1e-6, join=(lo > 0))
                self.act(bufB[:, lo:hi], bufA[:, lo:hi], AF.Sin, [R_Ax], [R_B], scale=0.5, join=(lo > 0))
                self.act(bufB[:, lo:hi], bufB[:, lo:hi], AF.Square, [R_B], [R_B], join=(lo > 0))
                self.act(cosT[:, lo:hi], bufB[:, lo:hi], AF.Identity, [R_B], [R_cos], scale=-2.0, bias=1.0, join=(lo > 0))

            for c in range(8):
                self.ld(uc[:], self.ybT[c * 128:(c + 1) * 128, :], [self.R_ybT], [R_uc])
                ybanks = [self.pbanks[n] for n in range(NT)]
                for si in range(4):
                    s_ = c * 4 + si
                    for n in range(NT):
                        ns = slice(n * 512, (n + 1) * 512)
                        pre, R_pre = self.bank('B')
                        pim, R_pim = self.bank('B')
                        self.mm(pre[:], self.LB(s_, 0), uc[:, ns], True, True, [R_LB, self.R_act, R_uc], [R_pre], False)
                        self.mm(pim[:], self.LB(s_, 1), uc[:, ns], True, True, [R_LB, self.R_act, R_uc], [R_pim], False)
                        self.cp(prb[:, ns], pre[:], [R_pre], [R_prb], eng='act', join=(n > 0))
                        self.cp(pib[:, ns], pim[:], [R_pim], [R_pib], eng='act', join=(n > 0))
                    self.memset(bufA[:, 0:1], 0.0, [R_A])
                    for k in range(11):
                        Lk = 1 << k
                        Rw = R_Ahi if k == 10 else R_A
                        self.ts(bufA[:, Lk:2 * Lk], bufA[:, 0:Lk], dsh[:, k, s_:s_ + 1], None, ALU.add, None, [R_A, R_dsh], [Rw])
                        self.ts(mtmp[:, 0:Lk], bufA[:, Lk:2 * Lk], TWO_PI, -TWO_PI, ALU.is_ge, ALU.mult, [Rw], [R_mt])
                        self.tt(bufA[:, Lk:2 * Lk], bufA[:, Lk:2 * Lk], mtmp[:, 0:Lk], ALU.add, [Rw, R_mt], [Rw])
                        if k == 9:
                            conv(0, 1024, R_A)
                    conv(1024, 2048, R_Ahi)
                    self.tt(bre[:], prb[:], cosT[:], ALU.mult, [R_prb, R_cos], [R_bre])
                    self.tt(sre[:], pib[:], sinT[:], ALU.mult, [R_pib, R_sin], [R_sre])
                    self.tt(bre[:], bre[:], sre[:], ALU.add, [R_bre, R_sre], [R_bre])
                    self.tt(bim[:], pib[:], cosT[:], ALU.mult, [R_pib, R_cos], [R_bim])
                    self.tt(sim[:], prb[:], sinT[:], ALU.mult, [R_prb, R_sin], [R_sim])
                    self.tt(bim[:], bim[:], sim[:], ALU.subtract, [R_bim, R_sim], [R_bim])
                    rho_b = RHO[:, s_:s_ + 1].broadcast_to([128, T])
                    S.op('dve', lambda en, rho_b=rho_b: en.tensor_tensor_scan(out=bre[:], data0=rho_b, data1=bre[:], initial=0.0, op0=ALU.mult, op1=ALU.add),
                         reads=[R_sm, R_bre], writes=[R_bre])
                    S.op('dve', lambda en, rho_b=rho_b: en.tensor_tensor_scan(out=bim[:], data0=rho_b, data1=bim[:], initial=0.0, op0=ALU.mult, op1=ALU.add),
                         reads=[R_sm, R_bim], writes=[R_bim])
                    self.tt(prb[:], bre[:], cosT[:], ALU.mult, [R_bre, R_cos], [R_prb])
                    self.tt(pib[:], bim[:], sinT[:], ALU.mult, [R_bim, R_sin], [R_pib])
                    self.tt(sre[:], prb[:], pib[:], ALU.subtract, [R_prb, R_pib], [R_sre])
                    self.tt(prb[:], bim[:], cosT[:], ALU.mult, [R_bim, R_cos], [R_prb])
                    self.tt(pib[:], bre[:], sinT[:], ALU.mult, [R_bre, R_sin], [R_pib])
                    self.tt(sim[:], prb[:], pib[:], ALU.add, [R_prb, R_pib], [R_sim])
                    for n in range(NT):
                        ns = slice(n * 512, (n + 1) * 512)
                        yb_, R_yb = ybanks[n]
                        self.mm(yb_[:], self.LC(s_, 0), sre[:, ns], si == 0, False, [R_LC, self.R_act, R_sre], [R_yb], si > 0)
                        self.mm(yb_[:], self.LC(s_, 1), sim[:, ns], False, si == 3, [R_LC, self.R_act, R_sim], [R_yb], True)
                for n in range(NT):
                    ns = slice(n * 512, (n + 1) * 512)
                    yb_, R_yb = ybanks[n]
                    self.stt(zt[:], uc[:, ns], odv[:, 0, c:c + 1], yb_[:], ALU.mult, ALU.add, [R_uc, R_odv, R_yb], [R_zt])
                    self.act(self.actbuf[:, c, ns], zt[:], AF.Gelu_apprx_tanh, [R_zt], [R_z, self.R_act], join=True)
            S.barrier()
            st3.close()
            sg = [self.sb(st, 's5sg%d' % i, [128, 512], F32) for i in range(2)]; R_sg = [Res('s5sg%d' % i) for i in range(2)]
            go = [self.sb(st, 's5go%d' % i, [128, 512], BF16) for i in range(2)]; R_go = [Res('s5go%d' % i) for i in range(2)]
            gc = 0
            for t in range(2):
                slot, R_w = self.wnext(('glu', l, t))
                wt = self.wview(slot, 8, 512)
                for c in range(4):
                    oc = t * 4 + c
                    for n in range(NT):
                        ns = slice(n * 512, (n + 1) * 512)
                        pt, R_pt = self.bank('B')
                        for kc in range(8):
                            self.mm(pt[:], wt[:, kc, c * 128:(c + 1) * 128], self.actbuf[:, kc, ns], kc == 0, kc == 7, [R_w, R_z, self.R_act], [R_pt], kc > 0)
                        g_ = gc % 2; gc += 1
                        self.act(sg[g_][:], pt[:], AF.Sigmoid, [R_pt, R_odv], [R_sg[g_]], bias=odv[:, 1, oc:oc + 1])
                        self.tt(go[g_][:], sg[g_][:], self.actbuf[:, oc, ns], ALU.mult, [R_sg[g_], R_z, self.R_act], [R_go[g_]])
                        self.ld(self.mixT[(8 + oc) * 128:(9 + oc) * 128, ns], go[g_][:], [R_go[g_]], [self.R_mixT], join=True)
        S.barrier()

    def resid_load(self, dch, n, xt, R_xt, i):
        ns = slice(n * 512, (n + 1) * 512)
        rows = slice(dch * 128, (dch + 1) * 128)
        self.ld(xt[i][:], self.xT[rows, ns], [self.R_xTn[n]], [R_xt[i]])

    def resid_epilogue(self, pt, R_pt, gcol, dch, n, xt, R_xt, i, first, preloaded=False):
        ns = slice(n * 512, (n + 1) * 512)
        rows = slice(dch * 128, (dch + 1) * 128)
        if not preloaded:
            self.ld(xt[i][:], self.xT[rows, ns], [self.R_xTn[n]], [R_xt[i]])
        self.stt(xt[i][:], pt[:], self.modT[:, gcol + dch:gcol + dch + 1], xt[i][:], ALU.mult, ALU.add, [R_pt, self.R_modT, R_xt[i]], [R_xt[i]])
        self.ld(self.xT[rows, ns], xt[i][:], [R_xt[i]], [self.R_xTn[n]], join=True)

    def phase_outproj(self, l):
        S = self.S
        with contextlib.ExitStack() as st:
            xt = [self.sb(st, 'opx%d' % i, [128, 512], F32) for i in range(4)]; R_xt = [Res('opx%d' % i) for i in range(4)]
            R_ag = [Res('opag%d' % i) for i in range(4)]
            for gi in range(4):
                self.ld(self.actbuf[:, gi * 4:(gi + 1) * 4, :], self.mixT[gi * 512:(gi + 1) * 512, :].rearrange("(k p) t -> p k t", p=128), [self.R_mixT], [R_ag[gi]])
            cnt = 0
            tiles = [(t * 4 + c, n) for t in range(4) for c in range(4) for n in range(NT)]
            self.resid_load(tiles[0][0], tiles[0][1], xt, R_xt, 0)
            for t in range(4):
                slot, R_w = self.wnext(('wout', l, t))
                wt = self.wview(slot, KC, 512)
                for c in range(4):
                    dch = t * 4 + c
                    for n in range(NT):
                        pt, R_pt = self.bank()
                        for kc in range(KC):
                            self.mm(pt[:], wt[:, kc, c * 128:(c + 1) * 128], self.actbuf[:, kc, n * 512:(n + 1) * 512], kc == 0, kc == KC - 1, [R_w, R_ag[kc // 4]], [R_pt], kc > 0)
                        if cnt + 1 < len(tiles):
                            self.resid_load(tiles[cnt + 1][0], tiles[cnt + 1][1], xt, R_xt, (cnt + 1) % 4)
                        self.resid_epilogue(pt, R_pt, 32, dch, n, xt, R_xt, cnt % 4, cnt == 0, preloaded=True)
                        cnt += 1
        S.barrier()

    def phase_ffn_in(self, l):
        S = self.S
        with contextlib.ExitStack() as st:
            cw = self.sb(st, 'fcw', [128, 86, 3], F32); R_cw = Res('fcw')
            cb = self.sb(st, 'fcb', [128, 86], F32); R_cb = Res('fcb')
            self.ld(cw[:], self.ffn_convw_d[:, l], [], [R_cw])
            self.ld(cb[:], self.ffn_convb_d[:, l], [], [R_cb])
            U = [[self.sb(st, 'fU%d_%d' % (s_, br), [128, T + 2], F32) for br in range(2)] for s_ in range(2)]
            R_U = [[Res('fU%d_%d' % (s_, br)) for br in range(2)] for s_ in range(2)]
            acc = [self.sb(st, 'facc%d' % br, [128, T], F32) for br in range(2)]; R_acc = [Res('facc%d' % br) for br in range(2)]
            gg = self.sb(st, 'fgg', [128, T], BF16); R_gg = Res('fgg')
            mid = [self.sb(st, 'fmid%d' % i, [128, T], BF16) for i in range(2)]; R_mid = [Res('fmid%d' % i) for i in range(2)]
            for s_ in range(2):
                for br in range(2):
                    self.memset(U[s_][br][:, 0:2], 0.0, [R_U[s_][br]])
            jc = 0
            for t in range(22):
                slot, R_w = self.wnext(('ffin', l, t))
                j0 = 2 * t
                nj = min(2, NJ - j0)
                w = nj * 128
                wt = self.wview(slot, KC, 2 * w)
                for jj in range(nj):
                    j = j0 + jj
                    s_ = jc % 2; jc += 1
                    for br in range(2):
                        cidx = br * NJ + j
                        for n in range(NT):
                            pt, R_pt = self.bank('7')
                            for kc in range(KC):
                                self.mm(pt[:], wt[:, kc, br * w + jj * 128: br * w + (jj + 1) * 128], self.actbuf[:, kc, n * 512:(n + 1) * 512],
                                        kc == 0, kc == KC - 1, [R_w, self.R_act], [R_pt], kc > 0)
                            self.cp(U[s_][br][:, 2 + n * 512: 2 + (n + 1) * 512], pt[:], [R_pt], [R_U[s_][br]], eng='act', join=True)
                        Ub, R_Ub = U[s_][br], R_U[s_][br]
                        self.ts(acc[br][:], Ub[:, 2:T + 2], cw[:, cidx, 2:3], cb[:, cidx:cidx + 1], ALU.mult, ALU.add, [R_Ub, R_cw, R_cb], [R_acc[br]])
                        for i in (1, 0):
                            self.stt(acc[br][:], Ub[:, i:T + i], cw[:, cidx, i:i + 1], acc[br][:], ALU.mult, ALU.add, [R_Ub, R_cw, R_acc[br]], [R_acc[br]])
                    self.act(gg[:], acc[0][:], AF.Gelu_apprx_tanh, [R_acc[0]], [R_gg])
                    m = j % 2
                    self.tt(mid[m][:], gg[:], acc[1][:], ALU.mult, [R_gg, R_acc[1]], [R_mid[m]])
                    self.ld(self.midT[j * 128:(j + 1) * 128, :], mid[m][:], [R_mid[m]], [self.R_midT], join=(j > 0))
                if l + 1 < self.nlayers:
                    self.ada_tile(l + 1, t)
            if l + 1 < self.nlayers:
                for t in (22, 23):
                    self.ada_tile(l + 1, t)
                self.ada_finish(l + 1)
        S.barrier()

    def phase_ffn_out(self, l):
        S = self.S
        NA = 31
        with contextlib.ExitStack() as st:
            midB = self.sb(st, 'fmidB', [128, NJ - NA, 1024], BF16); R_midB = Res('fmidB')
            midA = self.actbuf[:].rearrange("p k t -> p (k t)").rearrange("p (j t) -> p j t", t=1024)
            xt = [self.sb(st, 'fox%d' % i, [128, 512], F32) for i in range(4)]; R_xt = [Res('fox%d' % i) for i in range(4)]
            cnt = 0; bs = 0
            grp_bounds = [(0, 8), (8, 16), (16, 24), (24, NA), (NA, 37), (37, NJ)]
            R_mg = [Res('fmg%d' % i) for i in range(6)]
            grp_of = {}
            for gi, (a_, b_) in enumerate(grp_bounds):
                for j in range(a_, b_):
                    grp_of[j] = gi
            for th in range(2):
                tsl = slice(th * 1024, (th + 1) * 1024)
                for gi, (a_, b_) in enumerate(grp_bounds):
                    dst = midA[:, a_:b_, :] if b_ <= NA else midB[:, a_ - NA:b_ - NA, :]
                    self.ld(dst, self.midT[a_ * 128:b_ * 128, tsl].rearrange("(j p) t -> p j t", p=128), [self.R_midT], [R_mg[gi]])
                for dp in range(8):
                    slot, R_w = self.wnext(('ffout', l, th, dp))
                    wt = self.wview(slot, NJ, 256)
                    base = (bs % 2) * 4; bs += 1
                    banks = [[self.pbanks[base + dd * 2 + n2] for n2 in range(2)] for dd in range(2)]
                    for q_ in range(4):
                        self.resid_load(dp * 2 + q_ // 2, th * 2 + q_ % 2, xt, R_xt, (cnt + q_) % 4)
                    for j in range(NJ):
                        R_src = R_mg[grp_of[j]]
                        src = midA[:, j, :] if j < NA else midB[:, j - NA, :]
                        for dd in range(2):
                            for n2 in range(2):
                                pt, R_pt = banks[dd][n2]
                                self.mm(pt[:], wt[:, j, dd * 128:(dd + 1) * 128], src[:, n2 * 512:(n2 + 1) * 512], j == 0, j == NJ - 1, [R_w, R_src], [R_pt], j > 0)
                    for dd in range(2):
                        for n2 in range(2):
                            pt, R_pt = banks[dd][n2]
                            self.resid_epilogue(pt, R_pt, 80, dp * 2 + dd, th * 2 + n2, xt, R_xt, cnt % 4, cnt == 0, preloaded=True)
                            cnt += 1
        S.barrier()


_CACHE = {}
BF = ml_dtypes.bfloat16


def _const_inputs():
    c = {}
    c['identf'] = np.eye(128, dtype=np.float32)
    c['identb'] = np.eye(128, dtype=np.float32).astype(BF)
    j = np.arange(128)[:, None]; i = np.arange(128)[None, :]
    cur = np.where(j <= i, 0.0, NEG).astype(np.float32)
    prevA = np.where(j >= i, 0.0, NEG).astype(np.float32)
    prevC = np.where(j > i, 0.0, NEG).astype(np.float32)
    masks = np.stack([np.tile(cur, (1, 4)), np.tile(prevA, (1, 4)), np.tile(prevC, (1, 4))], axis=1)
    c['masks'] = masks.astype(BF)
    m3 = np.stack([np.tile(cur[:, qt * 32:(qt + 1) * 32], (1, 16)) for qt in range(4)], axis=1)
    c['m3'] = m3.astype(BF)
    pa = np.zeros((128, 128), np.float32)
    for m in range(128):
        pa[(m + 64) % 128, m] = 1.0
    pc = np.zeros((128, 128), np.float32)
    for m in range(128):
        blk = m // 64; d = m % 64
        pc[blk * 64 + (d + 32) % 64, m] = 1.0
    c['pswap'] = np.stack([pa, pc], axis=1).astype(BF)
    rc = np.zeros((128, 4), np.float32)
    for p in range(128):
        rc[p, 0] = np.float32(10000.0) ** (-np.float32(p % 64) / np.float32(64))
        rc[p, 1] = -1.0 if p < 64 else 1.0
        d = p % 64
        rc[p, 2] = np.float32(10000.0) ** (-np.float32(d % 32) / np.float32(32))
        rc[p, 3] = -1.0 if d < 32 else 1.0
    c['ropec'] = rc
    return c


def _fm(v, nchunk):
    v = np.asarray(v)
    lead = v.shape[:-1]
    v = v.reshape(lead + (nchunk, 128))
    return np.ascontiguousarray(np.moveaxis(v, -1, 0))


def _shared_inputs(inp):
    m = _const_inputs()
    m['norm_final_T'] = _fm(inp['norm_final'], KC)
    m['norm_mix_T'] = _fm(inp['norm_mix'], KC)
    m['norm_ffn_T'] = _fm(inp['norm_ffn'], KC)
    m['ada_w'] = np.asarray(inp['ada_w'])
    m['ada_bT'] = _fm(inp['ada_b'], 96)
    m['ev_w_in'] = np.asarray(inp['ev_w_in'])
    m['ev_w_out'] = np.asarray(inp['ev_w_out'])
    m['ev_convw_T'] = np.ascontiguousarray(np.transpose(_fm(inp['ev_conv_w'], 8), (0, 1, 3, 2)))
    vec = np.stack([_fm(inp['ev_conv_b'], 8), _fm(inp['ev_gate_a_b'], 8), _fm(inp['ev_gate_x_b'], 8), _fm(inp['ev_lambda'], 8)], axis=2)
    m['ev_vec_T'] = np.ascontiguousarray(vec)
    ga = np.asarray(inp['ev_gate_a_w']); gx = np.asarray(inp['ev_gate_x_w'])
    gw = np.stack([ga, gx], axis=1)
    m['ev_gw'] = np.ascontiguousarray(np.transpose(gw, (3, 0, 1, 2, 4)))
    m['od_w_in'] = np.asarray(inp['od_w_in'])
    m['od_w_out'] = np.asarray(inp['od_w_out'])
    m['od_glu_w'] = np.asarray(inp['od_glu_w'])
    def st_layout(v):
        v = np.asarray(v)
        v = v.reshape((v.shape[0], 32, 128) + v.shape[2:])
        return np.ascontiguousarray(np.moveaxis(v, 2, 0))
    a_re = np.asarray(inp['od_a_re']).reshape(2, 4096); a_im = np.asarray(inp['od_a_im']).reshape(2, 4096)
    ldt = np.repeat(np.asarray(inp['od_log_dt']), 64, axis=1)
    m['s5v'] = np.ascontiguousarray(np.stack([st_layout(a_re), st_layout(a_im), st_layout(ldt)], axis=2))
    b_re = np.asarray(inp['od_b_re']).reshape(2, 4096, 16); b_im = np.asarray(inp['od_b_im']).reshape(2, 4096, 16)
    m['s5b'] = np.ascontiguousarray(np.stack([st_layout(b_re), st_layout(b_im)], axis=2))
    c_re = np.transpose(np.asarray(inp['od_c_re']), (0, 1, 3, 2)).reshape(2, 4096, 16)
    c_im = np.transpose(np.asarray(inp['od_c_im']), (0, 1, 3, 2)).reshape(2, 4096, 16)
    m['s5c'] = np.ascontiguousarray(np.stack([st_layout(c_re), st_layout(c_im)], axis=2))
    sk = np.asarray(inp['od_sinks'])
    skT = np.concatenate([np.broadcast_to(sk[:, None, 0:8], (2, 64, 8)), np.broadcast_to(sk[:, None, 8:16], (2, 64, 8))], axis=1)
    m['odv'] = np.ascontiguousarray(np.stack([_fm(inp['od_d'], 8), _fm(inp['od_glu_b'], 8), np.transpose(skT, (1, 0, 2))], axis=2))
    m['ffn_w_in'] = np.asarray(inp['ffn_w_in'])
    m['ffn_w_out'] = np.asarray(inp['ffn_w_out'])
    m['ffn_convw_T'] = np.ascontiguousarray(np.transpose(_fm(inp['ffn_conv_w'], 86), (0, 1, 3, 2)))
    m['ffn_convb_T'] = _fm(inp['ffn_conv_b'], 86)
    return m


def kernel(**inputs):
    x = np.asarray(inputs['x'])
    B = x.shape[0]
    if 'nc' not in _CACHE:
        bld = Builder()
        _CACHE['nc'] = bld.build()
        _CACHE['names'] = bld.inputs
    nc = _CACHE['nc']
    shared = _shared_inputs(inputs)
    pos = np.asarray(inputs['positions']).astype(np.int32)
    c = np.asarray(inputs['c'])
    in_maps = []
    NCORE = B
    for core in range(NCORE):
        b = core % B
        m = dict(shared)
        m['x'] = np.ascontiguousarray(x[b])
        m['pos128'] = np.ascontiguousarray(np.broadcast_to(pos[b][None, :], (128, T)))
        m['cT'] = _fm(c[b], KC)
        m = {k: v for k, v in m.items() if k in _CACHE['names']}
        in_maps.append(m)
    res = run_bass_kernel_spmd(nc, in_maps, core_ids=list(range(NCORE)))
    out = np.stack([res.results[b]['out'] for b in range(B)], axis=0)
    return out.astype(np.float32)
```

```python
import contextlib
import numpy as np
import ml_dtypes
import concourse.bass as bass
import concourse.mybir as mybir
from concourse.bass_utils import run_bass_kernel_spmd

F32 = mybir.dt.float32
BF16 = mybir.dt.bfloat16
I32 = mybir.dt.int32
ALU = mybir.AluOpType
AF = mybir.ActivationFunctionType

D = 2048
T = 2048
KC = 16
NT = 4
DEPTH = 4
DFF = 5504
NJ = 43
EPS = 1e-6
NEG = -30000.0
ENGS = ['pe', 'act', 'dve', 'pool', 'sp']


class Res:
    def __init__(self, name):
        self.name = name
        self.wr = []
        self.rd = []


class Chan:
    def __init__(self, name):
        self.name = name
        self.count = 0
        self.sem = None


def _add(lst, tok):
    if tok[0] == 'c':
        for i, t in enumerate(lst):
            if t[0] == 'c' and t[1] == tok[1]:
                if t[2] < tok[2]:
                    lst[i] = tok
                return
        lst.append(tok)
    else:
        if tok not in lst:
            lst.append(tok)


class Sched:
    def __init__(self, nc):
        self.nc = nc
        self.ops = {e: [] for e in ENGS}
        self.chans = {}
        self.pending = {e: [] for e in ENGS}

    def _deps(self, eng, reads, writes, join):
        toks = []
        for r in reads:
            toks += r.wr
        for w in writes:
            if not join:
                toks += w.wr
            toks += w.rd
        toks += self.pending[eng]
        self.pending[eng] = []
        deps = []
        for t in toks:
            if t[0] == 'd':
                deps.append(('d', t[1], t[1].count) if len(t) == 2 else t)
            else:
                deps.append(t)
        return deps

    def _commit(self, tok, reads, writes, join):
        for r in reads:
            _add(r.rd, tok)
        for w in writes:
            if join:
                _add(w.wr, tok)
            else:
                w.wr = [tok]
            w.rd = []

    def op(self, eng, fn, reads=(), writes=(), join=False):
        deps = self._deps(eng, reads, writes, join)
        idx = len(self.ops[eng])
        o = dict(kind='c', fn=fn, deps=deps, sig=False)
        self.ops[eng].append(o)
        self._commit(('c', eng, idx), reads, writes, join)
        return o

    def dma(self, eng, fn, reads=(), writes=(), join=False, chan=None):
        cname = chan or ('ch_' + writes[0].name)
        if cname not in self.chans:
            self.chans[cname] = Chan(cname)
        ch = self.chans[cname]
        deps = self._deps(eng, reads, writes, join)
        ch.count += 1
        o = dict(kind='d', fn=fn, deps=deps, chan=ch)
        self.ops[eng].append(o)
        self._commit(('d', ch), reads, writes, join)
        return o

    def barrier(self, engs=('pe', 'act', 'dve', 'sp')):
        toks = []
        for e in ENGS:
            if self.ops[e]:
                for i in range(len(self.ops[e]) - 1, -1, -1):
                    if self.ops[e][i]['kind'] == 'c':
                        toks.append(('c', e, i))
                        break
        for ch in self.chans.values():
            toks.append(('d', ch, ch.count))
        for e in engs:
            self.pending[e] = list(toks)

    def emit(self):
        nc = self.nc
        for e in ENGS:
            for o in self.ops[e]:
                for d in o['deps']:
                    if d[0] == 'c':
                        self.ops[d[1]][d[2]]['sig'] = True
        for e in ENGS:
            n = 0
            for o in self.ops[e]:
                if o['kind'] == 'c' and o['sig']:
                    n += 1
                    o['sigval'] = n
        with contextlib.ExitStack() as st:
            esem = {e: st.enter_context(nc.semaphore('s_' + e)) for e in ENGS}
            for i, (cname, ch) in enumerate(self.chans.items()):
                ch.sem = st.enter_context(nc.semaphore('d%d' % i))
            block = st.enter_context(nc.Block())

            def run(e, eng):
                seen = {}
                for o in self.ops[e]:
                    need = {}
                    for d in o['deps']:
                        if d[0] == 'c':
                            key = ('c', d[1]); sem = esem[d[1]]
                            val = self.ops[d[1]][d[2]]['sigval']
                        else:
                            key = ('d', d[1].name); sem = d[1].sem
                            val = 16 * d[2]
                        if val > need.get(key, (None, 0))[1]:
                            need[key] = (sem, val)
                    for key, (sem, val) in need.items():
                        if seen.get(key, 0) >= val:
                            continue
                        seen[key] = val
                        eng.wait_ge(sem, val)
                    ins = o['fn'](eng)
                    if o['kind'] == 'd':
                        ins.then_inc(o['chan'].sem, 16)
                    elif o['sig']:
                        ins.then_inc(esem[e], 1)
                if e == 'sp':
                    for ch in self.chans.values():
                        eng.wait_ge(ch.sem, 16 * ch.count)

            @block.tensor
            def _(eng):
                run('pe', eng)

            @block.scalar
            def _(eng):
                run('act', eng)

            @block.vector
            def _(eng):
                run('dve', eng)

            @block.gpsimd
            def _(eng):
                run('pool', eng)

            @block.sync
            def _(eng):
                run('sp', eng)


WSLOT = 11264


class Builder:
    def __init__(self, nlayers=DEPTH, dbg=False):
        self.nlayers = nlayers
        self.dbg = dbg
        self.nc = bass.Bass("TRN2", target_bir_lowering=False)
        self.S = Sched(self.nc)
        self.inputs = {}
        self.gst = contextlib.ExitStack()
        self._sbn = 0
        self.wq = []
        self.wissued = 0
        self.wtaken = 0

    def din(self, name, shape, dt=F32):
        t = self.nc.dram_tensor(name, list(shape), dt, kind="ExternalInput").ap()
        self.inputs[name] = (list(shape), dt)
        return t

    def dscr(self, name, shape, dt):
        return self.nc.dram_tensor(name, list(shape), dt, kind="Internal").ap()

    def sb(self, st, name, shape, dt):
        self._sbn += 1
        return st.enter_context(self.nc.sbuf_tensor('s%d_%s' % (self._sbn, name), list(shape), dt))

    def mm(self, out, lhsT, rhs, start, stop, reads, writes, join):
        self.S.op('pe', lambda e: e.matmul(out, lhsT=lhsT, rhs=rhs, start=start, stop=stop,
                                           skip_group_check=True), reads=reads, writes=writes, join=join)

    def tr(self, out, in_, reads, writes, join):
        self.S.op('pe', lambda e: e.transpose(out, in_, self.identf[:]), reads=list(reads) + [self.R_identf], writes=writes, join=join)

    def act(self, out, in_, func, reads, writes, bias=None, scale=None, join=False):
        kw = {}
        if bias is not None:
            kw['bias'] = bias
        if scale is not None:
            kw['scale'] = scale
        self.S.op('act', lambda e: e.activation(out=out, in_=in_, func=func, **kw), reads=reads, writes=writes, join=join)

    def tt(self, out, in0, in1, op, reads, writes, join=False):
        self.S.op('dve', lambda e: e.tensor_tensor(out=out, in0=in0, in1=in1, op=op), reads=reads, writes=writes, join=join)

    def ts(self, out, in0, s1, s2, op0, op1, reads, writes, join=False):
        if op1 is None:
            self.S.op('dve', lambda e: e.tensor_scalar(out=out, in0=in0, scalar1=s1, scalar2=None, op0=op0), reads=reads, writes=writes, join=join)
        else:
            self.S.op('dve', lambda e: e.tensor_scalar(out=out, in0=in0, scalar1=s1, scalar2=s2, op0=op0, op1=op1), reads=reads, writes=writes, join=join)

    def stt(self, out, in0, scalar, in1, op0, op1, reads, writes, join=False):
        self.S.op('dve', lambda e: e.scalar_tensor_tensor(out=out, in0=in0, scalar=scalar, in1=in1, op0=op0, op1=op1), reads=reads, writes=writes, join=join)

    def cp(self, out, in_, reads, writes, eng='dve', join=False):
        if eng == 'dve':
            self.S.op('dve', lambda e: e.tensor_copy(out=out, in_=in_), reads=reads, writes=writes, join=join)
        else:
            self.S.op('act', lambda e: e.activation(out=out, in_=in_, func=AF.Copy), reads=reads, writes=writes, join=join)

    def memset(self, ap, val, writes, join=False):
        self.S.op('dve', lambda e: e.memset(ap, val), writes=writes, join=join)

    def ld(self, out, in_, reads, writes, join=False, eng='sp', chan=None):
        self.S.dma(eng, lambda e: e.dma_start(out=out, in_=in_), reads=reads, writes=writes, join=join, chan=chan)

    def bank(self, grp=None):
        if grp is None:
            i = self.pnext % 8; self.pnext += 1
        elif grp == '7':
            i = self.p7 % 7; self.p7 += 1
        elif grp == 'A':
            i = self.pa % 4; self.pa += 1
        else:
            i = 4 + self.pbn % 4; self.pbn += 1
        return self.pbanks[i]

    def wplan(self, key, parts):
        self.wq.append((key, parts))

    def _wissue(self, upto):
        while self.wissued < min(upto, len(self.wq)):
            i = self.wissued
            slot, R = self.wslots[i % 3]
            key, parts = self.wq[i]
            for pi, (dst, src) in enumerate(parts):
                self.ld(dst(slot), src, [], [R], join=(pi > 0), eng='pool')
            self.wissued += 1

    def wnext(self, key):
        i = self.wtaken
        assert self.wq[i][0] == key, (self.wq[i][0], key)
        self._wissue(i + 3)
        self.wtaken += 1
        return self.wslots[i % 3]

    @staticmethod
    def wview(slot, k, n):
        return slot[:, 0:k * n].rearrange("p (k n) -> p k n", n=n)

    def plan_k2048(self, key, W, c0, w):
        def dst(lo, hi):
            return lambda slot: self.wview(slot, KC, w)[:, lo:hi, :]
        parts = []
        for lo in (0, 8):
            parts.append((dst(lo, lo + 8), W[lo * 128:(lo + 8) * 128, c0:c0 + w].rearrange("(k p) n -> p k n", p=128)))
        self.wplan(key, parts)

    def build(self):
        nc, S = self.nc, self.S
        g = self.gst
        L = self.nlayers
        x_d = self.din('x', [T, D])
        self.out_d = nc.dram_tensor('out', [T, D], F32, kind="ExternalOutput").ap()
        identf_d = self.din('identf', [128, 128])
        identb_d = self.din('identb', [128, 128], BF16)
        masks_d = self.din('masks', [128, 3, 512], BF16)
        m3_d = self.din('m3', [128, 4, 512], BF16)
        pswap_d = self.din('pswap', [128, 2, 128], BF16)
        ropec_d = self.din('ropec', [128, 4])
        pos_d = self.din('pos128', [128, T], I32)
        cT_d = self.din('cT', [128, KC])
        normfin_d = self.din('norm_final_T', [128, KC])
        normmix_d = self.din('norm_mix_T', [128, DEPTH, KC])
        normffn_d = self.din('norm_ffn_T', [128, DEPTH, KC])
        self.ada_w = self.din('ada_w', [L, D, 6 * D])
        adab_d = self.din('ada_bT', [128, DEPTH, 96])
        NE = (L + 1) // 2
        self.ev_w_in = self.din('ev_w_in', [NE, D, 5120])
        self.ev_w_out = self.din('ev_w_out', [NE, D, D])
        self.ev_convw_d = self.din('ev_convw_T', [128, 2, 8, 4])
        self.ev_vec_d = self.din('ev_vec_T', [128, 2, 4, 8])
        self.ev_gw_d = self.din('ev_gw', [128, 2, 2, 8, 128])
        NO = max(L // 2, 1)
        self.od_w_in = self.din('od_w_in', [NO, D, 2304])
        self.od_w_out = self.din('od_w_out', [NO, D, D])
        self.od_glu_w = self.din('od_glu_w', [NO, 1024, 1024])
        self.s5v_d = self.din('s5v', [128, 2, 3, 32])
        self.s5b_d = self.din('s5b', [128, 2, 2, 32, 16])
        self.s5c_d = self.din('s5c', [128, 2, 2, 32, 16])
        self.odv_d = self.din('odv', [128, 2, 3, 8])
        self.ffn_w_in = self.din('ffn_w_in', [L, D, 2 * DFF])
        self.ffn_w_out = self.din('ffn_w_out', [L, DFF, D])
        self.ffn_convw_d = self.din('ffn_convw_T', [128, DEPTH, 86, 3])
        self.ffn_convb_d = self.din('ffn_convb_T', [128, DEPTH, 86])
        if self.dbg:
            self.dbg_x = nc.dram_tensor('dbg_x', [D, T], F32, kind="ExternalOutput").ap()
            self.dbg_mix = nc.dram_tensor('dbg_mix', [D, T], BF16, kind="ExternalOutput").ap()
            self.dbg_h = nc.dram_tensor('dbg_h', [128, KC, T], BF16, kind="ExternalOutput").ap()
            self.dbg_mod = nc.dram_tensor('dbg_mod', [128, 96], F32, kind="ExternalOutput").ap()
        self.xT = self.dscr('xT', [D, T], F32); self.R_xTn = [Res('xT%d' % n) for n in range(NT)]
        self.qT = self.dscr('qT', [1024, T], BF16); self.R_qT = Res('qT')
        self.kT = self.dscr('kT', [1024, T], BF16); self.R_kT = Res('kT')
        self.vv = self.dscr('vv', [T, 1024], BF16); self.R_vv = Res('vv')
        self.xbT = self.dscr('xbT', [1024, T], F32); self.R_xbT = Res('xbT')
        self.ybT = self.dscr('ybT', [1024, T], BF16); self.R_ybT = Res('ybT')
        self.mixT = self.dscr('mixT', [D, T], BF16); self.R_mixT = Res('mixT')
        self.midT = self.dscr('midT', [DFF, T], BF16); self.R_midT = Res('midT')
        self.ropeT = self.dscr('ropeT', [4, 128, T], F32); self.R_ropeT = Res('ropeT')
        self.identf = self.sb(g, 'identf', [128, 128], F32); self.R_identf = Res('identf')
        self.identb = self.sb(g, 'identb', [128, 128], BF16); self.R_identb = Res('identb')
        self.onesb = self.sb(g, 'onesb', [128, 128], BF16); self.R_onesb = Res('onesb')
        self.masks = self.sb(g, 'masks', [128, 3, 512], BF16); self.R_masks = Res('masks')
        self.m3 = self.sb(g, 'm3', [128, 4, 512], BF16); self.R_m3 = Res('m3')
        self.pswap = self.sb(g, 'pswap', [128, 2, 128], BF16); self.R_pswap = Res('pswap')
        self.normfin = self.sb(g, 'normfin', [128, KC], F32); self.R_normfin = Res('normfin')
        self.normmix = self.sb(g, 'normmix', [128, DEPTH, KC], F32); self.R_normmix = Res('normmix')
        self.normffn = self.sb(g, 'normffn', [128, DEPTH, KC], F32); self.R_normffn = Res('normffn')
        self.adab = self.sb(g, 'adab', [128, DEPTH, 96], F32); self.R_adab = Res('adab')
        self.epsc = self.sb(g, 'epsc', [128, 1], F32); self.R_epsc = Res('epsc')
        self.cond = self.sb(g, 'cond', [128, KC], BF16); self.R_cond = Res('cond')
        self.modT_ = [self.sb(g, 'modT%d' % i, [128, 96], F32) for i in range(2)]; self.R_modT_ = [Res('modT%d' % i) for i in range(2)]
        self.gm_ = [self.sb(g, 'gm%d' % i, [128, 2, KC], F32) for i in range(2)]; self.R_gm_ = [Res('gm%d' % i) for i in range(2)]
        self.modT, self.R_modT, self.gm, self.R_gm = self.modT_[0], self.R_modT_[0], self.gm_[0], self.R_gm_[0]
        self.actbuf = self.sb(g, 'actbuf', [128, KC, T], BF16); self.R_act = Res('actbuf')
        self.wslots = []
        for i in range(3):
            self.wslots.append((self.sb(g, 'wslot%d' % i, [128, WSLOT], BF16), Res('wslot%d' % i)))
        self.pb = [g.enter_context(nc.psum_tensor('pb%d' % i, [128, 512], F32)) for i in range(8)]
        self.pbanks = [(self.pb[i], Res('pb%d' % i)) for i in range(8)]
        self.pnext = 0; self.pa = 0; self.pbn = 0; self.p7 = 0
        for dst, src, R in [(self.identf, identf_d, self.R_identf), (self.identb, identb_d, self.R_identb),
                            (self.masks, masks_d, self.R_masks), (self.m3, m3_d, self.R_m3), (self.pswap, pswap_d, self.R_pswap),
                            (self.normfin, normfin_d, self.R_normfin), (self.normmix, normmix_d, self.R_normmix),
                            (self.normffn, normffn_d, self.R_normffn), (self.adab, adab_d, self.R_adab)]:
            self.ld(dst[:], src, [], [R], chan='init')
        self.memset(self.onesb[:], 1.0, [self.R_onesb])
        self.memset(self.epsc[:], EPS, [self.R_epsc])

        for l in range(L):
            self.plan_layer(l)

        self.phase_cond(cT_d)
        self.phase_rope_tables(pos_d, ropec_d)
        self.phase_load_x(x_d)
        for l in range(L):
            self.layer(l)
        if self.dbg:
            self.ld(self.dbg_x, self.xT, self.R_xTn, [Res('dbgx')], chan='ch_out')
            self.ld(self.dbg_mix, self.mixT, [self.R_mixT], [Res('dbgm')], chan='ch_out')
        self.phase_final(self.out_d)
        S.emit()
        return nc

    def plan_layer(self, l):
        if l == 0:
            for t in range(24):
                self.plan_k2048(('ada', l, t), self.ada_w[l], t * 512, 512)
        if l % 2 == 0:
            e = l // 2
            for t in range(10):
                self.plan_k2048(('evin', l, t), self.ev_w_in[e], t * 512, 512)
            for t in range(4):
                self.plan_k2048(('wout', l, t), self.ev_w_out[e], t * 512, 512)
        else:
            o = l // 2
            W = self.od_w_in[o]
            for t in range(2):
                parts = []
                for c in range(4):
                    for half in range(2):
                        hd = half * 8 + t * 4 + c
                        parts.append(((lambda slot, c=c, half=half: self.wview(slot, KC, 512)[:, :, c * 128 + half * 64: c * 128 + (half + 1) * 64]),
                                      W[:, hd * 64:(hd + 1) * 64].rearrange("(k p) n -> p k n", p=128)))
                self.wplan(('odq', l, t), parts)
            self.plan_k2048(('odkv', l, 0), W, 1024, 256)
            for t in range(2):
                self.plan_k2048(('odu', l, t), W, 1280 + t * 512, 512)
            G = self.od_glu_w[o]
            for t in range(2):
                self.wplan(('glu', l, t), [((lambda slot: self.wview(slot, 8, 512)), G[:, t * 512:(t + 1) * 512].rearrange("(k p) n -> p k n", p=128))])
            W2_ = self.od_w_out[o]
            for t in range(4):
                cs_ = slice(t * 512, (t + 1) * 512)
                parts = [((lambda slot: self.wview(slot, KC, 512)[0:64, 0:8, :]), W2_[0:512, cs_].rearrange("(k p) n -> p k n", p=64)),
                         ((lambda slot: self.wview(slot, KC, 512)[64:128, 0:8, :]), W2_[512:1024, cs_].rearrange("(k p) n -> p k n", p=64)),
                         ((lambda slot: self.wview(slot, KC, 512)[:, 8:16, :]), W2_[1024:2048, cs_].rearrange("(k p) n -> p k n", p=128))]
                self.wplan(('wout', l, t), parts)
        W = self.ffn_w_in[l]
        for t in range(22):
            j0 = 2 * t
            nj = min(2, NJ - j0)
            w = nj * 128
            parts = []
            for br in range(2):
                for lo in (0, 8):
                    parts.append(((lambda slot, lo=lo, br=br, w=w: self.wview(slot, KC, 2 * w)[:, lo:lo + 8, br * w:(br + 1) * w]),
                                  W[lo * 128:(lo + 8) * 128, br * DFF + j0 * 128: br * DFF + j0 * 128 + w].rearrange("(k p) n -> p k n", p=128)))
            self.wplan(('ffin', l, t), parts)
            if l + 1 < self.nlayers:
                self.plan_k2048(('ada', l + 1, t), self.ada_w[l + 1], t * 512, 512)
        if l + 1 < self.nlayers:
            for t in (22, 23):
                self.plan_k2048(('ada', l + 1, t), self.ada_w[l + 1], t * 512, 512)
        W2 = self.ffn_w_out[l]
        for th in range(2):
            for dp in range(8):
                parts = []
                for (lo, hi) in ((0, 11), (11, 22), (22, 33), (33, 43)):
                    parts.append(((lambda slot, lo=lo, hi=hi: self.wview(slot, NJ, 256)[:, lo:hi, :]),
                                  W2[lo * 128:hi * 128, dp * 256:(dp + 1) * 256].rearrange("(k p) n -> p k n", p=128)))
                self.wplan(('ffout', l, th, dp), parts)

    def phase_cond(self, cT_d):
        with contextlib.ExitStack() as st:
            c = self.sb(st, 'c_in', [128, KC], F32); R_c = Res('c_in')
            self.ld(c[:], cT_d, [], [R_c])
            self.act(self.cond[:], c[:], AF.Silu, [R_c], [self.R_cond])
        self.S.barrier()

    def phase_rope_tables(self, pos_d, ropec_d):
        S = self.S
        with contextlib.ExitStack() as st:
            posi = self.sb(st, 'posi', [128, T], I32); R_posi = Res('posi')
            posf = self.sb(st, 'posf', [128, T], F32); R_posf = Res('posf')
            rc = self.sb(st, 'ropec', [128, 4], F32); R_rc = Res('ropec')
            ang = self.sb(st, 'ang', [128, T], F32); R_ang = Res('ang')
            ki = self.sb(st, 'ki', [128, T], I32); R_ki = Res('ki')
            kf = self.sb(st, 'kf', [128, T], F32); R_kf = Res('kf')
            tmp = self.sb(st, 'rtmp', [128, T], F32); R_tmp = Res('rtmp')
            cs = self.sb(st, 'rcos', [128, T], F32); R_cs = Res('rcos')
            self.ld(posi[:], pos_d, [], [R_posi])
            self.ld(rc[:], ropec_d, [], [R_rc])
            self.cp(posf[:], posi[:], [R_posi], [R_posf])
            C1 = 6.28125
            C2 = float(2 * np.pi - C1)
            for v in range(2):
                inv = rc[:, 2 * v:2 * v + 1]; sgn = rc[:, 2 * v + 1:2 * v + 2]
                self.ts(ang[:], posf[:], inv, None, ALU.mult, None, [R_posf, R_rc], [R_ang])
                self.ts(tmp[:], ang[:], float(1.0 / (2 * np.pi)), None, ALU.mult, None, [R_ang], [R_tmp])
                self.cp(ki[:], tmp[:], [R_tmp], [R_ki])
                self.cp(kf[:], ki[:], [R_ki], [R_kf])
                self.stt(ang[:], kf[:], -C1, ang[:], ALU.mult, ALU.add, [R_kf, R_ang], [R_ang])
                self.stt(ang[:], kf[:], -C2, ang[:], ALU.mult, ALU.add, [R_kf, R_ang], [R_ang])
                self.ts(tmp[:], ang[:], float(np.pi), float(-2 * np.pi), ALU.is_gt, ALU.mult, [R_ang], [R_tmp])
                self.tt(ang[:], ang[:], tmp[:], ALU.add, [R_ang, R_tmp], [R_ang])
                self.ts(tmp[:], ang[:], float(-np.pi), float(2 * np.pi), ALU.is_lt, ALU.mult, [R_ang], [R_tmp])
                self.tt(ang[:], ang[:], tmp[:], ALU.add, [R_ang, R_tmp], [R_ang])
                self.ts(ang[:], ang[:], 3.14159, -3.14159, ALU.min, ALU.max, [R_ang], [R_ang])
                self.act(tmp[:], ang[:], AF.Sin, [R_ang], [R_tmp])
                self.ts(tmp[:], tmp[:], sgn, None, ALU.mult, None, [R_tmp, R_rc], [R_tmp])
                self.ld(self.ropeT[2 * v + 1], tmp[:], [R_tmp], [self.R_ropeT], join=True)
                self.act(cs[:], ang[:], AF.Sin, [R_ang], [R_cs], scale=0.5)
                self.act(cs[:], cs[:], AF.Square, [R_cs], [R_cs])
                self.ts(cs[:], cs[:], -2.0, 1.0, ALU.mult, ALU.add, [R_cs], [R_cs])
                self.ld(self.ropeT[2 * v], cs[:], [R_cs], [self.R_ropeT], join=True)
        S.barrier()

    def phase_load_x(self, x_d):
        S = self.S
        with contextlib.ExitStack() as st:
            xt = [self.sb(st, 'lx_in%d' % i, [128, 4, D], F32) for i in range(1)]
            R_xt = [Res('lx_in%d' % i) for i in range(1)]
            ot = [self.sb(st, 'lx_o%d' % i, [128, 512], F32) for i in range(4)]
            R_ot = [Res('lx_o%d' % i) for i in range(4)]
            cnt = 0
            for n in range(NT):
                b = 0
                self.ld(xt[b][:], x_d[n * 512:(n + 1) * 512, :].rearrange("(a p) d -> p a d", p=128), [], [R_xt[b]])
                for kc in range(KC):
                    pt, R_pt = self.bank()
                    for a in range(4):
                        self.tr(pt[:, a * 128:(a + 1) * 128], xt[b][:, a, kc * 128:(kc + 1) * 128], [R_xt[b]], [R_pt], a > 0)
                    o = cnt % 4; cnt += 1
                    self.cp(ot[o][:], pt[:], [R_pt], [R_ot[o]], eng=('act' if kc % 2 else 'dve'))
                    self.ld(self.xT[kc * 128:(kc + 1) * 128, n * 512:(n + 1) * 512], ot[o][:], [R_ot[o]], [self.R_xTn[n]], join=(kc > 0))
        S.barrier()

    def norm_tile(self, xin, R_xin, sq, R_sq, rstd, R_rstd, ncols):
        S = self.S
        self.act(sq[:], xin[:], AF.Square, [R_xin], [R_sq])
        pt, R_pt = self.bank()
        for kc in range(KC):
            self.mm(pt[:, 0:ncols], self.onesb[:], sq[:, kc, :], kc == 0, kc == KC - 1, [self.R_onesb, R_sq], [R_pt], kc > 0)
        self.act(rstd[:], pt[:, 0:ncols], AF.Sqrt, [R_pt, self.R_epsc], [R_rstd], bias=self.epsc[:], scale=1.0 / D)
        S.op('dve', lambda e: e.reciprocal(out=rstd[:], in_=rstd[:]), reads=[R_rstd], writes=[R_rstd])

    def phase_final(self, out_d):
        S = self.S
        NC_ = 256
        with contextlib.ExitStack() as st:
            xin = [self.sb(st, 'fx%d' % i, [128, KC, NC_], F32) for i in range(1)]
            R_xin = [Res('fx%d' % i) for i in range(1)]
            sq = self.sb(st, 'fsq', [128, KC, NC_], BF16); R_sq = Res('fsq')
            rstd = self.sb(st, 'frstd', [128, NC_], F32); R_rstd = Res('frstd')
            yt = self.sb(st, 'fy', [128, KC, NC_], F32); R_yt = Res('fy')
            ot = [self.sb(st, 'fo%d' % i, [128, D], F32) for i in range(2)]
            R_ot = [Res('fo%d' % i) for i in range(2)]
            oc = 0
            for n in range(T // NC_):
                b = 0
                self.ld(xin[b][:], self.xT[:, n * NC_:(n + 1) * NC_].rearrange("(k p) t -> p k t", p=128), [self.R_xTn[n // 2]], [R_xin[b]])
                self.norm_tile(xin[b], R_xin[b], sq, R_sq, rstd, R_rstd, NC_)
                for kc in range(KC):
                    self.stt(yt[:, kc, :], xin[b][:, kc, :], self.normfin[:, kc:kc + 1], rstd[:], ALU.mult, ALU.mult,
                             [R_xin[b], self.R_normfin, R_rstd], [R_yt], join=(kc > 0))
                for a in range(NC_ // 128):
                    o = oc % 2; oc += 1
                    for q in range(4):
                        pt, R_pt = self.bank()
                        for j in range(4):
                            kc = q * 4 + j
                            self.tr(pt[:, j * 128:(j + 1) * 128], yt[:, kc, a * 128:(a + 1) * 128], [R_yt], [R_pt], j > 0)
                        self.cp(ot[o][:, q * 512:(q + 1) * 512], pt[:], [R_pt], [R_ot[o]], eng=('act' if q % 2 else 'dve'), join=(q > 0))
                    r0 = n * NC_ + a * 128
                    self.ld(out_d[r0:r0 + 128, :], ot[o][:], [R_ot[o]], [Res('outd')], chan='ch_out')
        S.barrier()

    def layer(self, l):
        self.modT, self.R_modT, self.gm, self.R_gm = self.modT_[l % 2], self.R_modT_[l % 2], self.gm_[l % 2], self.R_gm_[l % 2]
        stop = getattr(self, 'stop', None)
        order = ['ada', 'norm', 'inproj', 'attn', 'lru', 'outproj', 'norm2', 'ffn_in', 'ffn_out']
        lim = order.index(stop) if (stop and l == self.nlayers - 1) else 99
        if l == 0:
            self.phase_ada(l)
        if lim < 1: return
        self.phase_norm(l, 0)
        if self.dbg and l == self.nlayers - 1:
            self.ld(self.dbg_h, self.actbuf[:], [self.R_act], [Res('dbgh')], chan='ch_out')
            self.ld(self.dbg_mod, self.modT[:], [self.R_modT], [Res('dbgmod')], chan='ch_out')
        if lim < 2: return
        if l % 2 == 0:
            self.phase_inproj_even(l)
            if lim < 3: return
            self.phase_attn_A(l)
            if lim < 4: return
            self.phase_lru(l)
        else:
            self.phase_inproj_odd(l)
            if lim < 3: return
            self.phase_attn_C(l)
            if lim < 4: return
            self.phase_s5(l)
        if lim < 5: return
        self.phase_outproj(l)
        if lim < 6: return
        self.phase_norm(l, 1)
        if lim < 7: return
        self.phase_ffn_in(l)
        if lim < 8: return
        self.phase_ffn_out(l)

    def ada_tile(self, l, t):
        pt, R_pt = self.pbanks[7]
        slot, R_w = self.wnext(('ada', l, t))
        wt = self.wview(slot, KC, 512)
        for j in range(4):
            col = t * 4 + j
            for kc in range(KC):
                self.mm(pt[:, col:col + 1], wt[:, kc, j * 128:(j + 1) * 128], self.cond[:, kc:kc + 1], kc == 0, kc == KC - 1,
                        [R_w, self.R_cond], [R_pt], not (t == 0 and j == 0 and kc == 0))

    def ada_finish(self, l):
        pt, R_pt = self.pbanks[7]
        modT, R_modT, gm, R_gm = self.modT_[l % 2], self.R_modT_[l % 2], self.gm_[l % 2], self.R_gm_[l % 2]
        self.tt(modT[:], pt[:, 0:96], self.adab[:, l, :], ALU.add, [R_pt, self.R_adab], [R_modT])
        self.stt(gm[:, 0, :], modT[:, 16:32], 1.0, self.normmix[:, l, :], ALU.add, ALU.mult, [R_modT, self.R_normmix], [R_gm])
        self.stt(gm[:, 1, :], modT[:, 64:80], 1.0, self.normffn[:, l, :], ALU.add, ALU.mult, [R_modT, self.R_normffn], [R_gm], join=True)

    def phase_ada(self, l):
        for t in range(24):
            self.ada_tile(l, t)
        self.ada_finish(l)
        self.S.barrier()

    def phase_norm(self, l, which):
        S = self.S
        NC_ = 256
        sh0 = 0 if which == 0 else 48
        with contextlib.ExitStack() as st:
            xin = [self.sb(st, 'nx%d' % i, [128, KC, NC_], F32) for i in range(2)]
            R_xin = [Res('nx%d' % i) for i in range(2)]
            sq_ = [self.sb(st, 'nsq%d' % i, [128, KC, NC_], BF16) for i in range(2)]; R_sq_ = [Res('nsq%d' % i) for i in range(2)]
            rstd_ = [self.sb(st, 'nrstd%d' % i, [128, NC_], F32) for i in range(2)]; R_rstd_ = [Res('nrstd%d' % i) for i in range(2)]
            tmp = [self.sb(st, 'ntmp%d' % i, [128, NC_], F32) for i in range(2)]
            R_tmp = [Res('ntmp%d' % i) for i in range(2)]
            for n in range(T // NC_):
                b = n % 2
                sq, R_sq, rstd, R_rstd = sq_[b], R_sq_[b], rstd_[b], R_rstd_[b]
                self.ld(xin[b][:], self.xT[:, n * NC_:(n + 1) * NC_].rearrange("(k p) t -> p k t", p=128), [self.R_xTn[n // 2]], [R_xin[b]])
                self.norm_tile(xin[b], R_xin[b], sq, R_sq, rstd, R_rstd, NC_)
                for kc in range(KC):
                    tb = kc % 2
                    self.tt(tmp[tb][:], xin[b][:, kc, :], rstd[:], ALU.mult, [R_xin[b], R_rstd], [R_tmp[tb]])
                    self.act(self.actbuf[:, kc, n * NC_:(n + 1) * NC_], tmp[tb][:], AF.Identity, [R_tmp[tb], self.R_gm, self.R_modT], [self.R_act],
                             bias=self.modT[:, sh0 + kc:sh0 + kc + 1], scale=self.gm[:, which, kc:kc + 1], join=not (n == 0 and kc == 0))
        S.barrier()

    def phase_inproj_even(self, l):
        S = self.S
        with contextlib.ExitStack() as st:
            cosT = self.sb(st, 'cosA', [128, T], F32); R_cos = Res('cosA')
            sinT = self.sb(st, 'sinA', [128, T], F32); R_sin = Res('sinA')
            self.ld(cosT[:], self.ropeT[0], [self.R_ropeT], [R_cos])
            self.ld(sinT[:], self.ropeT[1], [self.R_ropeT], [R_sin])
            qb = [self.sb(st, 'qraw%d' % i, [128, 512], BF16) for i in range(3)]; R_qb = [Res('qraw%d' % i) for i in range(3)]
            t1 = [self.sb(st, 'rt1_%d' % i, [128, 512], F32) for i in range(3)]; R_t1 = [Res('rt1_%d' % i) for i in range(3)]
            t2 = [self.sb(st, 'rt2_%d' % i, [128, 512], F32) for i in range(3)]; R_t2 = [Res('rt2_%d' % i) for i in range(3)]
            ob = [self.sb(st, 'obf%d' % i, [128, 512], BF16) for i in range(4)]; R_ob = [Res('obf%d' % i) for i in range(4)]
            of = [self.sb(st, 'of32_%d' % i, [128, 512], F32) for i in range(2)]; R_of = [Res('of32_%d' % i) for i in range(2)]
            cnt = {'q': 0, 'o': 0, 'f': 0}
            first = {'qT': True, 'kT': True, 'vv': True, 'xbT': True, 'ybT': True}
            pend = []

            def store(dst, src, R_src, name, R_dst):
                self.ld(dst, src, [R_src], [R_dst], join=not first[name])
                first[name] = False

            def flush():
                while pend:
                    pt, R_pt, i, ns, kind, ch = pend.pop(0)
                    o = cnt['o'] % 4; cnt['o'] += 1
                    p2, R_p2 = self.bank('B')
                    self.mm(p2[:], self.pswap[:, 0, :], qb[i][:], True, True, [self.R_pswap, R_qb[i]], [R_p2], False)
                    self.tt(t1[i][:], pt[:], cosT[:, ns], ALU.mult, [R_pt, R_cos, R_qb[i]], [R_t1[i]])
                    self.tt(t2[i][:], p2[:], sinT[:, ns], ALU.mult, [R_p2, R_sin], [R_t2[i]])
                    self.tt(ob[o][:], t1[i][:], t2[i][:], ALU.add, [R_t1[i], R_t2[i]], [R_ob[o]])
                    if kind == 'q':
                        store(self.qT[ch * 128:(ch + 1) * 128, ns], ob[o][:], R_ob[o], 'qT', self.R_qT)
                    else:
                        store(self.kT[ch * 128:(ch + 1) * 128, ns], ob[o][:], R_ob[o], 'kT', self.R_kT)

            for t in range(10):
                slot, R_w = self.wnext(('evin', l, t))
                if t >= getattr(self, 'ntile_lim', 99):
                    continue
                wt = self.wview(slot, KC, 512)
                kind = ['q', 'q', 'k', 'k', 'v', 'v', 'xb', 'xb', 'yb', 'yb'][t]
                half = t % 2
                if kind == 'v':
                    flush()
                    for tt_ in range(16):
                        pt, R_pt = self.bank('A')
                        for kc in range(KC):
                            self.mm(pt[:], self.actbuf[:, kc, tt_ * 128:(tt_ + 1) * 128], wt[:, kc, :], kc == 0, kc == KC - 1, [R_w, self.R_act], [R_pt], kc > 0)
                        o = cnt['o'] % 4; cnt['o'] += 1
                        self.cp(ob[o][:], pt[:], [R_pt], [R_ob[o]], eng=('act' if tt_ % 2 else 'dve'))
                        store(self.vv[tt_ * 128:(tt_ + 1) * 128, half * 512:(half + 1) * 512], ob[o][:], R_ob[o], 'vv', self.R_vv)
                    continue
                for c in range(4):
                    ch = half * 4 + c
                    for n in range(NT):
                        pt, R_pt = self.bank('A')
                        for kc in range(KC):
                            self.mm(pt[:], wt[:, kc, c * 128:(c + 1) * 128], self.actbuf[:, kc, n * 512:(n + 1) * 512], kc == 0, kc == KC - 1, [R_w, self.R_act], [R_pt], kc > 0)
                        ns = slice(n * 512, (n + 1) * 512)
                        if kind in ('q', 'k'):
                            i = cnt['q'] % 3; cnt['q'] += 1
                            self.cp(qb[i][:], pt[:], [R_pt], [R_qb[i]], eng='act')
                            flush()
                            pend.append((pt, R_pt, i, ns, kind, ch))
                        elif kind == 'xb':
                            f = cnt['f'] % 2; cnt['f'] += 1
                            self.cp(of[f][:], pt[:], [R_pt], [R_of[f]], eng=('act' if n % 2 else 'dve'))
                            store(self.xbT[ch * 128:(ch + 1) * 128, ns], of[f][:], R_of[f], 'xbT', self.R_xbT)
                        else:
                            o = cnt['o'] % 4; cnt['o'] += 1
                            self.act(ob[o][:], pt[:], AF.Gelu_apprx_tanh, [R_pt], [R_ob[o]])
                            store(self.ybT[ch * 128:(ch + 1) * 128, ns], ob[o][:], R_ob[o], 'ybT', self.R_ybT)
            flush()
        S.barrier()

    def phase_attn_A(self, l):
        S = self.S
        scale = 128.0 ** -0.5
        with contextlib.ExitStack() as st:
            qh = [self.sb(st, 'qh%d' % i, [128, T], BF16) for i in range(2)]; R_qh = [Res('qh%d' % i) for i in range(2)]
            kh = [self.sb(st, 'kh%d' % i, [128, T], BF16) for i in range(2)]; R_kh = [Res('kh%d' % i) for i in range(2)]
            v1 = [self.sb(st, 'v1_%d' % i, [128, 16, 128], BF16) for i in range(2)]; R_v1 = [Res('v1_%d' % i) for i in range(2)]
            v2 = [self.sb(st, 'v2_%d' % i, [128, 4, 4, 128], BF16) for i in range(2)]; R_v2 = [Res('v2_%d' % i) for i in range(2)]
            v3 = [self.sb(st, 'v3_%d' % i, [128, 16, 128], BF16) for i in range(2)]; R_v3 = [Res('v3_%d' % i) for i in range(2)]
            P = [self.sb(st, 'P%d' % i, [128, 512], BF16) for i in range(4)]; R_P = [Res('P%d' % i) for i in range(4)]
            rden = [self.sb(st, 'rden%d' % i, [128, 512], F32) for i in range(2)]; R_rden = [Res('rden%d' % i) for i in range(2)]
            ao = [self.sb(st, 'ao%d' % i, [128, 512], BF16) for i in range(2)]; R_ao = [Res('ao%d' % i) for i in range(2)]
            cur4 = self.masks[:, 0, :]; prev4 = self.masks[:, 1, :]
            state = {'pc': 0, 'ec': 0, 'first_store': True}
            rounds = []

            def loads(hd):
                b = hd % 2
                hs = slice(hd * 128, (hd + 1) * 128)
                self.ld(qh[b][:], self.qT[hs, :], [self.R_qT], [R_qh[b]])
                self.ld(kh[b][:], self.kT[hs, :], [self.R_kT], [R_kh[b]])
                self.ld(v1[b][:], self.vv[:, hs].rearrange("(blk j) c -> j blk c", j=128), [self.R_vv], [R_v1[b]])
                for r in range(4):
                    self.ld(v2[b][:, r, :, :], self.vv[:, hs].rearrange("(sb j r) c -> j r sb c", j=128, r=4)[:, r, :, :], [self.R_vv], [R_v2[b]], join=(r > 0))
                self.ld(v3[b][:], self.vv[:, hs].rearrange("(j r) c -> j r c", r=16), [self.R_vv], [R_v3[b]])

            for hd in range(8):
                for qt in range(NT):
                    ctx = {'O': None, 'first': True}
                    specs = []
                    for pat in (1, 2):
                        for which in ('prev', 'cur'):
                            if which == 'prev' and pat == 2 and qt == 0:
                                continue
                            specs.append((pat, which))
                    specs.append((3, 'cur'))
                    for si_, (pat, which) in enumerate(specs):
                        rd = {}

                        def S_part(hd=hd, qt=qt, pat=pat, which=which, ctx=ctx, rd=rd, si_=si_):
                            b = hd % 2
                            if hd == 0 and qt == 0 and si_ == 0:
                                loads(0)
                            if qt == 1 and si_ == 0 and hd + 1 < 8:
                                loads(hd + 1)
                            if si_ == 0:
                                ctx['O'] = self.bank('A'); ctx['DEN'] = self.bank('A')
                                ctx['stO'] = True; ctx['stD'] = True
                            q_, k_ = qh[b], kh[b]
                            Rq, Rk = R_qh[b], R_kh[b]
                            Sb, R_S = self.bank('B')
                            p_i = state['pc'] % 4; state['pc'] += 1
                            Pt, R_Pt = P[p_i], R_P[p_i]
                            rd['Pt'] = (Pt, R_Pt)
                            if pat == 3:
                                for r in range(16):
                                    self.mm(Sb[:, r * 32:(r + 1) * 32], k_[:, r:T:16], q_[:, qt * 512 + r:(qt + 1) * 512:16], r == 0, False, [Rk, Rq], [R_S], r > 0)
                                self.mm(Sb[:], self.identb[:], self.m3[:, qt, :], False, True, [self.R_identb, self.R_m3], [R_S], True)
                                self.act(Pt[:], Sb[:], AF.Exp, [R_S], [R_Pt], scale=scale)
                                return
                            c0 = 128 if (pat == 1 and which == 'prev' and qt == 0) else 0
                            rd['c0'] = c0
                            fs = True
                            for s_ in range(4):
                                if pat == 1:
                                    B = qt * 4 + s_
                                    kb = B - 1 if which == 'prev' else B
                                    if kb < 0:
                                        continue
                                    kap = k_[:, kb * 128:(kb + 1) * 128]
                                    qap = q_[:, B * 128:(B + 1) * 128]
                                else:
                                    sbk = qt - 1 if which == 'prev' else qt
                                    kap = k_[:, sbk * 512 + s_:(sbk + 1) * 512:4]
                                    qap = q_[:, qt * 512 + s_:(qt + 1) * 512:4]
                                self.mm(Sb[:, s_ * 128:(s_ + 1) * 128], kap, qap, fs, False, [Rk, Rq], [R_S], not fs)
                                fs = False
                            msk = prev4 if which == 'prev' else cur4
                            self.mm(Sb[:, c0:512], self.identb[:], msk[:, c0:512], False, True, [self.R_identb, self.R_masks], [R_S], True)
                            self.act(Pt[:, c0:512], Sb[:, c0:512], AF.Exp, [R_S], [R_Pt], scale=scale)

                        def PV_part(hd=hd, qt=qt, pat=pat, which=which, ctx=ctx, rd=rd):
                            b = hd % 2
                            O, R_O = ctx['O']; DEN, R_DEN = ctx['DEN']
                            Pt, R_Pt = rd['Pt']

                            def pv(out_ap, lhsT, rhs, reads):
                                self.mm(out_ap, lhsT, rhs, ctx['stO'], False, reads, [R_O], not ctx['stO'])
                                ctx['stO'] = False

                            def den(out_ap, rhs, reads):
                                self.mm(out_ap, self.onesb[:], rhs, ctx['stD'], False, reads + [self.R_onesb], [R_DEN], not ctx['stD'])
                                ctx['stD'] = False

                            if pat == 3:
                                for r in range(16):
                                    pv(O[:, r:512:16], v3[b][:, r, :], Pt[:, r * 32:(r + 1) * 32], [R_v3[b], R_Pt])
                                    den(DEN[:, r:512:16], Pt[:, r * 32:(r + 1) * 32], [R_Pt])
                                return
                            c0 = rd['c0']
                            for s_ in range(4):
                                if pat == 1:
                                    B = qt * 4 + s_
                                    kb = B - 1 if which == 'prev' else B
                                    if kb < 0:
                                        continue
                                    pv(O[:, s_ * 128:(s_ + 1) * 128], v1[b][:, kb, :], Pt[:, s_ * 128:(s_ + 1) * 128], [R_v1[b], R_Pt])
                                else:
                                    sbk = qt - 1 if which == 'prev' else qt
                                    pv(O[:, s_:512:4], v2[b][:, s_, sbk, :], Pt[:, s_ * 128:(s_ + 1) * 128], [R_v2[b], R_Pt])
                                    den(DEN[:, s_:512:4], Pt[:, s_ * 128:(s_ + 1) * 128], [R_Pt])
                            if pat == 1:
                                den(DEN[:, c0:512], Pt[:, c0:512], [R_Pt])

                        post = None
                        if si_ == len(specs) - 1:
                            def post(hd=hd, qt=qt, ctx=ctx):
                                O, R_O = ctx['O']; DEN, R_DEN = ctx['DEN']
                                hs = slice(hd * 128, (hd + 1) * 128)
                                e = state['ec'] % 2; state['ec'] += 1
                                S.op('dve', lambda en, e=e, DEN=DEN: en.reciprocal(out=rden[e][:], in_=DEN[:]), reads=[R_DEN], writes=[R_rden[e]])
                                self.tt(ao[e][:], O[:], rden[e][:], ALU.mult, [R_O, R_rden[e]], [R_ao[e]])
                                self.ld(self.mixT[hs, qt * 512:(qt + 1) * 512], ao[e][:], [R_ao[e]], [self.R_mixT], join=not state['first_store'])
                                state['first_store'] = False
                        rounds.append((S_part, PV_part, post))
            prev = None
            for rnd in rounds:
                rnd[0]()
                if prev is not None:
                    prev[1]()
                    if prev[2]:
                        prev[2]()
                prev = rnd
            prev[1]()
            if prev[2]:
                prev[2]()
        S.barrier()

    def phase_lru(self, l):
        S = self.S
        e_ = l // 2
        with contextlib.ExitStack() as st:
            cw = self.sb(st, 'lcw', [128, 8, 4], F32); R_cw = Res('lcw')
            vec = self.sb(st, 'lvec', [128, 4, 8], F32); R_vec = Res('lvec')
            gwf = self.sb(st, 'lgwf', [128, 2, 8, 128], F32); R_gwf = Res('lgwf')
            gw = self.sb(st, 'lgw', [128, 2, 8, 128], BF16); R_gw = Res('lgw')
            cc = self.sb(st, 'lcc', [128, 2, 8], F32); R_cc = Res('lcc')
            self.ld(cw[:], self.ev_convw_d[:, e_], [], [R_cw])
            self.ld(vec[:], self.ev_vec_d[:, e_], [], [R_vec])
            self.ld(gwf[:], self.ev_gw_d[:, e_], [], [R_gwf])
            self.cp(gw[:], gwf[:], [R_gwf], [R_gw], eng='act')
            self.act(cc[:, 0, :], vec[:, 3, :], AF.Exp, [R_vec], [R_cc], scale=-1.0)
            self.act(cc[:, 0, :], cc[:, 0, :], AF.Ln, [R_cc], [R_cc], bias=1.0)
            self.ts(cc[:, 1, :], cc[:, 0, :], -16.0, None, ALU.mult, None, [R_cc], [R_cc])
            self.ts(cc[:, 0, :], cc[:, 0, :], -8.0, None, ALU.mult, None, [R_cc], [R_cc])
            xpad = self.sb(st, 'lxpad', [128, T + 3], F32); R_xpad = Res('lxpad')
            xc = self.sb(st, 'lxc', [128, T], F32); R_xc = Res('lxc')
            xcb = self.sb(st, 'lxcb', [128, T], BF16); R_xcb = Res('lxcb')
            rg = self.sb(st, 'lrg', [128, T], F32); R_rg = Res('lrg')
            ig = self.sb(st, 'lig', [128, T], F32); R_ig = Res('lig')
            aa = self.sb(st, 'laa', [128, T], F32); R_aa = Res('laa')
            yb = self.sb(st, 'lyb', [128, T], BF16); R_yb = Res('lyb')
            ob = self.sb(st, 'lob', [128, T], BF16); R_ob = Res('lob')
            self.memset(xpad[:, 0:3], 0.0, [R_xpad])
            for c in range(8):
                cs = slice(c * 128, (c + 1) * 128)
                self.ld(xpad[:, 3:T + 3], self.xbT[cs, :], [self.R_xbT], [R_xpad], join=True)
                self.ld(yb[:], self.ybT[cs, :], [self.R_ybT], [R_yb])
                self.ts(xc[:], xpad[:, 3:T + 3], cw[:, c, 3:4], vec[:, 0, c:c + 1], ALU.mult, ALU.add, [R_xpad, R_cw, R_vec], [R_xc])
                for i in (2, 1, 0):
                    self.stt(xc[:], xpad[:, i:T + i], cw[:, c, i:i + 1], xc[:], ALU.mult, ALU.add, [R_xpad, R_cw, R_xc], [R_xc])
                self.cp(xcb[:], xc[:], [R_xc], [R_xcb], eng='act')
                for gi, (dst, R_dst) in enumerate(((rg, R_rg), (ig, R_ig))):
                    for n in range(NT):
                        pt, R_pt = self.bank()
                        self.mm(pt[:], gw[:, gi, c, :], xcb[:, n * 512:(n + 1) * 512], True, True, [R_gw, R_xcb], [R_pt], False)
                        self.act(dst[:, n * 512:(n + 1) * 512], pt[:], AF.Sigmoid, [R_pt, R_vec], [R_dst], bias=vec[:, 1 + gi, c:c + 1], join=(n > 0))
                self.act(aa[:], rg[:], AF.Exp, [R_rg, R_cc], [R_aa], scale=cc[:, 0, c:c + 1])
                self.act(rg[:], rg[:], AF.Exp, [R_rg, R_cc], [R_rg], scale=cc[:, 1, c:c + 1])
                self.act(rg[:], rg[:], AF.Sqrt, [R_rg], [R_rg], scale=-1.0, bias=1.0)
                self.tt(ig[:], ig[:], xc[:], ALU.mult, [R_ig, R_xc], [R_ig])
                self.tt(ig[:], ig[:], rg[:], ALU.mult, [R_ig, R_rg], [R_ig])
                S.op('dve', lambda en: en.tensor_tensor_scan(out=xc[:], data0=aa[:], data1=ig[:], initial=0.0, op0=ALU.mult, op1=ALU.add),
                     reads=[R_aa, R_ig], writes=[R_xc])
                self.tt(ob[:], xc[:], yb[:], ALU.mult, [R_xc, R_yb], [R_ob])
                self.ld(self.mixT[(8 + c) * 128:(9 + c) * 128, :], ob[:], [R_ob], [self.R_mixT], join=True)
        S.barrier()


    def phase_inproj_odd(self, l):
        S = self.S
        with contextlib.ExitStack() as st:
            cosT = self.sb(st, 'cosC', [128, T], F32); R_cos = Res('cosC')
            sinT = self.sb(st, 'sinC', [128, T], F32); R_sin = Res('sinC')
            self.ld(cosT[:], self.ropeT[2], [self.R_ropeT], [R_cos])
            self.ld(sinT[:], self.ropeT[3], [self.R_ropeT], [R_sin])
            qb = [self.sb(st, 'qraw%d' % i, [128, 512], BF16) for i in range(2)]; R_qb = [Res('qraw%d' % i) for i in range(2)]
            t1 = [self.sb(st, 'rt1_%d' % i, [128, 512], F32) for i in range(2)]; R_t1 = [Res('rt1_%d' % i) for i in range(2)]
            t2 = [self.sb(st, 'rt2_%d' % i, [128, 512], F32) for i in range(2)]; R_t2 = [Res('rt2_%d' % i) for i in range(2)]
            ob = [self.sb(st, 'obf%d' % i, [128, 512], BF16) for i in range(4)]; R_ob = [Res('obf%d' % i) for i in range(4)]
            cnt = {'q': 0, 'o': 0}
            first = {'qT': True, 'kT': True, 'vv': True, 'ybT': True}

            def store(dst, src, R_src, name, R_dst):
                self.ld(dst, src, [R_src], [R_dst], join=not first[name])
                first[name] = False

            def proj_fm(wt, R_w, c, n):
                pt, R_pt = self.bank('A')
                for kc in range(KC):
                    self.mm(pt[:], wt[:, kc, c * 128:(c + 1) * 128], self.actbuf[:, kc, n * 512:(n + 1) * 512], kc == 0, kc == KC - 1, [R_w, self.R_act], [R_pt], kc > 0)
                return pt, R_pt

            def rope_store(pt, R_pt, n, dst, name, R_dst):
                ns = slice(n * 512, (n + 1) * 512)
                i = cnt['q'] % 2; cnt['q'] += 1
                o = cnt['o'] % 4; cnt['o'] += 1
                self.cp(qb[i][:], pt[:], [R_pt], [R_qb[i]], eng='act')
                p2, R_p2 = self.bank('B')
                self.mm(p2[:], self.pswap[:, 1, :], qb[i][:], True, True, [self.R_pswap, R_qb[i]], [R_p2], False)
                self.tt(t1[i][:], pt[:], cosT[:, ns], ALU.mult, [R_pt, R_cos, R_qb[i]], [R_t1[i]])
                self.tt(t2[i][:], p2[:], sinT[:, ns], ALU.mult, [R_p2, R_sin], [R_t2[i]])
                self.tt(ob[o][:], t1[i][:], t2[i][:], ALU.add, [R_t1[i], R_t2[i]], [R_ob[o]])
                store(dst, ob[o][:], R_ob[o], name, R_dst)

            for t in range(2):
                slot, R_w = self.wnext(('odq', l, t))
                wt = self.wview(slot, KC, 512)
                for c in range(4):
                    jt = t * 4 + c
                    for n in range(NT):
                        pt, R_pt = proj_fm(wt, R_w, c, n)
                        rope_store(pt, R_pt, n, self.qT[jt * 128:(jt + 1) * 128, n * 512:(n + 1) * 512], 'qT', self.R_qT)
            slot, R_w = self.wnext(('odkv', l, 0))
            wt = self.wview(slot, KC, 256)
            for n in range(NT):
                pt, R_pt = proj_fm(wt, R_w, 0, n)
                rope_store(pt, R_pt, n, self.kT[0:128, n * 512:(n + 1) * 512], 'kT', self.R_kT)
            for tt_ in range(16):
                pt, R_pt = self.bank('A')
                for kc in range(KC):
                    self.mm(pt[:, 0:128], self.actbuf[:, kc, tt_ * 128:(tt_ + 1) * 128], wt[:, kc, 128:256], kc == 0, kc == KC - 1, [R_w, self.R_act], [R_pt], kc > 0)
                o = cnt['o'] % 4; cnt['o'] += 1
                self.cp(ob[o][:, 0:128], pt[:, 0:128], [R_pt], [R_ob[o]], eng=('act' if tt_ % 2 else 'dve'))
                store(self.vv[tt_ * 128:(tt_ + 1) * 128, 0:128], ob[o][:, 0:128], R_ob[o], 'vv', self.R_vv)
            for t in range(2):
                slot, R_w = self.wnext(('odu', l, t))
                wt = self.wview(slot, KC, 512)
                for c in range(4):
                    ch = t * 4 + c
                    for n in range(NT):
                        pt, R_pt = proj_fm(wt, R_w, c, n)
                        o = cnt['o'] % 4; cnt['o'] += 1
                        self.cp(ob[o][:], pt[:], [R_pt], [R_ob[o]], eng=('act' if n % 2 else 'dve'))
                        store(self.ybT[ch * 128:(ch + 1) * 128, n * 512:(n + 1) * 512], ob[o][:], R_ob[o], 'ybT', self.R_ybT)
        S.barrier()

    def phase_attn_C(self, l):
        S = self.S
        o_ = l // 2
        scale = 64.0 ** -0.5
        with contextlib.ExitStack() as st:
            odv = self.sb(st, 'codv', [128, 3, 8], F32); R_odv = Res('codv')
            esk = self.sb(st, 'cesk', [128, 8], F32); R_esk = Res('cesk')
            self.ld(odv[:], self.odv_d[:, o_], [], [R_odv])
            self.act(esk[:], odv[:, 2, :], AF.Exp, [R_odv], [R_esk])
            kh = self.sb(st, 'ckh', [128, T], BF16); R_kh = Res('ckh')
            v1 = self.sb(st, 'cv1', [128, 16, 128], BF16); R_v1 = Res('cv1')
            self.ld(kh[:], self.kT[0:128, :], [self.R_kT], [R_kh])
            self.ld(v1[:], self.vv[:, 0:128].rearrange("(blk j) c -> j blk c", j=128), [self.R_vv], [R_v1])
            qh = [self.sb(st, 'cqh%d' % i, [128, T], BF16) for i in range(2)]; R_qh = [Res('cqh%d' % i) for i in range(2)]
            P = [self.sb(st, 'cP%d' % i, [128, 512], BF16) for i in range(4)]; R_P = [Res('cP%d' % i) for i in range(4)]
            rden = [self.sb(st, 'crden%d' % i, [128, 512], F32) for i in range(2)]; R_rden = [Res('crden%d' % i) for i in range(2)]
            ao = [self.sb(st, 'cao%d' % i, [128, 512], BF16) for i in range(2)]; R_ao = [Res('cao%d' % i) for i in range(2)]
            cur4 = self.masks[:, 0, :]; prev4 = self.masks[:, 2, :]
            state = {'pc': 0, 'ec': 0, 'first_store': True}
            rounds = []
            for jt in range(8):
                for qt in range(NT):
                    ctx = {}
                    specs = [(hb, which) for hb in range(2) for which in ('prev', 'cur')]
                    for si_, (hb, which) in enumerate(specs):
                        rd = {}

                        def S_part(jt=jt, qt=qt, hb=hb, which=which, ctx=ctx, rd=rd, si_=si_):
                            b = jt % 2
                            if jt == 0 and qt == 0 and si_ == 0:
                                self.ld(qh[0][:], self.qT[0:128, :], [self.R_qT], [R_qh[0]])
                            if qt == 1 and si_ == 0 and jt + 1 < 8:
                                self.ld(qh[(jt + 1) % 2][:], self.qT[(jt + 1) * 128:(jt + 2) * 128, :], [self.R_qT], [R_qh[(jt + 1) % 2]])
                            if si_ == 0:
                                ctx['O'] = self.bank('A'); ctx['DEN'] = self.bank('A')
                            q_ = qh[b]; Rq = R_qh[b]
                            ps = slice(hb * 64, (hb + 1) * 64)
                            Sb, R_S = self.bank('B')
                            p_i = state['pc'] % 4; state['pc'] += 1
                            Pt, R_Pt = P[p_i], R_P[p_i]
                            rd['Pt'] = (Pt, R_Pt)
                            c0 = 128 if (which == 'prev' and qt == 0) else 0
                            rd['c0'] = c0
                            fs = True
                            for s_ in range(4):
                                B = qt * 4 + s_
                                kb = B - 1 if which == 'prev' else B
                                if kb < 0:
                                    continue
                                self.mm(Sb[:, s_ * 128:(s_ + 1) * 128], kh[ps, kb * 128:(kb + 1) * 128], q_[ps, B * 128:(B + 1) * 128], fs, False, [R_kh, Rq], [R_S], not fs)
                                fs = False
                            msk = prev4 if which == 'prev' else cur4
                            self.mm(Sb[:, c0:512], self.identb[:], msk[:, c0:512], False, True, [self.R_identb, self.R_masks], [R_S], True)
                            self.act(Pt[:, c0:512], Sb[:, c0:512], AF.Exp, [R_S], [R_Pt], scale=scale)

                        def PV_part(jt=jt, qt=qt, hb=hb, which=which, ctx=ctx, rd=rd, si_=si_):
                            O, R_O = ctx['O']; DEN, R_DEN = ctx['DEN']
                            Pt, R_Pt = rd['Pt']; c0 = rd['c0']
                            ps = slice(hb * 64, (hb + 1) * 64)
                            for s_ in range(4):
                                B = qt * 4 + s_
                                kb = B - 1 if which == 'prev' else B
                                if kb < 0:
                                    continue
                                stO = ctx.get(('stO', hb), True)
                                self.mm(O[ps, s_ * 128:(s_ + 1) * 128], v1[:, kb, ps], Pt[:, s_ * 128:(s_ + 1) * 128], stO, False, [R_v1, R_Pt], [R_O], not (stO and hb == 0))
                                ctx[('stO', hb)] = False
                            stD = ctx.get(('stD', hb), True)
                            self.mm(DEN[ps, c0:512], self.onesb[:, 0:64], Pt[:, c0:512], stD, False, [self.R_onesb, R_Pt], [R_DEN], not (stD and hb == 0))
                            ctx[('stD', hb)] = False

                        post = None
                        if si_ == len(specs) - 1:
                            def post(jt=jt, qt=qt, ctx=ctx):
                                O, R_O = ctx['O']; DEN, R_DEN = ctx['DEN']
                                e = state['ec'] % 2; state['ec'] += 1
                                self.ts(rden[e][:], DEN[:], esk[:, jt:jt + 1], None, ALU.add, None, [R_DEN, R_esk], [R_rden[e]])
                                S.op('dve', lambda en, e=e: en.reciprocal(out=rden[e][:], in_=rden[e][:]), reads=[R_rden[e]], writes=[R_rden[e]])
                                self.tt(ao[e][:], O[:], rden[e][:], ALU.mult, [R_O, R_rden[e]], [R_ao[e]])
                                self.ld(self.mixT[jt * 128:(jt + 1) * 128, qt * 512:(qt + 1) * 512], ao[e][:], [R_ao[e]], [self.R_mixT], join=not state['first_store'])
                                state['first_store'] = False
                        rounds.append((S_part, PV_part, post))
            prev = None
            for rnd in rounds:
                rnd[0]()
                if prev is not None:
                    prev[1]()
                    if prev[2]:
                        prev[2]()
                prev = rnd
            prev[1]()
            if prev[2]:
                prev[2]()
        S.barrier()

    def LB(self, st_, r):
        return self.actbuf[:, 8 + st_ // 8, ((st_ % 8) * 2 + r) * 128:((st_ % 8) * 2 + r + 1) * 128]

    def LC(self, st_, r):
        return self.actbuf[:, 12 + st_ // 8, ((st_ % 8) * 2 + r) * 128:((st_ % 8) * 2 + r + 1) * 128]

    def phase_s5(self, l):
        S = self.S
        o_ = l // 2
        TWO_PI = float(2 * np.pi)
        R_LB = Res('LB'); R_LC = Res('LC'); R_z = Res('z')
        with contextlib.ExitStack() as st:
            odv = self.sb(st, 's5odv', [128, 3, 8], F32); R_odv = Res('s5odv')
            sm = self.sb(st, 's5sm', [128, 12, 32], F32); R_sm = Res('s5sm')
            dsh = self.sb(st, 's5dsh', [128, 11, 32], F32); R_dsh = Res('s5dsh')
            st2 = contextlib.ExitStack()
            sv = self.sb(st2, 's5v', [128, 3, 32], F32); R_sv = Res('s5v')
            Bt = self.sb(st2, 's5b', [128, 2, 32, 16], F32); R_Bt = Res('s5b')
            Ct = self.sb(st2, 's5c', [128, 2, 32, 16], F32); R_Ct = Res('s5c')
            self.ld(sv[:], self.s5v_d[:, o_], [], [R_sv])
            self.ld(Bt[:], self.s5b_d[:, o_], [], [R_Bt])
            self.ld(Ct[:], self.s5c_d[:, o_], [], [R_Ct])
            self.ld(odv[:], self.odv_d[:, o_], [], [R_odv])
            DT, LR, TH, RHO, SN, CS, X_, Y_, KRE, KIM, NKIM, TMP = [sm[:, i, :] for i in range(12)]
            rs = [R_sm]
            self.act(DT, sv[:, 2, :], AF.Exp, [R_sv], rs)
            self.tt(LR, sv[:, 0, :], DT, ALU.mult, [R_sv, R_sm], rs)
            self.tt(TH, sv[:, 1, :], DT, ALU.mult, [R_sv, R_sm], rs)
            self.act(RHO, LR, AF.Exp, rs, rs)
            for _ in range(4):
                self.ts(TMP, TH, float(np.pi), -TWO_PI, ALU.is_gt, ALU.mult, rs, rs)
                self.tt(TH, TH, TMP, ALU.add, rs, rs)
            self.act(SN, TH, AF.Sin, rs, rs)
            self.act(CS, TH, AF.Sin, rs, rs, scale=0.5)
            self.act(CS, CS, AF.Square, rs, rs)
            self.ts(CS, CS, -2.0, 1.0, ALU.mult, ALU.add, rs, rs)
            self.tt(X_, RHO, CS, ALU.mult, rs, rs)
            self.ts(X_, X_, -1.0, None, ALU.add, None, rs, rs)
            self.tt(Y_, RHO, SN, ALU.mult, rs, rs)
            self.tt(TMP, sv[:, 0, :], sv[:, 0, :], ALU.mult, [R_sv], rs)
            self.tt(KRE, sv[:, 1, :], sv[:, 1, :], ALU.mult, [R_sv], rs)
            self.tt(TMP, TMP, KRE, ALU.add, rs, rs)
            S.op('dve', lambda en: en.reciprocal(out=TMP, in_=TMP), reads=rs, writes=rs)
            self.tt(KRE, X_, sv[:, 0, :], ALU.mult, rs + [R_sv], rs)
            self.tt(KIM, Y_, sv[:, 1, :], ALU.mult, rs + [R_sv], rs)
            self.tt(KRE, KRE, KIM, ALU.add, rs, rs)
            self.tt(KRE, KRE, TMP, ALU.mult, rs, rs)
            self.tt(KIM, Y_, sv[:, 0, :], ALU.mult, rs + [R_sv], rs)
            self.tt(NKIM, X_, sv[:, 1, :], ALU.mult, rs + [R_sv], rs)
            self.tt(KIM, KIM, NKIM, ALU.subtract, rs, rs)
            self.tt(KIM, KIM, TMP, ALU.mult, rs, rs)
            self.ts(NKIM, KIM, -1.0, None, ALU.mult, None, rs, rs)
            self.ts(TMP, TH, 0.0, TWO_PI, ALU.is_lt, ALU.mult, rs, rs)
            self.tt(dsh[:, 0, :], TH, TMP, ALU.add, rs, [R_dsh])
            for k in range(10):
                self.ts(dsh[:, k + 1, :], dsh[:, k, :], 2.0, None, ALU.mult, None, [R_dsh], [R_dsh])
                self.ts(TMP, dsh[:, k + 1, :], TWO_PI, -TWO_PI, ALU.is_ge, ALU.mult, [R_dsh], rs)
                self.tt(dsh[:, k + 1, :], dsh[:, k + 1, :], TMP, ALU.add, [R_dsh, R_sm], [R_dsh])
            Mt = [self.sb(st2, 's5M%d' % i, [128, 128], F32) for i in range(2)]; R_Mt = [Res('s5M%d' % i) for i in range(2)]
            self.memset(self.actbuf[:, 8:16, :], 0.0, [self.R_act])
            mc = 0
            for s_ in range(32):
                p0 = (s_ % 4) * 32
                for r in range(2):
                    m = mc % 2; mc += 1
                    self.memset(Mt[m][:], 0.0, [R_Mt[m]])
                    for hf in range(2):
                        rows = slice(hf * 64, (hf + 1) * 64)
                        cols = slice(p0 + hf * 16, p0 + hf * 16 + 16)
                        if r == 0:
                            self.ts(Mt[m][rows, cols], Bt[rows, 0, s_, :], KRE[rows, s_:s_ + 1], None, ALU.mult, None, [R_Bt, R_sm], [R_Mt[m]], join=True)
                            self.stt(Mt[m][rows, cols], Bt[rows, 1, s_, :], NKIM[rows, s_:s_ + 1], Mt[m][rows, cols], ALU.mult, ALU.add, [R_Bt, R_sm, R_Mt[m]], [R_Mt[m]])
                        else:
                            self.ts(Mt[m][rows, cols], Bt[rows, 1, s_, :], KRE[rows, s_:s_ + 1], None, ALU.mult, None, [R_Bt, R_sm], [R_Mt[m]], join=True)
                            self.stt(Mt[m][rows, cols], Bt[rows, 0, s_, :], KIM[rows, s_:s_ + 1], Mt[m][rows, cols], ALU.mult, ALU.add, [R_Bt, R_sm, R_Mt[m]], [R_Mt[m]])
                    pt, R_pt = self.bank('B')
                    self.tr(pt[:, 0:128], Mt[m][:], [R_Mt[m]], [R_pt], False)
                    self.cp(self.LB(s_, r), pt[:, 0:128], [R_pt], [R_LB, self.R_act], eng='act', join=True)
                    for hf in range(2):
                        rows = slice(hf * 64, (hf + 1) * 64)
                        cols = slice(p0 + hf * 16, p0 + hf * 16 + 16)
                        if r == 0:
                            self.cp(self.LC(s_, 0)[rows, cols], Ct[rows, 0, s_, :], [R_Ct], [R_LC, self.R_act], join=True)
                        else:
                            self.ts(self.LC(s_, 1)[rows, cols], Ct[rows, 1, s_, :], -1.0, None, ALU.mult, None, [R_Ct], [R_LC, self.R_act], join=True)
            S.barrier()
            st2.close()
            st3 = contextlib.ExitStack()
            uc = self.sb(st3, 's5u', [128, T], BF16); R_uc = Res('s5u')
            bufA = self.sb(st3, 's5A', [128, T], F32); R_A = Res('s5A')
            bufB = self.sb(st3, 's5B', [128, T], F32); R_B = Res('s5B')
            mtmp = self.sb(st3, 's5mt', [128, 1024], F32); R_mt = Res('s5mt')
            cosT = self.sb(st3, 's5cos', [128, T], BF16); R_cos = Res('s5cos')
            sinT = self.sb(st3, 's5sin', [128, T], BF16); R_sin = Res('s5sin')
            prb = self.sb(st3, 's5prb', [128, T], BF16); R_prb = Res('s5prb')
            pib = self.sb(st3, 's5pib', [128, T], BF16); R_pib = Res('s5pib')
            bre = self.sb(st3, 's5bre', [128, T], BF16); R_bre = Res('s5bre')
            bim = self.sb(st3, 's5bim', [128, T], BF16); R_bim = Res('s5bim')
            sre = self.sb(st3, 's5sre', [128, T], BF16); R_sre = Res('s5sre')
            sim = self.sb(st3, 's5sim', [128, T], BF16); R_sim = Res('s5sim')
            zt = self.sb(st3, 's5zt', [128, 512], F32); R_zt = Res('s5zt')

            R_Ahi = Res('s5Ahi')

            def conv(lo, hi, R_Ax):
                self.act(sinT[:, lo:hi], bufA[:, lo:hi], AF.Sin, [R_Ax], [R_sin], scale=-1.0, bias=float(np.pi) - 1e-6, join=(lo > 0))
                self.act(bufB[:, lo:hi], bufA[:, lo:hi], AF.Sin, [R_Ax], [R_B], scale=0.5, join=(lo > 0))
                self.act(bufB[:, lo:hi], bufB[:, lo:hi], AF.Square, [R_B], [R_B], join=(lo > 0))
                self.act(cosT[:, lo:hi], bufB[:, lo:hi], AF.Identity, [R_B], [R_cos], scale=-2.0, bias=1.0, join=(lo > 0))

            for c in range(8):
                self.ld(uc[:], self.ybT[c * 128:(c + 1) * 128, :], [self.R_ybT], [R_uc])
                ybanks = [self.pbanks[n] for n in range(NT)]
                for si in range(4):
                    s_ = c * 4 + si
                    for n in range(NT):
                        ns = slice(n * 512, (n + 1) * 512)
                        pre, R_pre = self.bank('B')
                        pim, R_pim = self.bank('B')
                        self.mm(pre[:], self.LB(s_, 0), uc[:, ns], True, True, [R_LB, self.R_act, R_uc], [R_pre], False)
                        self.mm(pim[:], self.LB(s_, 1), uc[:, ns], True, True, [R_LB, self.R_act, R_uc], [R_pim], False)
                        self.cp(prb[:, ns], pre[:], [R_pre], [R_prb], eng='act', join=(n > 0))
                        self.cp(pib[:, ns], pim[:], [R_pim], [R_pib], eng='act', join=(n > 0))
                    self.memset(bufA[:, 0:1], 0.0, [R_A])
                    for k in range(11):
                        Lk = 1 << k
                        Rw = R_Ahi if k == 10 else R_A
                        self.ts(bufA[:, Lk:2 * Lk], bufA[:, 0:Lk], dsh[:, k, s_:s_ + 1], None, ALU.add, None, [R_A, R_dsh], [Rw])
                        self.ts(mtmp[:, 0:Lk], bufA[:, Lk:2 * Lk], TWO_PI, -TWO_PI, ALU.is_ge, ALU.mult, [Rw], [R_mt])
                        self.tt(bufA[:, Lk:2 * Lk], bufA[:, Lk:2 * Lk], mtmp[:, 0:Lk], ALU.add, [Rw, R_mt], [Rw])
                        if k == 9:
                            conv(0, 1024, R_A)
                    conv(1024, 2048, R_Ahi)
                    self.tt(bre[:], prb[:], cosT[:], ALU.mult, [R_prb, R_cos], [R_bre])
                    self.tt(sre[:], pib[:], sinT[:], ALU.mult, [R_pib, R_sin], [R_sre])
                    self.tt(bre[:], bre[:], sre[:], ALU.add, [R_bre, R_sre], [R_bre])
                    self.tt(bim[:], pib[:], cosT[:], ALU.mult, [R_pib, R_cos], [R_bim])
                    self.tt(sim[:], prb[:], sinT[:], ALU.mult, [R_prb, R_sin], [R_sim])
                    self.tt(bim[:], bim[:], sim[:], ALU.subtract, [R_bim, R_sim], [R_bim])
                    rho_b = RHO[:, s_:s_ + 1].broadcast_to([128, T])
                    S.op('dve', lambda en, rho_b=rho_b: en.tensor_tensor_scan(out=bre[:], data0=rho_b, data1=bre[:], initial=0.0, op0=ALU.mult, op1=ALU.add),
                         reads=[R_sm, R_bre], writes=[R_bre])
                    S.op('dve', lambda en, rho_b=rho_b: en.tensor_tensor_scan(out=bim[:], data0=rho_b, data1=bim[:], initial=0.0, op0=ALU.mult, op1=ALU.add),
                         reads=[R_sm, R_bim], writes=[R_bim])
                    self.tt(prb[:], bre[:], cosT[:], ALU.mult, [R_bre, R_cos], [R_prb])
                    self.tt(pib[:], bim[:], sinT[:], ALU.mult, [R_bim, R_sin], [R_pib])
                    self.tt(sre[:], prb[:], pib[:], ALU.subtract, [R_prb, R_pib], [R_sre])
                    self.tt(prb[:], bim[:], cosT[:], ALU.mult, [R_bim, R_cos], [R_prb])
                    self.tt(pib[:], bre[:], sinT[:], ALU.mult, [R_bre, R_sin], [R_pib])
                    self.tt(sim[:], prb[:], pib[:], ALU.add, [R_prb, R_pib], [R_sim])
                    for n in range(NT):
                        ns = slice(n * 512, (n + 1) * 512)
                        yb_, R_yb = ybanks[n]
                        self.mm(yb_[:], self.LC(s_, 0), sre[:, ns], si == 0, False, [R_LC, self.R_act, R_sre], [R_yb], si > 0)
                        self.mm(yb_[:], self.LC(s_, 1), sim[:, ns], False, si == 3, [R_LC, self.R_act, R_sim], [R_yb], True)
                for n in range(NT):
                    ns = slice(n * 512, (n + 1) * 512)
                    yb_, R_yb = ybanks[n]
                    self.stt(zt[:], uc[:, ns], odv[:, 0, c:c + 1], yb_[:], ALU.mult, ALU.add, [R_uc, R_odv, R_yb], [R_zt])
                    self.act(self.actbuf[:, c, ns], zt[:], AF.Gelu_apprx_tanh, [R_zt], [R_z, self.R_act], join=True)
            S.barrier()
            st3.close()
            sg = [self.sb(st, 's5sg%d' % i, [128, 512], F32) for i in range(2)]; R_sg = [Res('s5sg%d' % i) for i in range(2)]
            go = [self.sb(st, 's5go%d' % i, [128, 512], BF16) for i in range(2)]; R_go = [Res('s5go%d' % i) for i in range(2)]
            gc = 0
            for t in range(2):
                slot, R_w = self.wnext(('glu', l, t))
                wt = self.wview(slot, 8, 512)
                for c in range(4):
                    oc = t * 4 + c
                    for n in range(NT):
                        ns = slice(n * 512, (n + 1) * 512)
                        pt, R_pt = self.bank('B')
                        for kc in range(8):
                            self.mm(pt[:], wt[:, kc, c * 128:(c + 1) * 128], self.actbuf[:, kc, ns], kc == 0, kc == 7, [R_w, R_z, self.R_act], [R_pt], kc > 0)
                        g_ = gc % 2; gc += 1
                        self.act(sg[g_][:], pt[:], AF.Sigmoid, [R_pt, R_odv], [R_sg[g_]], bias=odv[:, 1, oc:oc + 1])
                        self.tt(go[g_][:], sg[g_][:], self.actbuf[:, oc, ns], ALU.mult, [R_sg[g_], R_z, self.R_act], [R_go[g_]])
                        self.ld(self.mixT[(8 + oc) * 128:(9 + oc) * 128, ns], go[g_][:], [R_go[g_]], [self.R_mixT], join=True)
        S.barrier()

    def resid_load(self, dch, n, xt, R_xt, i):
        ns = slice(n * 512, (n + 1) * 512)
        rows = slice(dch * 128, (dch + 1) * 128)
        self.ld(xt[i][:], self.xT[rows, ns], [self.R_xTn[n]], [R_xt[i]])

    def resid_epilogue(self, pt, R_pt, gcol, dch, n, xt, R_xt, i, first, preloaded=False):
        ns = slice(n * 512, (n + 1) * 512)
        rows = slice(dch * 128, (dch + 1) * 128)
        if not preloaded:
            self.ld(xt[i][:], self.xT[rows, ns], [self.R_xTn[n]], [R_xt[i]])
        self.stt(xt[i][:], pt[:], self.modT[:, gcol + dch:gcol + dch + 1], xt[i][:], ALU.mult, ALU.add, [R_pt, self.R_modT, R_xt[i]], [R_xt[i]])
        self.ld(self.xT[rows, ns], xt[i][:], [R_xt[i]], [self.R_xTn[n]], join=True)

    def phase_outproj(self, l):
        S = self.S
        with contextlib.ExitStack() as st:
            xt = [self.sb(st, 'opx%d' % i, [128, 512], F32) for i in range(4)]; R_xt = [Res('opx%d' % i) for i in range(4)]
            R_ag = [Res('opag%d' % i) for i in range(4)]
            for gi in range(4):
                self.ld(self.actbuf[:, gi * 4:(gi + 1) * 4, :], self.mixT[gi * 512:(gi + 1) * 512, :].rearrange("(k p) t -> p k t", p=128), [self.R_mixT], [R_ag[gi]])
            cnt = 0
            tiles = [(t * 4 + c, n) for t in range(4) for c in range(4) for n in range(NT)]
            self.resid_load(tiles[0][0], tiles[0][1], xt, R_xt, 0)
            for t in range(4):
                slot, R_w = self.wnext(('wout', l, t))
                wt = self.wview(slot, KC, 512)
                for c in range(4):
                    dch = t * 4 + c
                    for n in range(NT):
                        pt, R_pt = self.bank()
                        for kc in range(KC):
                            self.mm(pt[:], wt[:, kc, c * 128:(c + 1) * 128], self.actbuf[:, kc, n * 512:(n + 1) * 512], kc == 0, kc == KC - 1, [R_w, R_ag[kc // 4]], [R_pt], kc > 0)
                        if cnt + 1 < len(tiles):
                            self.resid_load(tiles[cnt + 1][0], tiles[cnt + 1][1], xt, R_xt, (cnt + 1) % 4)
                        self.resid_epilogue(pt, R_pt, 32, dch, n, xt, R_xt, cnt % 4, cnt == 0, preloaded=True)
                        cnt += 1
        S.barrier()

    def phase_ffn_in(self, l):
        S = self.S
        with contextlib.ExitStack() as st:
            cw = self.sb(st, 'fcw', [128, 86, 3], F32); R_cw = Res('fcw')
            cb = self.sb(st, 'fcb', [128, 86], F32); R_cb = Res('fcb')
            self.ld(cw[:], self.ffn_convw_d[:, l], [], [R_cw])
            self.ld(cb[:], self.ffn_convb_d[:, l], [], [R_cb])
            U = [[self.sb(st, 'fU%d_%d' % (s_, br), [128, T + 2], F32) for br in range(2)] for s_ in range(2)]
            R_U = [[Res('fU%d_%d' % (s_, br)) for br in range(2)] for s_ in range(2)]
            acc = [self.sb(st, 'facc%d' % br, [128, T], F32) for br in range(2)]; R_acc = [Res('facc%d' % br) for br in range(2)]
            gg = self.sb(st, 'fgg', [128, T], BF16); R_gg = Res('fgg')
            mid = [self.sb(st, 'fmid%d' % i, [128, T], BF16) for i in range(2)]; R_mid = [Res('fmid%d' % i) for i in range(2)]
            for s_ in range(2):
                for br in range(2):
                    self.memset(U[s_][br][:, 0:2], 0.0, [R_U[s_][br]])
            jc = 0
            for t in range(22):
                slot, R_w = self.wnext(('ffin', l, t))
                j0 = 2 * t
                nj = min(2, NJ - j0)
                w = nj * 128
                wt = self.wview(slot, KC, 2 * w)
                for jj in range(nj):
                    j = j0 + jj
                    s_ = jc % 2; jc += 1
                    for br in range(2):
                        cidx = br * NJ + j
                        for n in range(NT):
                            pt, R_pt = self.bank('7')
                            for kc in range(KC):
                                self.mm(pt[:], wt[:, kc, br * w + jj * 128: br * w + (jj + 1) * 128], self.actbuf[:, kc, n * 512:(n + 1) * 512],
                                        kc == 0, kc == KC - 1, [R_w, self.R_act], [R_pt], kc > 0)
                            self.cp(U[s_][br][:, 2 + n * 512: 2 + (n + 1) * 512], pt[:], [R_pt], [R_U[s_][br]], eng='act', join=True)
                        Ub, R_Ub = U[s_][br], R_U[s_][br]
                        self.ts(acc[br][:], Ub[:, 2:T + 2], cw[:, cidx, 2:3], cb[:, cidx:cidx + 1], ALU.mult, ALU.add, [R_Ub, R_cw, R_cb], [R_acc[br]])
                        for i in (1, 0):
                            self.stt(acc[br][:], Ub[:, i:T + i], cw[:, cidx, i:i + 1], acc[br][:], ALU.mult, ALU.add, [R_Ub, R_cw, R_acc[br]], [R_acc[br]])
                    self.act(gg[:], acc[0][:], AF.Gelu_apprx_tanh, [R_acc[0]], [R_gg])
                    m = j % 2
                    self.tt(mid[m][:], gg[:], acc[1][:], ALU.mult, [R_gg, R_acc[1]], [R_mid[m]])
                    self.ld(self.midT[j * 128:(j + 1) * 128, :], mid[m][:], [R_mid[m]], [self.R_midT], join=(j > 0))
                if l + 1 < self.nlayers:
                    self.ada_tile(l + 1, t)
            if l + 1 < self.nlayers:
                for t in (22, 23):
                    self.ada_tile(l + 1, t)
                self.ada_finish(l + 1)
        S.barrier()

    def phase_ffn_out(self, l):
        S = self.S
        NA = 31
        with contextlib.ExitStack() as st:
            midB = self.sb(st, 'fmidB', [128, NJ - NA, 1024], BF16); R_midB = Res('fmidB')
            midA = self.actbuf[:].rearrange("p k t -> p (k t)").rearrange("p (j t) -> p j t", t=1024)
            xt = [self.sb(st, 'fox%d' % i, [128, 512], F32) for i in range(4)]; R_xt = [Res('fox%d' % i) for i in range(4)]
            cnt = 0; bs = 0
            grp_bounds = [(0, 8), (8, 16), (16, 24), (24, NA), (NA, 37), (37, NJ)]
            R_mg = [Res('fmg%d' % i) for i in range(6)]
            grp_of = {}
            for gi, (a_, b_) in enumerate(grp_bounds):
                for j in range(a_, b_):
                    grp_of[j] = gi
            for th in range(2):
                tsl = slice(th * 1024, (th + 1) * 1024)
                for gi, (a_, b_) in enumerate(grp_bounds):
                    dst = midA[:, a_:b_, :] if b_ <= NA else midB[:, a_ - NA:b_ - NA, :]
                    self.ld(dst, self.midT[a_ * 128:b_ * 128, tsl].rearrange("(j p) t -> p j t", p=128), [self.R_midT], [R_mg[gi]])
                for dp in range(8):
                    slot, R_w = self.wnext(('ffout', l, th, dp))
                    wt = self.wview(slot, NJ, 256)
                    base = (bs % 2) * 4; bs += 1
                    banks = [[self.pbanks[base + dd * 2 + n2] for n2 in range(2)] for dd in range(2)]
                    for q_ in range(4):
                        self.resid_load(dp * 2 + q_ // 2, th * 2 + q_ % 2, xt, R_xt, (cnt + q_) % 4)
                    for j in range(NJ):
                        R_src = R_mg[grp_of[j]]
                        src = midA[:, j, :] if j < NA else midB[:, j - NA, :]
                        for dd in range(2):
                            for n2 in range(2):
                                pt, R_pt = banks[dd][n2]
                                self.mm(pt[:], wt[:, j, dd * 128:(dd + 1) * 128], src[:, n2 * 512:(n2 + 1) * 512], j == 0, j == NJ - 1, [R_w, R_src], [R_pt], j > 0)
                    for dd in range(2):
                        for n2 in range(2):
                            pt, R_pt = banks[dd][n2]
                            self.resid_epilogue(pt, R_pt, 80, dp * 2 + dd, th * 2 + n2, xt, R_xt, cnt % 4, cnt == 0, preloaded=True)
                            cnt += 1
        S.barrier()


_CACHE = {}
BF = ml_dtypes.bfloat16


def _const_inputs():
    c = {}
    c['identf'] = np.eye(128, dtype=np.float32)
    c['identb'] = np.eye(128, dtype=np.float32).astype(BF)
    j = np.arange(128)[:, None]; i = np.arange(128)[None, :]
    cur = np.where(j <= i, 0.0, NEG).astype(np.float32)
    prevA = np.where(j >= i, 0.0, NEG).astype(np.float32)
    prevC = np.where(j > i, 0.0, NEG).astype(np.float32)
    masks = np.stack([np.tile(cur, (1, 4)), np.tile(prevA, (1, 4)), np.tile(prevC, (1, 4))], axis=1)
    c['masks'] = masks.astype(BF)
    m3 = np.stack([np.tile(cur[:, qt * 32:(qt + 1) * 32], (1, 16)) for qt in range(4)], axis=1)
    c['m3'] = m3.astype(BF)
    pa = np.zeros((128, 128), np.float32)
    for m in range(128):
        pa[(m + 64) % 128, m] = 1.0
    pc = np.zeros((128, 128), np.float32)
    for m in range(128):
        blk = m // 64; d = m % 64
        pc[blk * 64 + (d + 32) % 64, m] = 1.0
    c['pswap'] = np.stack([pa, pc], axis=1).astype(BF)
    rc = np.zeros((128, 4), np.float32)
    for p in range(128):
        rc[p, 0] = np.float32(10000.0) ** (-np.float32(p % 64) / np.float32(64))
        rc[p, 1] = -1.0 if p < 64 else 1.0
        d = p % 64
        rc[p, 2] = np.float32(10000.0) ** (-np.float32(d % 32) / np.float32(32))
        rc[p, 3] = -1.0 if d < 32 else 1.0
    c['ropec'] = rc
    return c


def _fm(v, nchunk):
    v = np.asarray(v)
    lead = v.shape[:-1]
    v = v.reshape(lead + (nchunk, 128))
    return np.ascontiguousarray(np.moveaxis(v, -1, 0))


def _shared_inputs(inp):
    m = _const_inputs()
    m['norm_final_T'] = _fm(inp['norm_final'], KC)
    m['norm_mix_T'] = _fm(inp['norm_mix'], KC)
    m['norm_ffn_T'] = _fm(inp['norm_ffn'], KC)
    m['ada_w'] = np.asarray(inp['ada_w'])
    m['ada_bT'] = _fm(inp['ada_b'], 96)
    m['ev_w_in'] = np.asarray(inp['ev_w_in'])
    m['ev_w_out'] = np.asarray(inp['ev_w_out'])
    m['ev_convw_T'] = np.ascontiguousarray(np.transpose(_fm(inp['ev_conv_w'], 8), (0, 1, 3, 2)))
    vec = np.stack([_fm(inp['ev_conv_b'], 8), _fm(inp['ev_gate_a_b'], 8), _fm(inp['ev_gate_x_b'], 8), _fm(inp['ev_lambda'], 8)], axis=2)
    m['ev_vec_T'] = np.ascontiguousarray(vec)
    ga = np.asarray(inp['ev_gate_a_w']); gx = np.asarray(inp['ev_gate_x_w'])
    gw = np.stack([ga, gx], axis=1)
    m['ev_gw'] = np.ascontiguousarray(np.transpose(gw, (3, 0, 1, 2, 4)))
    m['od_w_in'] = np.asarray(inp['od_w_in'])
    m['od_w_out'] = np.asarray(inp['od_w_out'])
    m['od_glu_w'] = np.asarray(inp['od_glu_w'])
    def st_layout(v):
        v = np.asarray(v)
        v = v.reshape((v.shape[0], 32, 128) + v.shape[2:])
        return np.ascontiguousarray(np.moveaxis(v, 2, 0))
    a_re = np.asarray(inp['od_a_re']).reshape(2, 4096); a_im = np.asarray(inp['od_a_im']).reshape(2, 4096)
    ldt = np.repeat(np.asarray(inp['od_log_dt']), 64, axis=1)
    m['s5v'] = np.ascontiguousarray(np.stack([st_layout(a_re), st_layout(a_im), st_layout(ldt)], axis=2))
    b_re = np.asarray(inp['od_b_re']).reshape(2, 4096, 16); b_im = np.asarray(inp['od_b_im']).reshape(2, 4096, 16)
    m['s5b'] = np.ascontiguousarray(np.stack([st_layout(b_re), st_layout(b_im)], axis=2))
    c_re = np.transpose(np.asarray(inp['od_c_re']), (0, 1, 3, 2)).reshape(2, 4096, 16)
    c_im = np.transpose(np.asarray(inp['od_c_im']), (0, 1, 3, 2)).reshape(2, 4096, 16)
    m['s5c'] = np.ascontiguousarray(np.stack([st_layout(c_re), st_layout(c_im)], axis=2))
    sk = np.asarray(inp['od_sinks'])
    skT = np.concatenate([np.broadcast_to(sk[:, None, 0:8], (2, 64, 8)), np.broadcast_to(sk[:, None, 8:16], (2, 64, 8))], axis=1)
    m['odv'] = np.ascontiguousarray(np.stack([_fm(inp['od_d'], 8), _fm(inp['od_glu_b'], 8), np.transpose(skT, (1, 0, 2))], axis=2))
    m['ffn_w_in'] = np.asarray(inp['ffn_w_in'])
    m['ffn_w_out'] = np.asarray(inp['ffn_w_out'])
    m['ffn_convw_T'] = np.ascontiguousarray(np.transpose(_fm(inp['ffn_conv_w'], 86), (0, 1, 3, 2)))
    m['ffn_convb_T'] = _fm(inp['ffn_conv_b'], 86)
    return m


def kernel(**inputs):
    x = np.asarray(inputs['x'])
    B = x.shape[0]
    if 'nc' not in _CACHE:
        bld = Builder()
        _CACHE['nc'] = bld.build()
        _CACHE['names'] = bld.inputs
    nc = _CACHE['nc']
    shared = _shared_inputs(inputs)
    pos = np.asarray(inputs['positions']).astype(np.int32)
    c = np.asarray(inputs['c'])
    in_maps = []
    NCORE = B
    for core in range(NCORE):
        b = core % B
        m = dict(shared)
        m['x'] = np.ascontiguousarray(x[b])
        m['pos128'] = np.ascontiguousarray(np.broadcast_to(pos[b][None, :], (128, T)))
        m['cT'] = _fm(c[b], KC)
        m = {k: v for k, v in m.items() if k in _CACHE['names']}
        in_maps.append(m)
    res = run_bass_kernel_spmd(nc, in_maps, core_ids=list(range(NCORE)))
    out = np.stack([res.results[b]['out'] for b in range(B)], axis=0)
    return out.astype(np.float32)
```
